# Optimizing a Trainium2 kernel written in Bass

```python
import jax, jax.numpy as jnp
from jax import lax
import numpy as np

D_MODEL = 1024
BATCH = 8
SEQ = 8192
DEPTH = 2

CHUNK = 128
A_HEADS = 4
A_HEAD_DIM = D_MODEL // 8
A_WIDTH = A_HEADS * A_HEAD_DIM
POOL_WINDOWS = (2, 4, 8, 16)
POOL_GROUPS = len(POOL_WINDOWS)
POOL_DIM = D_MODEL // 8
B_WIDTH = POOL_GROUPS * POOL_DIM
MIX_WIDTH = A_WIDTH + B_WIDTH
IN_WIDTH = 2 * A_WIDTH + B_WIDTH
CONV_DIM = D_MODEL
CONV_WIDTH = 31
FFN_DIM = ((8 * D_MODEL // 3 + 127) // 128) * 128
FFN_CONV_WIDTH = 3
DEEPNORM_ALPHA = (2 * DEPTH) ** 0.25
DEEPNORM_BETA = (8 * DEPTH) ** -0.25
ADA_INIT = 0.5
LN_EPS = 1e-5
N_EVEN = (DEPTH + 1) // 2
N_ODD = DEPTH // 2

kernel_name = "hybrid_sgu_pool_conformer_deepnorm_adaln"


def _layer_norm(x, g, b):
    xf = x.astype(jnp.float32)
    mu = jnp.mean(xf, axis=-1, keepdims=True)
    var = jnp.mean(jnp.square(xf - mu), axis=-1, keepdims=True)
    y = (xf - mu) * lax.rsqrt(var + LN_EPS)
    return (y * g + b).astype(x.dtype)


def _modulation(c, w, b):
    m = jax.nn.silu(c) @ w + b
    shift, scale, gate = jnp.split(m, 3, axis=-1)
    return shift[:, None, :], scale[:, None, :], gate[:, None, :]


def _depthwise_conv(x, w, b):
    ch = w.shape[1]
    y = lax.conv_general_dilated(x, w[:, None, :], window_strides=(1,), padding='SAME',
                                 dimension_numbers=('NWC', 'WIO', 'NWC'), feature_group_count=ch)
    return y + b


def _spatial_gating(u, v, ln_g, ln_b, ws, bs):
    v = _layer_norm(v, ln_g, ln_b)
    nb, s, _ = v.shape
    vh = v.reshape(nb, s // CHUNK, CHUNK, A_HEADS, A_HEAD_DIM)
    mixed = jnp.einsum('hpq,bnqhd->bnphd', ws, vh) + bs.T[None, None, :, :, None]
    return u * mixed.reshape(nb, s, A_WIDTH)


def _centred_mean_minus_self(z, window):
    s = z.shape[1]
    half = window // 2
    prefix = jnp.concatenate([jnp.zeros_like(z[:, :1]), jnp.cumsum(z, axis=1)], axis=1)
    padded = jnp.pad(prefix, ((0, 0), (half, half), (0, 0)), mode='edge')
    total = padded[:, 2 * half:2 * half + s] - padded[:, :s]
    t = jnp.arange(s)
    count = (jnp.minimum(t + half, s) - jnp.maximum(t - half, 0)).astype(jnp.float32)
    return total / count[None, :, None] - z


def _multiscale_pool(z, pool_w, pool_scale):
    nb, s, _ = z.shape
    zf = z.astype(jnp.float32).reshape(nb, s, POOL_GROUPS, POOL_DIM)
    pooled = jnp.stack([_centred_mean_minus_self(zf[:, :, g], w) for g, w in enumerate(POOL_WINDOWS)], axis=2)
    y = jnp.einsum('bsgc,gcd->bsgd', pooled.astype(z.dtype), pool_w)
    return y.reshape(nb, s, B_WIDTH) * pool_scale


def _sgu_pool_mixer(h, w_in, sgu_ln_g, sgu_ln_b, ws, bs, pool_w, pool_scale, w_out):
    proj = h @ w_in
    u_a = jax.nn.gelu(proj[..., :A_WIDTH])
    v_a = jax.nn.gelu(proj[..., A_WIDTH:2 * A_WIDTH])
    z_b = proj[..., 2 * A_WIDTH:]
    out_a = _spatial_gating(u_a, v_a, sgu_ln_g, sgu_ln_b, ws, bs)
    out_b = _multiscale_pool(z_b, pool_w, pool_scale)
    return jnp.concatenate([out_a, out_b], axis=-1) @ w_out


def _conformer_conv(h, w_in, dw, dw_b, ln_g, ln_b, w_out):
    a, g = jnp.split(h @ w_in, 2, axis=-1)
    z = a * jax.nn.sigmoid(g)
    z = _depthwise_conv(z, dw, dw_b)
    z = jax.nn.silu(_layer_norm(z, ln_g, ln_b))
    return z @ w_out


def _conv_ffn(h, w_up, dw, dw_b, w_down):
    z = _depthwise_conv(h @ w_up, dw, dw_b)
    g, v = jnp.split(z, 2, axis=-1)
    return (jax.nn.gelu(g) * v) @ w_down


def setup_inputs(seed: int = 0) -> dict:
    key = jax.random.key(seed)
    ks = jax.random.split(key, 32)

    def nrm(k, shape, scale):
        return scale * jax.random.normal(k, shape, jnp.float32)

    d = D_MODEL
    return {
        "x": nrm(ks[0], (BATCH, SEQ, d), 1.0),
        "c": nrm(ks[1], (BATCH, d), 1.0),
        "mix_ada_w": nrm(ks[2], (DEPTH, d, 3 * d), ADA_INIT * d ** -0.5),
        "mix_ada_b": nrm(ks[3], (DEPTH, 3 * d), 0.01),
        "mix_ln_g": 1.0 + nrm(ks[4], (DEPTH, d), 0.02),
        "mix_ln_b": nrm(ks[5], (DEPTH, d), 0.01),
        "ab_w_in": nrm(ks[6], (N_EVEN, d, IN_WIDTH), d ** -0.5),
        "ab_sgu_ln_g": 1.0 + nrm(ks[7], (N_EVEN, A_WIDTH), 0.02),
        "ab_sgu_ln_b": nrm(ks[8], (N_EVEN, A_WIDTH), 0.01),
        "ab_ws": nrm(ks[9], (N_EVEN, A_HEADS, CHUNK, CHUNK), CHUNK ** -0.5),
        "ab_bs": 1.0 + nrm(ks[10], (N_EVEN, A_HEADS, CHUNK), 0.02),
        "ab_pool_w": nrm(ks[11], (N_EVEN, POOL_GROUPS, POOL_DIM, POOL_DIM), POOL_DIM ** -0.5),
        "ab_pool_scale": 1.0 + nrm(ks[12], (N_EVEN, B_WIDTH), 0.1),
        "ab_w_out": nrm(ks[13], (N_EVEN, MIX_WIDTH, d), DEEPNORM_BETA * MIX_WIDTH ** -0.5),
        "cv_w_in": nrm(ks[14], (N_ODD, d, 2 * CONV_DIM), d ** -0.5),
        "cv_dw": nrm(ks[15], (N_ODD, CONV_WIDTH, CONV_DIM), CONV_WIDTH ** -0.5),
        "cv_dw_b": nrm(ks[16], (N_ODD, CONV_DIM), 0.01),
        "cv_ln_g": 1.0 + nrm(ks[17], (N_ODD, CONV_DIM), 0.02),
        "cv_ln_b": nrm(ks[18], (N_ODD, CONV_DIM), 0.01),
        "cv_w_out": nrm(ks[19], (N_ODD, CONV_DIM, d), DEEPNORM_BETA * CONV_DIM ** -0.5),
        "ffn_ada_w": nrm(ks[20], (DEPTH, d, 3 * d), ADA_INIT * d ** -0.5),
        "ffn_ada_b": nrm(ks[21], (DEPTH, 3 * d), 0.01),
        "ffn_w_up": nrm(ks[22], (DEPTH, d, 2 * FFN_DIM), d ** -0.5),
        "ffn_dw": nrm(ks[23], (DEPTH, FFN_CONV_WIDTH, 2 * FFN_DIM), FFN_CONV_WIDTH ** -0.5),
        "ffn_dw_b": nrm(ks[24], (DEPTH, 2 * FFN_DIM), 0.01),
        "ffn_w_down": nrm(ks[25], (DEPTH, FFN_DIM, d), DEEPNORM_BETA * FFN_DIM ** -0.5),
        "ffn_ln_g": 1.0 + nrm(ks[26], (DEPTH, d), 0.02),
        "ffn_ln_b": nrm(ks[27], (DEPTH, d), 0.01),
    }


def reference(x, c, mix_ada_w, mix_ada_b, mix_ln_g, mix_ln_b,
              ab_w_in, ab_sgu_ln_g, ab_sgu_ln_b, ab_ws, ab_bs, ab_pool_w, ab_pool_scale, ab_w_out,
              cv_w_in, cv_dw, cv_dw_b, cv_ln_g, cv_ln_b, cv_w_out,
              ffn_ada_w, ffn_ada_b, ffn_w_up, ffn_dw, ffn_dw_b, ffn_w_down, ffn_ln_g, ffn_ln_b):
    for i in range(DEPTH):
        j = i // 2
        shift, scale, gate = _modulation(c, mix_ada_w[i], mix_ada_b[i])
        h = x * (1.0 + scale) + shift
        if i % 2 == 0:
            y = _sgu_pool_mixer(h, ab_w_in[j], ab_sgu_ln_g[j], ab_sgu_ln_b[j], ab_ws[j], ab_bs[j],
                                ab_pool_w[j], ab_pool_scale[j], ab_w_out[j])
        else:
            y = _conformer_conv(h, cv_w_in[j], cv_dw[j], cv_dw_b[j], cv_ln_g[j], cv_ln_b[j], cv_w_out[j])
        x = _layer_norm(DEEPNORM_ALPHA * x + (1.0 + gate) * y, mix_ln_g[i], mix_ln_b[i])
        shift, scale, gate = _modulation(c, ffn_ada_w[i], ffn_ada_b[i])
        h = x * (1.0 + scale) + shift
        y = _conv_ffn(h, ffn_w_up[i], ffn_dw[i], ffn_dw_b[i], ffn_w_down[i])
        x = _layer_norm(DEEPNORM_ALPHA * x + (1.0 + gate) * y, ffn_ln_g[i], ffn_ln_b[i])
    return x
```

```python
from contextlib import ExitStack
import numpy as np
import concourse.bass as bass
import concourse.mybir as mybir
from concourse.bass_utils import run_bass_kernel_spmd

F32 = mybir.dt.float32
BF16 = mybir.dt.bfloat16
AF = mybir.ActivationFunctionType
ALU = mybir.AluOpType

P = 128
D = 1024
KC = 8
T = 256
FF = 2816
FJ = 22
ALPHA = 4.0 ** 0.25
EPS = 1e-5
EPS_R = EPS / (ALPHA * ALPHA)

OFF_C, OFF_ADAB, OFF_LN, OFF_PSC, OFF_CVB, OFF_CVLN, OFF_DWT, OFF_FDW, OFF_FDB, NPP = (
    0, 8, 104, 168, 172, 180, 196, 444, 708, 796)
PB_SG, PB_SB, PB_BS, PB_IC, PB_ID, NPB = 0, 512, 1024, 2048, 4096, 4224

ENGS = ("pe", "act", "dve", "pool", "sp")


class Sched:
    def __init__(self, nc, es):
        self.nc = nc
        self.es = es
        self.ops = {e: [] for e in ENGS}
        self.sems = {}
        self.cnt = {}
        self.known = {e: {} for e in ENGS}
        self.last_w = {}
        self.readers = {}
        self.deferred = []
        for e in ("pe", "act", "dve", "pool"):
            self._sem("E_" + e)

    def _sem(self, name):
        if name not in self.sems:
            self.sems[name] = self.es.enter_context(self.nc.semaphore(name))
            self.cnt[name] = 0
        return self.sems[name]

    def _deps(self, eng, reads, writes, is_dma):
        need = {}
        own = "E_" + eng

        def add(s, v):
            if (not is_dma) and s == own and eng == "pe":
                return
            if need.get(s, 0) < v:
                need[s] = v

        for k in reads:
            lw = self.last_w.get(k)
            if lw is not None:
                add(*lw)
        for k in writes:
            lw = self.last_w.get(k)
            if lw is not None:
                add(*lw)
            for s, v in self.readers.get(k, {}).items():
                add(s, v)
        waits = []
        kn = self.known[eng]
        for s, v in need.items():
            if kn.get(s, 0) < v:
                kn[s] = v
                waits.append((s, v))
        return waits

    def _commit(self, tok, reads, writes):
        s, v = tok
        for k in reads:
            r = self.readers.setdefault(k, {})
            if r.get(s, 0) < v:
                r[s] = v
        for k in writes:
            self.last_w[k] = tok
            self.readers[k] = {}

    def op(self, eng, fn, reads=(), writes=(), inc=True):
        waits = self._deps(eng, reads, writes, False)
        s = "E_" + eng
        tok = (s, self.cnt[s] + 1)
        if inc:
            self.cnt[s] += 1
        self.ops[eng].append((waits, fn, (s, 1) if inc else None))
        self._commit(tok, reads, writes)
        return tok

    def dma(self, eng, fn, sem, reads=(), writes=()):
        self._sem(sem)
        waits = self._deps(eng, reads, writes, True)
        self.cnt[sem] += 16
        tok = (sem, self.cnt[sem])
        self.ops[eng].append((waits, fn, (sem, 16)))
        self._commit(tok, reads, writes)
        return tok

    def barrier(self):
        for e in ENGS:
            waits = []
            kn = self.known[e]
            for s, v in self.cnt.items():
                if v > 0 and s != "E_" + e and kn.get(s, 0) < v:
                    kn[s] = v
                    waits.append((s, v))
            self.ops[e].append((waits, None, None))

    def wait_keys(self, eng, keys):
        waits = self._deps(eng, keys, (), True)
        self.ops[eng].append((waits, None, None))

    def defer(self, fn):
        self.deferred.append(fn)

    def flush(self, n=None):
        k = len(self.deferred) if n is None else min(n, len(self.deferred))
        for _ in range(k):
            self.deferred.pop(0)()

    def emit(self):
        nc = self.nc
        with nc.Block() as block:
            def run(engname):
                def body(e):
                    for waits, fn, inc in self.ops[engname]:
                        for s, v in waits:
                            e.wait_ge(self.sems[s], v)
                        if fn is None:
                            continue
                        inst = fn(e)
                        if inc is not None:
                            inst.then_inc(self.sems[inc[0]], inc[1])
                return body
            block.tensor(run("pe"))
            block.scalar(run("act"))
            block.vector(run("dve"))
            block.gpsimd(run("pool"))
            block.sync(run("sp"))


class Arena:
    def __init__(self, handle, n32):
        self.h = handle
        self.n = n32
        self.off = 0

    def alloc(self, nelem, dt):
        nb = nelem * (2 if dt == BF16 else 4)
        n32 = (nb + 15) // 16 * 4
        assert self.off + n32 <= self.n, f"arena overflow {self.off}+{n32}>{self.n}"
        ap = self.h[:, self.off:self.off + n32]
        self.off += n32
        if dt == BF16:
            ap = ap.bitcast(BF16)
        return ap[:, 0:nelem]


def build_nc(S_LEN=8192, phases=(0, 1, 2, 3)):
    NT = S_LEN // T
    nc = bass.Bass("TRN2", target_bir_lowering=False)

    def dram(name, shape, kind="ExternalInput"):
        return nc.dram_tensor(name, shape, F32, kind=kind).ap()

    xT = dram("xT", [D, S_LEN])
    outT = dram("outT", [D, S_LEN], "ExternalOutput")
    pp_d = dram("pp", [P, NPP])
    pb_d = dram("pb", [P, NPB])
    adaw_d = dram("adaw", [4, D, 3 * D])
    w_in0_d = dram("w_in0", [D, 1536])
    w_out0_d = dram("w_out0", [D, D])
    wsT_d = dram("wsT", [P, 512])
    poolw_d = dram("poolw", [P, 512])
    w_in1_d = dram("w_in1", [D, 2048])
    w_out1_d = dram("w_out1", [D, D])
    w_up_d = dram("w_up", [2, D, 2 * FF])
    w_down_d = dram("w_down", [2, FF, D])
    nph = len(phases)
    acts = [dram(f"act{i}", [D, S_LEN], "Internal") for i in range(max(nph - 1, 0))]
    srcs = [xT] + acts
    dsts = acts + [outT]

    with ExitStack() as es:
        S = Sched(nc, es)
        ARENA32 = 50176
        arena_h = es.enter_context(nc.sbuf_tensor("arena", [P, ARENA32], F32))
        A = Arena(arena_h, ARENA32)
        PS = [es.enter_context(nc.psum_tensor(f"ps{i}", [P, 512], F32)) for i in range(8)]

        pp = A.alloc(NPP, F32)
        modt = A.alloc(4 * 24, F32).rearrange("p (s f) -> p s f", s=4)
        sc1 = A.alloc(4 * 8, F32).rearrange("p (s f) -> p s f", s=4)
        g1a = A.alloc(4 * 8, F32).rearrange("p (s f) -> p s f", s=4)
        csil = A.alloc(8, F32)
        onesb = A.alloc(P, BF16)
        persist_mark = A.off

        S.dma("sp", lambda e: e.dma_start(out=pp, in_=pp_d), "ldpp", writes=["pp"])
        S.op("pool", lambda e: e.memset(onesb, 1.0 / D), writes=["onesb"])

        stg = [A.alloc(8 * 512, F32).rearrange("p (k f) -> p k f", k=8) for _ in range(2)]
        S.op("act", lambda e: e.activation(out=csil, in_=pp[:, OFF_C:OFF_C + 8], func=AF.Silu),
             reads=["pp"], writes=["csil"])
        gi = 0
        for s in range(4):
            mps = PS[s % 2]
            for g in range(6):
                st = stg[gi % 2]
                src = adaw_d[s].rearrange("(k p) f -> p k f", p=P)[:, :, g * 512:(g + 1) * 512]
                S.dma("sp", lambda e, st=st, src=src: e.dma_start(out=st, in_=src), f"ldst{gi % 2}",
                      writes=[("stg", gi % 2)])
                for fl in range(4):
                    fc = g * 4 + fl
                    for k in range(KC):
                        S.op("pe", lambda e, st=st, k=k, fl=fl, fc=fc, mps=mps: e.matmul(
                            mps[:, fc:fc + 1], lhsT=st[:, k, fl * P:(fl + 1) * P], rhs=csil[:, k:k + 1],
                            start=(k == 0), stop=(k == KC - 1)),
                            reads=[("stg", gi % 2), "csil"], writes=[("mps", s % 2)], inc=(k == KC - 1))
                gi += 1
            S.op("dve", lambda e, s=s, mps=mps: e.tensor_tensor(
                out=modt[:, s, :], in0=mps[:, 0:24], in1=pp[:, OFF_ADAB + 24 * s:OFF_ADAB + 24 * s + 24],
                op=ALU.add), reads=[("mps", s % 2), "pp"], writes=["modt"])
            S.op("dve", lambda e, s=s: e.tensor_scalar(
                out=sc1[:, s, :], in0=modt[:, s, 8:16], scalar1=1.0, scalar2=None, op0=ALU.add),
                reads=["modt"], writes=["sc1"])
            S.op("dve", lambda e, s=s: e.tensor_scalar(
                out=g1a[:, s, :], in0=modt[:, s, 16:24], scalar1=1.0, scalar2=1.0 / ALPHA,
                op0=ALU.add, op1=ALU.mult), reads=["modt"], writes=["g1a"])
        S.barrier()

        def load_weights_cast(dst2d, src2d, sem, key=None, last=False):
            S.dma("pool", lambda e: e.dma_start(out=dst2d, in_=src2d), sem,
                  writes=[key] if (last and key is not None) else [])

        def run_phase(pi, s):
            A.off = persist_mark
            src = srcs[pi].rearrange("(c p) t -> p c t", p=P)
            dst = dsts[pi].rearrange("(c p) t -> p c t", p=P)
            kind = ("mixA", "ffn", "mixC", "ffn")[s]
            H = {"mixA": 8, "ffn": 1, "mixC": 15}[kind]
            W = T + 2 * H
            lay = s // 2
            lng = pp[:, OFF_LN + s * 16:OFF_LN + s * 16 + 8]
            lnb = pp[:, OFF_LN + s * 16 + 8:OFF_LN + s * 16 + 16]

            xt = [A.alloc(KC * W, F32).rearrange("p (c w) -> p c w", c=KC) for _ in range(2)]
            ht = [A.alloc(KC * W, BF16).rearrange("p (c w) -> p c w", c=KC) for _ in range(2)]
            rbf = [A.alloc(T, BF16) for _ in range(2)]
            rsq = [A.alloc(T, BF16) for _ in range(2)]
            msq = A.alloc(T, F32)
            var = A.alloc(T, F32)
            rstd = A.alloc(T, F32)
            mean_ps = PS[6][:, 0:T]
            e2_ps = PS[7][:, 0:T]

            if kind == "ffn":
                wup = A.alloc(KC * 2 * FF, BF16).rearrange("p (k f) -> p k f", k=KC)
                wdn = A.alloc(FJ * D, BF16).rearrange("p (j d) -> p j d", j=FJ)
                at = A.alloc(FJ * T, BF16).rearrange("p (j t) -> p j t", j=FJ)
                accs = [[A.alloc(T, F32) for _ in range(3)] for _ in range(2)]
                CB = 1408
                for cb in (0, 2, 1, 3):
                    for k in range(KC):
                        load_weights_cast(wup[:, k, cb * CB:(cb + 1) * CB],
                                          w_up_d[lay, k * P:(k + 1) * P, cb * CB:(cb + 1) * CB],
                                          f"ldw{cb}", ("wup", cb), last=(k == KC - 1))
                for j in range(FJ):
                    load_weights_cast(wdn[:, j, :], w_down_d[lay, j * P:(j + 1) * P, :], "ldw4", "wdn",
                                      last=(j == FJ - 1))
                fdw = pp[:, OFF_FDW + lay * 132:OFF_FDW + (lay + 1) * 132].rearrange("p (c k) -> p c k", k=3)
                fdb = pp[:, OFF_FDB + lay * 44:OFF_FDB + (lay + 1) * 44]
            elif kind == "mixA":
                win = A.alloc(KC * 1536, BF16).rearrange("p (k f) -> p k f", k=KC)
                wout = A.alloc(KC * D, BF16).rearrange("p (k f) -> p k f", k=KC)
                wst = A.alloc(512, BF16).rearrange("p (h q) -> p h q", h=4)
                plw = A.alloc(512, BF16).rearrange("p (g d) -> p g d", g=4)
                at = A.alloc(KC * T, BF16).rearrange("p (j t) -> p j t", j=KC)
                pb = A.alloc(NPB, F32)
                u_sb = A.alloc(4 * T, F32).rearrange("p (h t) -> p h t", h=4)
                gv = [A.alloc(512, F32) for _ in range(2)]
                vn = [A.alloc(512, BF16) for _ in range(2)]
                zb = [A.alloc(W, F32) for _ in range(2)]
                sa = A.alloc(W, F32)
                sb = A.alloc(W, F32)
                tmpf = [A.alloc(T, F32) for _ in range(2)]
                pooled = [A.alloc(T, BF16) for _ in range(2)]
                st6 = A.alloc(8, F32)
                mv = A.alloc(4, F32)
                S.dma("sp", lambda e: e.dma_start(out=pb, in_=pb_d), "ldpb", writes=["pb"])
                for k in range(KC):
                    load_weights_cast(win[:, k, :], w_in0_d[k * P:(k + 1) * P, :], "ldw0", "win", last=(k == KC - 1))
                load_weights_cast(wst.rearrange("p h q -> p (h q)"), wsT_d, "ldw1", "wst", last=True)
                load_weights_cast(plw.rearrange("p g d -> p (g d)"), poolw_d, "ldw1", "wst", last=True)
                for k in range(KC):
                    load_weights_cast(wout[:, k, :], w_out0_d[k * P:(k + 1) * P, :], "ldw2", "wout", last=(k == KC - 1))
                psc = pp[:, OFF_PSC:OFF_PSC + 4]
            else:
                win = A.alloc(KC * 2048, BF16).rearrange("p (k f) -> p k f", k=KC)
                wout = A.alloc(KC * D, BF16).rearrange("p (k f) -> p k f", k=KC)
                dg = A.alloc(248 * P, BF16).rearrange("p (n d) -> p n d", n=248)
                at = A.alloc(KC * T, BF16).rearrange("p (j t) -> p j t", j=KC)
                zt = A.alloc(KC * W, BF16).rearrange("p (j w) -> p j w", j=KC)
                cz = A.alloc(KC * T, F32).rearrange("p (j t) -> p j t", j=KC)
                sg = [A.alloc(W, F32) for _ in range(2)]
                ident = A.alloc(P, F32)
                rstd2 = A.alloc(T, F32)
                S.dma("sp", lambda e: e.dma_start(out=ident, in_=pb_d[:, PB_ID:PB_ID + P]), "ldpb", writes=["ident"])
                for k in range(KC):
                    load_weights_cast(win[:, k, :], w_in1_d[k * P:(k + 1) * P, :], "ldw0", "win", last=(k == KC - 1))
                for k in range(KC):
                    load_weights_cast(wout[:, k, :], w_out1_d[k * P:(k + 1) * P, :], "ldw2", "wout", last=(k == KC - 1))
                dwt = pp[:, OFF_DWT:OFF_DWT + 248]
                for n in range(248):
                    eng = "dve" if n % 2 == 0 else "pool"
                    S.op(eng, lambda e, n=n: e.tensor_scalar(
                        out=dg[:, n, :], in0=ident, scalar1=dwt[:, n:n + 1], scalar2=None, op0=ALU.mult),
                        reads=["ident", "pp"], writes=[("dg", n // 31)])
                cvb = pp[:, OFF_CVB:OFF_CVB + 8]
                cvg = pp[:, OFF_CVLN:OFF_CVLN + 8]
                cvbb = pp[:, OFF_CVLN + 8:OFF_CVLN + 16]

            def stage_A(i):
                sl = i % 2
                lo, hi = i * T - H, i * T + T + H
                clo, chi = max(lo, 0), min(hi, S_LEN)
                a, b = clo - lo, chi - lo
                rk = [("act", pi - 1, ii) for ii in (i - 1, i, i + 1) if 0 <= ii < NT] if pi > 0 else []
                S.dma("sp", lambda e: e.dma_start(out=xt[sl][:, :, a:b], in_=src[:, :, clo:chi]),
                      f"ldx{sl}", reads=rk, writes=[("xt", sl)])
                for c in range(KC):
                    S.op("act", lambda e, c=c: e.activation(
                        out=ht[sl][:, c, a:b], in_=xt[sl][:, c, a:b], func=AF.Identity,
                        scale=sc1[:, s, c:c + 1], bias=modt[:, s, c:c + 1]),
                        reads=[("xt", sl), "sc1", "modt"], writes=[("ht", sl)])
                if a > 0:
                    S.op("pool", lambda e: e.memset(ht[sl][:, :, 0:a], 0.0), writes=[("ht", sl)])
                if b < W:
                    S.op("pool", lambda e: e.memset(ht[sl][:, :, b:W], 0.0), writes=[("ht", sl)])

            def stats_mm(src_bf, src_sq, m, par):
                def f():
                    S.op("pe", lambda e: e.matmul(mean_ps, lhsT=onesb, rhs=src_bf, start=(m == 0), stop=(m == KC - 1)),
                         reads=[("rbf", par), "onesb"], writes=["mean_ps"], inc=False)
                    S.op("pe", lambda e: e.matmul(e2_ps, lhsT=onesb, rhs=src_sq, start=(m == 0), stop=(m == KC - 1)),
                         reads=[("rsq", par), "onesb"], writes=["e2_ps"], inc=True)
                return f

            def ln_scalars(eps, rs):
                S.op("act", lambda e: e.activation(out=msq, in_=mean_ps, func=AF.Square),
                     reads=["mean_ps"], writes=["msq"])
                S.op("dve", lambda e: e.tensor_tensor(out=var, in0=e2_ps, in1=msq, op=ALU.subtract),
                     reads=["e2_ps", "msq"], writes=["var"])
                S.op("act", lambda e: e.activation(out=var, in_=var, func=AF.Sqrt, bias=eps_ap(eps), scale=1.0),
                     reads=["var"], writes=["var"])
                S.op("dve", lambda e: e.reciprocal(out=rs, in_=var), reads=["var"], writes=["rstd"])

            def E1(i, m, yps):
                sl = i % 2
                par = m % 2
                xi = xt[sl][:, m, H:H + T]
                S.op("dve", lambda e: e.scalar_tensor_tensor(
                    out=xi, in0=yps, scalar=g1a[:, s, m:m + 1], in1=xi, op0=ALU.mult, op1=ALU.add),
                    reads=[("yps", par), ("xt", sl), "g1a"], writes=[("xt", sl)])
                S.op("act", lambda e: e.activation(out=rbf[par], in_=xi, func=AF.Copy),
                     reads=[("xt", sl)], writes=[("rbf", par)])
                S.op("act", lambda e: e.activation(out=rsq[par], in_=xi, func=AF.Square),
                     reads=[("xt", sl)], writes=[("rsq", par)])
                S.defer(stats_mm(rbf[par], rsq[par], m, par))

            def E2(i):
                def f():
                    sl = i % 2
                    ln_scalars(EPS_R, rstd)
                    for m in range(KC):
                        xi = xt[sl][:, m, H:H + T]
                        S.op("dve", lambda e, xi=xi: e.tensor_tensor(out=xi, in0=xi, in1=mean_ps, op=ALU.subtract),
                             reads=[("xt", sl), "mean_ps"], writes=[("xt", sl)])
                        S.op("pool", lambda e, xi=xi: e.tensor_tensor(out=xi, in0=xi, in1=rstd, op=ALU.mult),
                             reads=[("xt", sl), "rstd"], writes=[("xt", sl)])
                        S.op("act", lambda e, xi=xi, m=m: e.activation(
                            out=xi, in_=xi, func=AF.Identity, scale=lng[:, m:m + 1], bias=lnb[:, m:m + 1]),
                            reads=[("xt", sl), "pp"], writes=[("xt", sl)])
                    S.dma("sp", lambda e: e.dma_start(out=dst[:, :, i * T:(i + 1) * T], in_=xt[sl][:, :, H:H + T]),
                          f"stx{sl}", reads=[("xt", sl)], writes=[("act", pi, i)])
                return f

            eps_tiles = {}

            def eps_ap(eps):
                return eps_tiles[eps]

            for epsv in (EPS, EPS_R):
                t_ = A.alloc(1, F32)
                eps_tiles[epsv] = t_
                S.op("pool", lambda e, t_=t_, epsv=epsv: e.memset(t_, epsv), writes=["epsc"])

            def body_ffn(i):
                sl = i % 2
                for j in range(FJ):
                    bz = (j % 2) * 2
                    for half in range(2):
                        zp = PS[bz + half][:, 0:W]
                        cidx = half * FJ + j
                        col = cidx * P
                        cb = col // 1408
                        for k in range(KC):
                            S.op("pe", lambda e, zp=zp, k=k, col=col: e.matmul(
                                zp, lhsT=wup[:, k, col:col + P], rhs=ht[sl][:, k, :], start=(k == 0), stop=(k == KC - 1)),
                                reads=[("wup", cb), ("ht", sl)], writes=[("zp", bz + half)], inc=(k == KC - 1))
                        acc = accs[j % 2][half]
                        S.op("act", lambda e, zp=zp, acc=acc, cidx=cidx: e.activation(
                            out=acc, in_=zp[:, 1:1 + T], func=AF.Identity,
                            scale=fdw[:, cidx, 1:2], bias=fdb[:, cidx:cidx + 1]),
                            reads=[("zp", bz + half), "pp"], writes=[("acc", j % 2, half)])
                        S.op("dve", lambda e, zp=zp, acc=acc, cidx=cidx: e.scalar_tensor_tensor(
                            out=acc, in0=zp[:, 0:T], scalar=fdw[:, cidx, 0:1], in1=acc, op0=ALU.mult, op1=ALU.add),
                            reads=[("zp", bz + half), ("acc", j % 2, half)], writes=[("acc", j % 2, half)])
                        S.op("dve", lambda e, zp=zp, acc=acc, cidx=cidx: e.scalar_tensor_tensor(
                            out=acc, in0=zp[:, 2:2 + T], scalar=fdw[:, cidx, 2:3], in1=acc, op0=ALU.mult, op1=ALU.add),
                            reads=[("zp", bz + half), ("acc", j % 2, half)], writes=[("acc", j % 2, half)])
                    ag, av, gg = accs[j % 2]
                    S.op("act", lambda e, ag=ag, gg=gg: e.activation(out=gg, in_=ag, func=AF.Gelu_apprx_tanh),
                         reads=[("acc", j % 2, 0)], writes=[("acc", j % 2, 2)])
                    S.op("pool", lambda e, av=av, gg=gg, j=j: e.tensor_tensor(out=at[:, j, :], in0=gg, in1=av, op=ALU.mult),
                         reads=[("acc", j % 2, 1), ("acc", j % 2, 2)], writes=[("at", j)])
                    S.flush(1)
                    if j == 4:
                        pend_e2()
                    if j == 11 and i + 1 < NT:
                        stage_A(i + 1)
                for m in range(KC):
                    yps = PS[4 + m % 2][:, 0:T]
                    for j in range(FJ):
                        S.op("pe", lambda e, yps=yps, j=j, m=m: e.matmul(
                            yps, lhsT=wdn[:, j, m * P:(m + 1) * P], rhs=at[:, j, :], start=(j == 0), stop=(j == FJ - 1)),
                            reads=["wdn", ("at", j)], writes=[("yps", m % 2)], inc=(j == FJ - 1))
                    if m >= 1:
                        S.flush(1)
                    E1(i, m, yps)

            def body_mixA(i):
                sl = i % 2
                edge = 0 if i == 0 else (1 if i == NT - 1 else None)
                h_ = ht[sl]
                for hd in range(4):
                    ups = PS[hd % 2][:, 0:T]
                    for k in range(KC):
                        S.op("pe", lambda e, ups=ups, k=k, hd=hd: e.matmul(
                            ups, lhsT=win[:, k, hd * P:(hd + 1) * P], rhs=h_[:, k, H:H + T], start=(k == 0), stop=(k == KC - 1)),
                            reads=["win", ("ht", sl)], writes=[("zp", hd % 2)], inc=(k == KC - 1))
                    S.op("act", lambda e, ups=ups, hd=hd: e.activation(out=u_sb[:, hd, :], in_=ups, func=AF.Gelu_apprx_tanh),
                         reads=[("zp", hd % 2)], writes=[("u", hd)])
                for c in range(2):
                    vps = PS[2 + c]
                    for k in range(KC):
                        S.op("pe", lambda e, vps=vps, k=k, c=c: e.matmul(
                            vps[:, :], lhsT=h_[:, k, H + c * P:H + (c + 1) * P], rhs=win[:, k, 512:1024],
                            start=(k == 0), stop=(k == KC - 1)),
                            reads=["win", ("ht", sl)], writes=[("zp", 2 + c)], inc=(k == KC - 1))
                    S.op("act", lambda e, vps=vps, c=c: e.activation(out=gv[c], in_=vps[:, :], func=AF.Gelu_apprx_tanh),
                         reads=[("zp", 2 + c)], writes=[("gv", c)])
                    S.op("dve", lambda e, c=c: e.bn_stats(out=st6[:, 0:6], in_=gv[c]), reads=[("gv", c)], writes=["st6"])
                    S.op("dve", lambda e: e.bn_aggr(out=mv[:, 0:2], in_=st6[:, 0:6]), reads=["st6"], writes=["mv"])
                    S.op("act", lambda e: e.activation(out=mv[:, 2:3], in_=mv[:, 1:2], func=AF.Sqrt, bias=eps_ap(EPS), scale=1.0),
                         reads=["mv", "epsc"], writes=["mv2"])
                    S.op("dve", lambda e: e.reciprocal(out=mv[:, 3:4], in_=mv[:, 2:3]), reads=["mv2"], writes=["mv3"])
                    S.op("dve", lambda e, c=c: e.tensor_scalar(
                        out=gv[c], in0=gv[c], scalar1=mv[:, 0:1], scalar2=mv[:, 3:4], op0=ALU.subtract, op1=ALU.mult),
                        reads=[("gv", c), "mv", "mv3"], writes=[("gv", c)])
                    S.op("pool", lambda e, c=c: e.tensor_tensor(out=gv[c], in0=gv[c], in1=pb[:, PB_SG:PB_SG + 512], op=ALU.mult),
                         reads=[("gv", c), "pb"], writes=[("gv", c)])
                    S.op("pool", lambda e, c=c: e.tensor_tensor(out=vn[c], in0=gv[c], in1=pb[:, PB_SB:PB_SB + 512], op=ALU.add),
                         reads=[("gv", c), "pb"], writes=[("vn", c)])
                    for hd in range(4):
                        mx = PS[4 + hd // 2][:, (hd % 2) * T + c * P:(hd % 2) * T + (c + 1) * P]
                        S.op("pe", lambda e, mx=mx, c=c, hd=hd: e.matmul(
                            mx, lhsT=vn[c][:, hd * P:(hd + 1) * P], rhs=wst[:, hd, :], start=True, stop=True),
                            reads=[("vn", c), "wst"], writes=[("yps", hd // 2)], inc=True)
                for hd in range(4):
                    mxf = PS[4 + hd // 2][:, (hd % 2) * T:(hd % 2 + 1) * T]
                    tf = tmpf[hd % 2]
                    S.op("dve", lambda e, mxf=mxf, tf=tf, hd=hd: e.tensor_tensor(
                        out=tf, in0=mxf, in1=pb[:, PB_BS + hd * T:PB_BS + (hd + 1) * T], op=ALU.add),
                        reads=[("yps", hd // 2), "pb"], writes=[("tmpf", hd % 2)])
                    S.op("pool", lambda e, tf=tf, hd=hd: e.tensor_tensor(out=at[:, hd, :], in0=tf, in1=u_sb[:, hd, :], op=ALU.mult),
                         reads=[("tmpf", hd % 2), ("u", hd)], writes=[("at", hd)])
                S.flush()
                pend_e2()
                if i + 1 < NT:
                    stage_A(i + 1)
                for g in range(4):
                    zps = PS[g % 2][:, 0:W]
                    for k in range(KC):
                        S.op("pe", lambda e, zps=zps, k=k, g=g: e.matmul(
                            zps, lhsT=win[:, k, 1024 + g * P:1024 + (g + 1) * P], rhs=h_[:, k, :], start=(k == 0), stop=(k == KC - 1)),
                            reads=["win", ("ht", sl)], writes=[("zp", g % 2)], inc=(k == KC - 1))
                    z = zb[g % 2]
                    S.op("act", lambda e, zps=zps, z=z: e.activation(out=z, in_=zps, func=AF.Copy),
                         reads=[("zp", g % 2)], writes=[("zb", g % 2)])
                    eng = "dve" if g % 2 == 0 else "pool"
                    spans = [(1, W, 1, 0), (2, W - 1, 1, 1), (4, W - 3, 2, 2), (8, W - 7, 4, 4)]
                    cur = z
                    bufs = [sa, sb]
                    for lv in range(g + 1):
                        a0, a1, dl, dr = spans[lv]
                        o = bufs[lv % 2]
                        if lv == 0:
                            i0, i1 = cur[:, 0:W - 1], cur[:, 1:W]
                        else:
                            i0, i1 = cur[:, a0 - dl:a1 - dl], cur[:, a0 + dr:a1 + dr]
                        S.op(eng, lambda e, o=o, a0=a0, a1=a1, i0=i0, i1=i1: e.tensor_tensor(
                            out=o[:, a0:a1], in0=i0, in1=i1, op=ALU.add),
                            reads=[("zb", g % 2), "sa", "sb"], writes=["sa" if lv % 2 == 0 else "sb"])
                        cur = o
                    pl = pooled[g % 2]
                    wdw = float(2 << g)
                    if edge is None:
                        S.op("dve", lambda e, cur=cur, z=z, pl=pl, wdw=wdw: e.scalar_tensor_tensor(
                            out=pl, in0=cur[:, H:H + T], scalar=1.0 / wdw, in1=z[:, H:H + T], op0=ALU.mult, op1=ALU.subtract),
                            reads=["sa", "sb", ("zb", g % 2)], writes=[("pooled", g % 2)])
                    else:
                        ic = pb[:, PB_IC + (edge * 4 + g) * T:PB_IC + (edge * 4 + g + 1) * T]
                        tf = tmpf[g % 2]
                        S.op(eng, lambda e, cur=cur, tf=tf, ic=ic: e.tensor_tensor(out=tf, in0=cur[:, H:H + T], in1=ic, op=ALU.mult),
                             reads=["sa", "sb", "pb"], writes=[("tmpf", g % 2)])
                        S.op(eng, lambda e, tf=tf, z=z, pl=pl: e.tensor_tensor(out=pl, in0=tf, in1=z[:, H:H + T], op=ALU.subtract),
                             reads=[("tmpf", g % 2), ("zb", g % 2)], writes=[("pooled", g % 2)])
                    pw = PS[2 + g % 2][:, 0:T]
                    S.op("pe", lambda e, pw=pw, pl=pl, g=g: e.matmul(pw, lhsT=plw[:, g, :], rhs=pl, start=True, stop=True),
                         reads=["wst", ("pooled", g % 2)], writes=[("zp", 2 + g % 2)], inc=True)
                    S.op("act", lambda e, pw=pw, g=g: e.activation(out=at[:, 4 + g, :], in_=pw, func=AF.Copy, scale=psc[:, g:g + 1]),
                         reads=[("zp", 2 + g % 2), "pp"], writes=[("at", 4 + g)])
                out_proj(i)

            def out_proj(i):
                for m in range(KC):
                    yps = PS[4 + m % 2][:, 0:T]
                    for k in range(KC):
                        S.op("pe", lambda e, yps=yps, k=k, m=m: e.matmul(
                            yps, lhsT=wout[:, k, m * P:(m + 1) * P], rhs=at[:, k, :], start=(k == 0), stop=(k == KC - 1)),
                            reads=["wout", ("at", k)], writes=[("yps", m % 2)], inc=(k == KC - 1))
                    if m >= 1:
                        S.flush(1)
                    E1(i, m, yps)

            def body_mixC(i):
                sl = i % 2
                h_ = ht[sl]
                for j in range(KC):
                    aps = PS[(j % 2) * 2][:, 0:W]
                    gps = PS[(j % 2) * 2 + 1][:, 0:W]
                    for half, zp in ((0, aps), (1, gps)):
                        col = half * D + j * P
                        for k in range(KC):
                            S.op("pe", lambda e, zp=zp, k=k, col=col: e.matmul(
                                zp, lhsT=win[:, k, col:col + P], rhs=h_[:, k, :], start=(k == 0), stop=(k == KC - 1)),
                                reads=["win", ("ht", sl)], writes=[("zp", (j % 2) * 2 + half)], inc=(k == KC - 1))
                    sgt = sg[j % 2]
                    S.op("act", lambda e, gps=gps, sgt=sgt: e.activation(out=sgt, in_=gps, func=AF.Sigmoid),
                         reads=[("zp", (j % 2) * 2 + 1)], writes=[("sg", j % 2)])
                    S.op("dve", lambda e, aps=aps, sgt=sgt, j=j: e.tensor_tensor(out=zt[:, j, :], in0=aps, in1=sgt, op=ALU.mult),
                         reads=[("zp", (j % 2) * 2), ("sg", j % 2)], writes=[("zt", j)])
                    if j == 1:
                        S.flush()
                        pend_e2()
                for j in range(KC):
                    cps = PS[4 + j % 2][:, 0:T]
                    for k in range(31):
                        S.op("pe", lambda e, cps=cps, j=j, k=k: e.matmul(
                            cps, lhsT=dg[:, j * 31 + k, :], rhs=zt[:, j, k:k + T], start=(k == 0), stop=(k == 30)),
                            reads=[("dg", j), ("zt", j)], writes=[("yps", j % 2)], inc=(k == 30))
                    par = j % 2
                    S.op("act", lambda e, cps=cps, j=j: e.activation(out=cz[:, j, :], in_=cps, func=AF.Identity, bias=cvb[:, j:j + 1], scale=1.0),
                         reads=[("yps", par), "pp"], writes=[("cz", j)])
                    S.op("act", lambda e, j=j, par=par: e.activation(out=rbf[par], in_=cz[:, j, :], func=AF.Copy),
                         reads=[("cz", j)], writes=[("rbf", par)])
                    S.op("act", lambda e, j=j, par=par: e.activation(out=rsq[par], in_=cz[:, j, :], func=AF.Square),
                         reads=[("cz", j)], writes=[("rsq", par)])
                    if j >= 1:
                        S.flush(1)
                    S.defer(stats_mm(rbf[par], rsq[par], j, par))
                    if j == 3 and i + 1 < NT:
                        stage_A(i + 1)
                S.flush()
                ln_scalars(EPS, rstd2)
                for j in range(KC):
                    S.op("dve", lambda e, j=j: e.tensor_tensor(out=cz[:, j, :], in0=cz[:, j, :], in1=mean_ps, op=ALU.subtract),
                         reads=[("cz", j), "mean_ps"], writes=[("cz", j)])
                    S.op("pool", lambda e, j=j: e.tensor_tensor(out=cz[:, j, :], in0=cz[:, j, :], in1=rstd2, op=ALU.mult),
                         reads=[("cz", j), "rstd"], writes=[("cz", j)])
                    S.op("act", lambda e, j=j: e.activation(out=at[:, j, :], in_=cz[:, j, :], func=AF.Silu,
                                                            scale=cvg[:, j:j + 1], bias=cvbb[:, j:j + 1]),
                         reads=[("cz", j), "pp"], writes=[("at", j)])
                out_proj(i)

            pend = []

            def pend_e2():
                while pend:
                    pend.pop(0)()

            body = {"ffn": body_ffn, "mixA": body_mixA, "mixC": body_mixC}[kind]
            stage_A(0)
            for i in range(NT):
                body(i)
                pend.append(E2(i))
                if i == NT - 1:
                    S.flush()
                    pend_e2()
                else:
                    pass
            S.barrier()

        for pi, s in enumerate(phases):
            run_phase(pi, s)

        S.wait_keys("sp", [("act", nph - 1, i) for i in range(NT)])
        S.emit()
    return nc


def _chunkT(v):
    return np.ascontiguousarray(v.reshape(-1, P).T)


def _icnt(S_LEN):
    ic = np.zeros((2, 4, T), np.float32)
    for e in range(2):
        for g in range(4):
            half = 1 << g
            t = np.arange(T) + (0 if e == 0 else S_LEN - T)
            cnt = np.minimum(t + half, S_LEN) - np.maximum(t - half, 0)
            ic[e, g] = 1.0 / cnt
    return ic


def prep_shared(inp, S_LEN):
    f = lambda a: np.ascontiguousarray(np.asarray(a, dtype=np.float32))
    sh = {}
    sh["adaw"] = f(np.stack([inp["mix_ada_w"][0], inp["ffn_ada_w"][0], inp["mix_ada_w"][1], inp["ffn_ada_w"][1]]))
    sh["w_in0"] = f(inp["ab_w_in"][0])
    sh["w_out0"] = f(inp["ab_w_out"][0])
    sh["wsT"] = f(np.transpose(inp["ab_ws"][0], (2, 0, 1)).reshape(P, 512))
    sh["poolw"] = f(np.transpose(inp["ab_pool_w"][0], (1, 0, 2)).reshape(P, 512))
    sh["w_in1"] = f(inp["cv_w_in"][0])
    sh["w_out1"] = f(inp["cv_w_out"][0])
    sh["w_up"] = f(inp["ffn_w_up"])
    sh["w_down"] = f(inp["ffn_w_down"])
    pb = np.zeros((P, NPB), np.float32)
    pb[:, PB_SG:PB_SG + 512] = np.broadcast_to(inp["ab_sgu_ln_g"][0][None, :], (P, 512))
    pb[:, PB_SB:PB_SB + 512] = np.broadcast_to(inp["ab_sgu_ln_b"][0][None, :], (P, 512))
    bs = np.asarray(inp["ab_bs"][0], np.float32)
    pb[:, PB_BS:PB_BS + 1024] = np.broadcast_to(np.tile(bs, (1, 2)).reshape(1, 1024), (P, 1024))
    pb[:, PB_IC:PB_IC + 2048] = np.broadcast_to(_icnt(S_LEN).reshape(1, 2048), (P, 2048))
    pb[:, PB_ID:PB_ID + P] = np.eye(P, dtype=np.float32)
    sh["pb"] = pb
    pp = np.zeros((P, NPP), np.float32)
    adab = [inp["mix_ada_b"][0], inp["ffn_ada_b"][0], inp["mix_ada_b"][1], inp["ffn_ada_b"][1]]
    lng = [inp["mix_ln_g"][0], inp["ffn_ln_g"][0], inp["mix_ln_g"][1], inp["ffn_ln_g"][1]]
    lnb = [inp["mix_ln_b"][0], inp["ffn_ln_b"][0], inp["mix_ln_b"][1], inp["ffn_ln_b"][1]]
    for s in range(4):
        pp[:, OFF_ADAB + 24 * s:OFF_ADAB + 24 * s + 24] = _chunkT(np.asarray(adab[s], np.float32))
        pp[:, OFF_LN + 16 * s:OFF_LN + 16 * s + 8] = _chunkT(np.asarray(lng[s], np.float32))
        pp[:, OFF_LN + 16 * s + 8:OFF_LN + 16 * s + 16] = _chunkT(np.asarray(lnb[s], np.float32))
    pp[:, OFF_PSC:OFF_PSC + 4] = _chunkT(np.asarray(inp["ab_pool_scale"][0], np.float32))
    pp[:, OFF_CVB:OFF_CVB + 8] = _chunkT(np.asarray(inp["cv_dw_b"][0], np.float32))
    pp[:, OFF_CVLN:OFF_CVLN + 8] = _chunkT(np.asarray(inp["cv_ln_g"][0], np.float32))
    pp[:, OFF_CVLN + 8:OFF_CVLN + 16] = _chunkT(np.asarray(inp["cv_ln_b"][0], np.float32))
    dw = np.asarray(inp["cv_dw"][0], np.float32)
    pp[:, OFF_DWT:OFF_DWT + 248] = dw.reshape(31, 8, P).transpose(2, 1, 0).reshape(P, 248)
    for l in range(2):
        fd = np.asarray(inp["ffn_dw"][l], np.float32)
        pp[:, OFF_FDW + 132 * l:OFF_FDW + 132 * (l + 1)] = fd.reshape(3, 44, P).transpose(2, 1, 0).reshape(P, 132)
        pp[:, OFF_FDB + 44 * l:OFF_FDB + 44 * (l + 1)] = _chunkT(np.asarray(inp["ffn_dw_b"][l], np.float32))
    return sh, pp


_NC_CACHE = {}


def kernel(**inputs):
    x = np.asarray(inputs["x"], np.float32)
    c = np.asarray(inputs["c"], np.float32)
    B, S_LEN, _ = x.shape
    sh, pp0 = prep_shared(inputs, S_LEN)
    key = (S_LEN,)
    if key not in _NC_CACHE:
        _NC_CACHE[key] = build_nc(S_LEN)
    nc = _NC_CACHE[key]
    in_maps = []
    for b in range(B):
        pp = pp0.copy()
        pp[:, OFF_C:OFF_C + 8] = _chunkT(c[b])
        m = dict(sh)
        m["pp"] = pp
        m["xT"] = np.ascontiguousarray(x[b].T)
        in_maps.append(m)
    res = run_bass_kernel_spmd(nc, in_maps, core_ids=list(range(B)))
    out = np.empty((B, S_LEN, D), np.float32)
    for b in range(B):
        out[b] = res.results[b]["outT"].T
    return out
```

```python
from contextlib import ExitStack
import numpy as np
import concourse.bass as bass
import concourse.mybir as mybir
from concourse.bass_utils import run_bass_kernel_spmd

F32 = mybir.dt.float32
BF16 = mybir.dt.bfloat16
AF = mybir.ActivationFunctionType
ALU = mybir.AluOpType

P = 128
D = 1024
KC = 8
T = 256
FF = 2816
FJ = 22
ALPHA = 4.0 ** 0.25
EPS = 1e-5
EPS_R = EPS / (ALPHA * ALPHA)

OFF_C, OFF_ADAB, OFF_LN, OFF_PSC, OFF_CVB, OFF_CVLN, OFF_DWT, OFF_FDW, OFF_FDB, OFF_SGG, OFF_SGB, NPP = (
    0, 8, 104, 168, 172, 180, 196, 444, 708, 796, 800, 804)
PB_SG, PB_SB, PB_BS, PB_IC, PB_ID, NPB = 0, 512, 1024, 2048, 4096, 4224

ENGS = ("pe", "act", "dve", "pool", "sp")


class Sched:
    def __init__(self, nc, es):
        self.nc = nc
        self.es = es
        self.ops = {e: [] for e in ENGS}
        self.sems = {}
        self.cnt = {}
        self.known = {e: {} for e in ENGS}
        self.last_w = {}
        self.readers = {}
        self.deferred = []
        for e in ("pe", "act", "dve", "pool"):
            self._sem("E_" + e)

    def _sem(self, name):
        if name not in self.sems:
            self.sems[name] = self.es.enter_context(self.nc.semaphore(name))
            self.cnt[name] = 0
        return self.sems[name]

    def _deps(self, eng, reads, writes, is_dma):
        need = {}
        own = "E_" + eng

        def add(s, v):
            if (not is_dma) and s == own and eng == "pe":
                return
            if need.get(s, 0) < v:
                need[s] = v

        for k in reads:
            lw = self.last_w.get(k)
            if lw is not None:
                add(*lw)
        for k in writes:
            lw = self.last_w.get(k)
            if lw is not None:
                add(*lw)
            for s, v in self.readers.get(k, {}).items():
                add(s, v)
        waits = []
        kn = self.known[eng]
        for s, v in need.items():
            if kn.get(s, 0) < v:
                kn[s] = v
                waits.append((s, v))
        return waits

    def _commit(self, tok, reads, writes):
        s, v = tok
        for k in reads:
            r = self.readers.setdefault(k, {})
            if r.get(s, 0) < v:
                r[s] = v
        for k in writes:
            self.last_w[k] = tok
            self.readers[k] = {}

    def op(self, eng, fn, reads=(), writes=(), inc=True):
        waits = self._deps(eng, reads, writes, False)
        s = "E_" + eng
        tok = (s, self.cnt[s] + 1)
        if inc:
            self.cnt[s] += 1
        self.ops[eng].append((waits, fn, (s, 1) if inc else None))
        self._commit(tok, reads, writes)
        return tok

    def dma(self, eng, fn, sem, reads=(), writes=()):
        self._sem(sem)
        waits = self._deps(eng, reads, writes, True)
        self.cnt[sem] += 16
        tok = (sem, self.cnt[sem])
        self.ops[eng].append((waits, fn, (sem, 16)))
        self._commit(tok, reads, writes)
        return tok

    def barrier(self):
        for e in ENGS:
            waits = []
            kn = self.known[e]
            for s, v in self.cnt.items():
                if v > 0 and s != "E_" + e and kn.get(s, 0) < v:
                    kn[s] = v
                    waits.append((s, v))
            self.ops[e].append((waits, None, None))

    def wait_keys(self, eng, keys):
        waits = self._deps(eng, keys, (), True)
        self.ops[eng].append((waits, None, None))

    def defer(self, fn):
        self.deferred.append(fn)

    def flush(self, n=None):
        k = len(self.deferred) if n is None else min(n, len(self.deferred))
        for _ in range(k):
            self.deferred.pop(0)()

    def emit(self):
        nc = self.nc
        with nc.Block() as block:
            def run(engname):
                def body(e):
                    for waits, fn, inc in self.ops[engname]:
                        for s, v in waits:
                            e.wait_ge(self.sems[s], v)
                        if fn is None:
                            continue
                        inst = fn(e)
                        if inc is not None:
                            inst.then_inc(self.sems[inc[0]], inc[1])
                return body
            block.tensor(run("pe"))
            block.scalar(run("act"))
            block.vector(run("dve"))
            block.gpsimd(run("pool"))
            block.sync(run("sp"))


class Arena:
    def __init__(self, handle, n32):
        self.h = handle
        self.n = n32
        self.off = 0

    def alloc(self, nelem, dt):
        nb = nelem * (2 if dt == BF16 else 4)
        n32 = (nb + 15) // 16 * 4
        assert self.off + n32 <= self.n, f"arena overflow {self.off}+{n32}>{self.n}"
        ap = self.h[:, self.off:self.off + n32]
        self.off += n32
        if dt == BF16:
            ap = ap.bitcast(BF16)
        return ap[:, 0:nelem]


def build_nc(S_LEN=8192, phases=(0, 1, 2, 3)):
    NT = S_LEN // T
    nc = bass.Bass("TRN2", target_bir_lowering=False)

    def dram(name, shape, kind="ExternalInput"):
        return nc.dram_tensor(name, shape, F32, kind=kind).ap()

    xT = dram("xT", [D, S_LEN])
    outT = dram("outT", [D, S_LEN], "ExternalOutput")
    pp_d = dram("pp", [P, NPP])
    pb_d = dram("pb", [P, NPB])
    adaw_d = dram("adaw", [4, D, 3 * D])
    w_in0_d = dram("w_in0", [D, 1536])
    w_out0_d = dram("w_out0", [D, D])
    wsT_d = dram("wsT", [P, 512])
    poolw_d = dram("poolw", [P, 512])
    w_in1_d = dram("w_in1", [D, 2048])
    w_out1_d = dram("w_out1", [D, D])
    w_up_d = dram("w_up", [2, D, 2 * FF])
    w_down_d = dram("w_down", [2, FF, D])
    nph = len(phases)
    acts = [dram(f"act{i}", [D, S_LEN], "Internal") for i in range(max(nph - 1, 0))]
    srcs = [xT] + acts
    dsts = acts + [outT]

    with ExitStack() as es:
        S = Sched(nc, es)
        ARENA32 = 50176
        arena_h = es.enter_context(nc.sbuf_tensor("arena", [P, ARENA32], F32))
        A = Arena(arena_h, ARENA32)
        PS = [es.enter_context(nc.psum_tensor(f"ps{i}", [P, 512], F32)) for i in range(8)]

        pp = A.alloc(NPP, F32)
        modt = A.alloc(4 * 24, F32).rearrange("p (s f) -> p s f", s=4)
        sc1 = A.alloc(4 * 8, F32).rearrange("p (s f) -> p s f", s=4)
        g1a = A.alloc(4 * 8, F32).rearrange("p (s f) -> p s f", s=4)
        csil = A.alloc(8, F32)
        onesb = A.alloc(P, BF16)
        persist_mark = A.off

        S.dma("sp", lambda e: e.dma_start(out=pp, in_=pp_d), "ldpp", writes=["pp"])
        S.op("pool", lambda e: e.memset(onesb, 1.0 / D), writes=["onesb"])

        stg = [A.alloc(8 * 512, F32).rearrange("p (k f) -> p k f", k=8) for _ in range(2)]
        S.op("act", lambda e: e.activation(out=csil, in_=pp[:, OFF_C:OFF_C + 8], func=AF.Silu),
             reads=["pp"], writes=["csil"])
        gi = 0
        for s in range(4):
            mps = PS[s % 2]
            for g in range(6):
                st = stg[gi % 2]
                src = adaw_d[s].rearrange("(k p) f -> p k f", p=P)[:, :, g * 512:(g + 1) * 512]
                S.dma("sp", lambda e, st=st, src=src: e.dma_start(out=st, in_=src), f"ldst{gi % 2}",
                      writes=[("stg", gi % 2)])
                for fl in range(4):
                    fc = g * 4 + fl
                    for k in range(KC):
                        S.op("pe", lambda e, st=st, k=k, fl=fl, fc=fc, mps=mps: e.matmul(
                            mps[:, fc:fc + 1], lhsT=st[:, k, fl * P:(fl + 1) * P], rhs=csil[:, k:k + 1],
                            start=(k == 0), stop=(k == KC - 1)),
                            reads=[("stg", gi % 2), "csil"], writes=[("mps", s % 2)], inc=(k == KC - 1))
                gi += 1
            S.op("dve", lambda e, s=s, mps=mps: e.tensor_tensor(
                out=modt[:, s, :], in0=mps[:, 0:24], in1=pp[:, OFF_ADAB + 24 * s:OFF_ADAB + 24 * s + 24],
                op=ALU.add), reads=[("mps", s % 2), "pp"], writes=["modt"])
            S.op("dve", lambda e, s=s: e.tensor_scalar(
                out=sc1[:, s, :], in0=modt[:, s, 8:16], scalar1=1.0, scalar2=None, op0=ALU.add),
                reads=["modt"], writes=["sc1"])
            S.op("dve", lambda e, s=s: e.tensor_scalar(
                out=g1a[:, s, :], in0=modt[:, s, 16:24], scalar1=1.0, scalar2=1.0 / ALPHA,
                op0=ALU.add, op1=ALU.mult), reads=["modt"], writes=["g1a"])
        S.barrier()

        def load_weights_cast(dst2d, src2d, sem, key=None, last=False):
            S.dma("pool", lambda e: e.dma_start(out=dst2d, in_=src2d), sem,
                  writes=[key] if (last and key is not None) else [])

        def run_phase(pi, s):
            A.off = persist_mark
            src = srcs[pi].rearrange("(c p) t -> p c t", p=P)
            dst = dsts[pi].rearrange("(c p) t -> p c t", p=P)
            kind = ("mixA", "ffn", "mixC", "ffn")[s]
            H = {"mixA": 8, "ffn": 1, "mixC": 15}[kind]
            W = T + 2 * H
            lay = s // 2
            lng = pp[:, OFF_LN + s * 16:OFF_LN + s * 16 + 8]
            lnb = pp[:, OFF_LN + s * 16 + 8:OFF_LN + s * 16 + 16]

            xt = [A.alloc(KC * W, F32).rearrange("p (c w) -> p c w", c=KC) for _ in range(2)]
            ht = [A.alloc(KC * W, BF16).rearrange("p (c w) -> p c w", c=KC) for _ in range(2)]
            rbf = [A.alloc(T, BF16) for _ in range(2)]
            rsq = [A.alloc(T, BF16) for _ in range(2)]
            msq = A.alloc(T, F32)
            var = A.alloc(T, F32)
            rstd = A.alloc(T, F32)
            mean_ps = PS[6][:, 0:T]
            e2_ps = PS[7][:, 0:T]

            if kind == "ffn":
                wup = A.alloc(KC * 2 * FF, BF16).rearrange("p (k f) -> p k f", k=KC)
                wdn = A.alloc(FJ * D, BF16).rearrange("p (j d) -> p j d", j=FJ)
                at = A.alloc(FJ * T, BF16).rearrange("p (j t) -> p j t", j=FJ)
                accs = [[A.alloc(T, F32) for _ in range(3)] for _ in range(2)]
                CB = 1408
                for cb in (0, 2, 1, 3):
                    for k in range(KC):
                        load_weights_cast(wup[:, k, cb * CB:(cb + 1) * CB],
                                          w_up_d[lay, k * P:(k + 1) * P, cb * CB:(cb + 1) * CB],
                                          f"ldw{cb}", ("wup", cb), last=(k == KC - 1))
                for j in range(FJ):
                    load_weights_cast(wdn[:, j, :], w_down_d[lay, j * P:(j + 1) * P, :], "ldw4", "wdn",
                                      last=(j == FJ - 1))
                fdw = pp[:, OFF_FDW + lay * 132:OFF_FDW + (lay + 1) * 132].rearrange("p (c k) -> p c k", k=3)
                fdb = pp[:, OFF_FDB + lay * 44:OFF_FDB + (lay + 1) * 44]
            elif kind == "mixA":
                win = A.alloc(KC * 1536, BF16).rearrange("p (k f) -> p k f", k=KC)
                wout = A.alloc(KC * D, BF16).rearrange("p (k f) -> p k f", k=KC)
                wst = A.alloc(512, BF16).rearrange("p (h q) -> p h q", h=4)
                plw = A.alloc(512, BF16).rearrange("p (g d) -> p g d", g=4)
                at = A.alloc(KC * T, BF16).rearrange("p (j t) -> p j t", j=KC)
                pb = A.alloc(NPB, F32)
                u_sb = A.alloc(4 * T, F32).rearrange("p (h t) -> p h t", h=4)
                gv = [A.alloc(512, F32) for _ in range(2)]
                vn = [A.alloc(512, BF16) for _ in range(2)]
                zb = [A.alloc(W, F32) for _ in range(2)]
                sab = [[A.alloc(W, F32) for _ in range(2)] for _ in range(2)]
                tmpf = [A.alloc(T, F32) for _ in range(2)]
                pooled = [A.alloc(T, BF16) for _ in range(2)]
                st6 = [A.alloc(8, F32) for _ in range(2)]
                mv = [A.alloc(4, F32) for _ in range(2)]
                mhalf = A.alloc(1, F32)
                ones1 = A.alloc(P, BF16)
                cbt = A.alloc(4 * T, F32).rearrange("p (h t) -> p h t", h=4)
                sgg = pp[:, OFF_SGG:OFF_SGG + 4]
                sgb = pp[:, OFF_SGB:OFF_SGB + 4]
                S.op("pool", lambda e: e.memset(mhalf, -0.5), writes=["mhalf"])
                S.op("pool", lambda e: e.memset(ones1, 1.0), writes=["ones1"])
                S.dma("sp", lambda e: e.dma_start(out=pb, in_=pb_d), "ldpb", writes=["pb"])
                for k in range(KC):
                    load_weights_cast(win[:, k, :], w_in0_d[k * P:(k + 1) * P, :], "ldw0", "win", last=(k == KC - 1))
                load_weights_cast(wst.rearrange("p h q -> p (h q)"), wsT_d, "ldw1", "wst", last=True)
                load_weights_cast(plw.rearrange("p g d -> p (g d)"), poolw_d, "ldw1", "wst", last=True)
                for k in range(KC):
                    load_weights_cast(wout[:, k, :], w_out0_d[k * P:(k + 1) * P, :], "ldw2", "wout", last=(k == KC - 1))
                psc = pp[:, OFF_PSC:OFF_PSC + 4]
                for hd in range(4):
                    rs_ps = PS[hd % 2][:, 0:P]
                    S.op("pe", lambda e, rs_ps=rs_ps, hd=hd: e.matmul(rs_ps, lhsT=ones1, rhs=wst[:, hd, :], start=True, stop=True),
                         reads=["ones1", "wst"], writes=[("zp", hd % 2)], inc=True)
                    for c in range(2):
                        S.op("dve", lambda e, rs_ps=rs_ps, hd=hd, c=c: e.scalar_tensor_tensor(
                            out=cbt[:, hd, c * P:(c + 1) * P], in0=rs_ps, scalar=sgb[:, hd:hd + 1],
                            in1=pb[:, PB_BS + hd * T + c * P:PB_BS + hd * T + (c + 1) * P], op0=ALU.mult, op1=ALU.add),
                            reads=[("zp", hd % 2), "pp", "pb"], writes=["cbt"])
            else:
                win = A.alloc(KC * 2048, BF16).rearrange("p (k f) -> p k f", k=KC)
                wout = A.alloc(KC * D, BF16).rearrange("p (k f) -> p k f", k=KC)
                dg = A.alloc(248 * P, BF16).rearrange("p (n d) -> p n d", n=248)
                at = A.alloc(KC * T, BF16).rearrange("p (j t) -> p j t", j=KC)
                zt = A.alloc(KC * W, BF16).rearrange("p (j w) -> p j w", j=KC)
                cz = A.alloc(KC * T, F32).rearrange("p (j t) -> p j t", j=KC)
                sg = [A.alloc(W, F32) for _ in range(2)]
                ident = A.alloc(P, F32)
                rstd2 = A.alloc(T, F32)
                S.dma("sp", lambda e: e.dma_start(out=ident, in_=pb_d[:, PB_ID:PB_ID + P]), "ldpb", writes=["ident"])
                for k in range(KC):
                    load_weights_cast(win[:, k, :], w_in1_d[k * P:(k + 1) * P, :], "ldw0", "win", last=(k == KC - 1))
                for k in range(KC):
                    load_weights_cast(wout[:, k, :], w_out1_d[k * P:(k + 1) * P, :], "ldw2", "wout", last=(k == KC - 1))
                dwt = pp[:, OFF_DWT:OFF_DWT + 248]
                for n in range(248):
                    eng = "dve" if n % 2 == 0 else "pool"
                    S.op(eng, lambda e, n=n: e.tensor_scalar(
                        out=dg[:, n, :], in0=ident, scalar1=dwt[:, n:n + 1], scalar2=None, op0=ALU.mult),
                        reads=["ident", "pp"], writes=[("dg", n // 31)])
                cvb = pp[:, OFF_CVB:OFF_CVB + 8]
                cvg = pp[:, OFF_CVLN:OFF_CVLN + 8]
                cvbb = pp[:, OFF_CVLN + 8:OFF_CVLN + 16]

            def stage_A(i):
                sl = i % 2
                lo, hi = i * T - H, i * T + T + H
                clo, chi = max(lo, 0), min(hi, S_LEN)
                a, b = clo - lo, chi - lo
                rk = [("act", pi - 1, ii) for ii in (i - 1, i, i + 1) if 0 <= ii < NT] if pi > 0 else []
                S.dma("sp", lambda e: e.dma_start(out=xt[sl][:, :, a:b], in_=src[:, :, clo:chi]),
                      f"ldx{sl}", reads=rk, writes=[("xt", sl, c) for c in range(KC)])
                for c in range(KC):
                    if c % 2 == 0:
                        S.op("act", lambda e, c=c: e.activation(
                            out=ht[sl][:, c, a:b], in_=xt[sl][:, c, a:b], func=AF.Identity,
                            scale=sc1[:, s, c:c + 1], bias=modt[:, s, c:c + 1]),
                            reads=[("xt", sl, c), "sc1", "modt"], writes=[("ht", sl, c)])
                    else:
                        S.op("dve", lambda e, c=c: e.tensor_scalar(
                            out=ht[sl][:, c, a:b], in0=xt[sl][:, c, a:b],
                            scalar1=sc1[:, s, c:c + 1], scalar2=modt[:, s, c:c + 1], op0=ALU.mult, op1=ALU.add),
                            reads=[("xt", sl, c), "sc1", "modt"], writes=[("ht", sl, c)])
                if a > 0:
                    S.op("pool", lambda e: e.memset(ht[sl][:, :, 0:a], 0.0), writes=[("ht", sl, c) for c in range(KC)])
                if b < W:
                    S.op("pool", lambda e: e.memset(ht[sl][:, :, b:W], 0.0), writes=[("ht", sl, c) for c in range(KC)])

            def stats_mm(src_bf, src_sq, m, par):
                def f():
                    S.op("pe", lambda e: e.matmul(mean_ps, lhsT=onesb, rhs=src_bf, start=(m == 0), stop=(m == KC - 1)),
                         reads=[("rbf", par), "onesb"], writes=["mean_ps"], inc=False)
                    S.op("pe", lambda e: e.matmul(e2_ps, lhsT=onesb, rhs=src_sq, start=(m == 0), stop=(m == KC - 1)),
                         reads=[("rsq", par), "onesb"], writes=["e2_ps"], inc=True)
                return f

            def ln_scalars(eps, rs):
                S.op("act", lambda e: e.activation(out=msq, in_=mean_ps, func=AF.Square),
                     reads=["mean_ps"], writes=["msq"])
                S.op("dve", lambda e: e.tensor_tensor(out=var, in0=e2_ps, in1=msq, op=ALU.subtract),
                     reads=["e2_ps", "msq"], writes=["var"])
                S.op("act", lambda e: e.activation(out=var, in_=var, func=AF.Sqrt, bias=eps_ap(eps), scale=1.0),
                     reads=["var"], writes=["var"])
                S.op("dve", lambda e: e.reciprocal(out=rs, in_=var), reads=["var"], writes=["rstd"])

            def E1(i, m, yps):
                sl = i % 2
                par = m % 2
                xi = xt[sl][:, m, H:H + T]
                S.op("dve", lambda e: e.scalar_tensor_tensor(
                    out=xi, in0=yps, scalar=g1a[:, s, m:m + 1], in1=xi, op0=ALU.mult, op1=ALU.add),
                    reads=[("yps", par), ("xt", sl, m), "g1a"], writes=[("xt", sl, m)])
                S.op("pool", lambda e: e.tensor_copy(out=rbf[par], in_=xi),
                     reads=[("xt", sl, m)], writes=[("rbf", par)])
                S.op("act", lambda e: e.activation(out=rsq[par], in_=xi, func=AF.Square),
                     reads=[("xt", sl, m)], writes=[("rsq", par)])
                S.defer(stats_mm(rbf[par], rsq[par], m, par))

            def E2(i):
                sl = i % 2

                def piece(m):
                    def f():
                        xi = xt[sl][:, m, H:H + T]
                        S.op("dve", lambda e: e.tensor_tensor(out=xi, in0=xi, in1=mean_ps, op=ALU.subtract),
                             reads=[("xt", sl, m), "mean_ps"], writes=[("xt", sl, m)])
                        S.op("pool", lambda e: e.tensor_tensor(out=xi, in0=xi, in1=rstd, op=ALU.mult),
                             reads=[("xt", sl, m), "rstd"], writes=[("xt", sl, m)])
                        S.op("act", lambda e: e.activation(
                            out=xi, in_=xi, func=AF.Identity, scale=lng[:, m:m + 1], bias=lnb[:, m:m + 1]),
                            reads=[("xt", sl, m), "pp"], writes=[("xt", sl, m)])
                    return f

                def store():
                    S.dma("sp", lambda e: e.dma_start(out=dst[:, :, i * T:(i + 1) * T], in_=xt[sl][:, :, H:H + T]),
                          f"stx{sl}", reads=[("xt", sl, c) for c in range(KC)], writes=[("act", pi, i)])
                return [lambda: ln_scalars(EPS_R, rstd)] + [piece(m) for m in range(KC)] + [store]

            eps_tiles = {}

            def eps_ap(eps):
                return eps_tiles[eps]

            for epsv in (EPS, EPS_R):
                t_ = A.alloc(1, F32)
                eps_tiles[epsv] = t_
                S.op("pool", lambda e, t_=t_, epsv=epsv: e.memset(t_, epsv), writes=["epsc"])

            def body_ffn(i):
                sl = i % 2
                for j in range(FJ):
                    bz = (j % 2) * 2
                    for half in range(2):
                        zp = PS[bz + half][:, 0:W]
                        cidx = half * FJ + j
                        col = cidx * P
                        cb = col // 1408
                        for k in range(KC):
                            S.op("pe", lambda e, zp=zp, k=k, col=col: e.matmul(
                                zp, lhsT=wup[:, k, col:col + P], rhs=ht[sl][:, k, :], start=(k == 0), stop=(k == KC - 1)),
                                reads=[("wup", cb), ("ht", sl, k)], writes=[("zp", bz + half)], inc=(k == KC - 1))
                        acc = accs[j % 2][half]
                        S.op("act", lambda e, zp=zp, acc=acc, cidx=cidx: e.activation(
                            out=acc, in_=zp[:, 1:1 + T], func=AF.Identity,
                            scale=fdw[:, cidx, 1:2], bias=fdb[:, cidx:cidx + 1]),
                            reads=[("zp", bz + half), "pp"], writes=[("acc", j % 2, half)])
                    for tap in (0, 2):
                        for half in range(2):
                            zp = PS[bz + half][:, 0:W]
                            cidx = half * FJ + j
                            acc = accs[j % 2][half]
                            S.op("dve", lambda e, zp=zp, acc=acc, cidx=cidx, tap=tap: e.scalar_tensor_tensor(
                                out=acc, in0=zp[:, tap:tap + T], scalar=fdw[:, cidx, tap:tap + 1], in1=acc,
                                op0=ALU.mult, op1=ALU.add),
                                reads=[("zp", bz + half), ("acc", j % 2, half)], writes=[("acc", j % 2, half)])
                    ag, av, gg = accs[j % 2]
                    S.op("act", lambda e, ag=ag, gg=gg: e.activation(out=gg, in_=ag, func=AF.Gelu_apprx_tanh),
                         reads=[("acc", j % 2, 0)], writes=[("acc", j % 2, 2)])
                    S.op("pool", lambda e, av=av, gg=gg, j=j: e.tensor_tensor(out=at[:, j, :], in0=gg, in1=av, op=ALU.mult),
                         reads=[("acc", j % 2, 1), ("acc", j % 2, 2)], writes=[("at", j)])
                    S.flush(1)
                    if j >= 2:
                        pend_e2(1)
                    if j == 13 and i + 1 < NT:
                        stage_A(i + 1)
                for m in range(KC):
                    yps = PS[4 + m % 2][:, 0:T]
                    for j in range(FJ):
                        S.op("pe", lambda e, yps=yps, j=j, m=m: e.matmul(
                            yps, lhsT=wdn[:, j, m * P:(m + 1) * P], rhs=at[:, j, :], start=(j == 0), stop=(j == FJ - 1)),
                            reads=["wdn", ("at", j)], writes=[("yps", m % 2)], inc=(j == FJ - 1))
                    if m >= 1:
                        S.flush(1)
                    E1(i, m, yps)

            def body_mixA(i):
                sl = i % 2
                edge = 0 if i == 0 else (1 if i == NT - 1 else None)
                h_ = ht[sl]
                hk = [("ht", sl, k) for k in range(KC)]
                for c in range(2):
                    vps = PS[2 + c]
                    for k in range(KC):
                        S.op("pe", lambda e, vps=vps, k=k, c=c: e.matmul(
                            vps[:, :], lhsT=h_[:, k, H + c * P:H + (c + 1) * P], rhs=win[:, k, 512:1024],
                            start=(k == 0), stop=(k == KC - 1)),
                            reads=["win", ("ht", sl, k)], writes=[("zp", 2 + c)], inc=(k == KC - 1))
                    S.flush()
                    pend_e2(1)
                    S.op("act", lambda e, vps=vps, c=c: e.activation(out=gv[c], in_=vps[:, :], func=AF.Gelu_apprx_tanh),
                         reads=[("zp", 2 + c)], writes=[("gv", c)])
                    S.op("dve", lambda e, c=c: e.bn_stats(out=st6[c][:, 0:6], in_=gv[c]), reads=[("gv", c)], writes=[("st6", c)])
                    S.op("dve", lambda e, c=c: e.bn_aggr(out=mv[c][:, 0:2], in_=st6[c][:, 0:6]), reads=[("st6", c)], writes=[("mv", c)])
                    S.op("pool", lambda e, c=c: e.tensor_scalar(out=mv[c][:, 2:3], in0=mv[c][:, 1:2], scalar1=EPS, scalar2=None, op0=ALU.add),
                         reads=[("mv", c)], writes=[("mv2", c)])
                    S.op("pool", lambda e, c=c: e.tensor_tensor(out=mv[c][:, 3:4], in0=mv[c][:, 2:3], in1=mhalf, op=ALU.pow),
                         reads=[("mv2", c), "mhalf"], writes=[("mv3", c)])
                    S.op("dve", lambda e, c=c: e.tensor_scalar(
                        out=vn[c], in0=gv[c], scalar1=mv[c][:, 0:1], scalar2=mv[c][:, 3:4], op0=ALU.subtract, op1=ALU.mult),
                        reads=[("gv", c), ("mv", c), ("mv3", c)], writes=[("vn", c)])
                for hd in range(4):
                    ups = PS[hd % 2][:, 0:T]
                    for k in range(KC):
                        S.op("pe", lambda e, ups=ups, k=k, hd=hd: e.matmul(
                            ups, lhsT=win[:, k, hd * P:(hd + 1) * P], rhs=h_[:, k, H:H + T], start=(k == 0), stop=(k == KC - 1)),
                            reads=["win", ("ht", sl, k)], writes=[("zp", hd % 2)], inc=(k == KC - 1))
                    pend_e2(1)
                    S.op("act", lambda e, ups=ups, hd=hd: e.activation(out=u_sb[:, hd, :], in_=ups, func=AF.Gelu_apprx_tanh),
                         reads=[("zp", hd % 2)], writes=[("u", hd)])
                S.flush()
                for c in range(2):
                    for hd in range(4):
                        mx = PS[4 + hd // 2][:, (hd % 2) * T + c * P:(hd % 2) * T + (c + 1) * P]
                        S.op("pe", lambda e, mx=mx, c=c, hd=hd: e.matmul(
                            mx, lhsT=vn[c][:, hd * P:(hd + 1) * P], rhs=wst[:, hd, :], start=True, stop=True),
                            reads=[("vn", c), "wst"], writes=[("yps", hd // 2)], inc=True)
                for hd in range(4):
                    mxf = PS[4 + hd // 2][:, (hd % 2) * T:(hd % 2 + 1) * T]
                    tf = tmpf[hd % 2]
                    S.op("dve", lambda e, mxf=mxf, tf=tf, hd=hd: e.scalar_tensor_tensor(
                        out=tf, in0=mxf, scalar=sgg[:, hd:hd + 1], in1=cbt[:, hd, :], op0=ALU.mult, op1=ALU.add),
                        reads=[("yps", hd // 2), "cbt", "pp"], writes=[("tmpf", hd % 2)])
                    S.op("pool", lambda e, tf=tf, hd=hd: e.tensor_tensor(out=at[:, hd, :], in0=tf, in1=u_sb[:, hd, :], op=ALU.mult),
                         reads=[("tmpf", hd % 2), ("u", hd)], writes=[("at", hd)])
                for g in range(4):
                    zps = PS[g % 2][:, 0:W]
                    for k in range(KC):
                        S.op("pe", lambda e, zps=zps, k=k, g=g: e.matmul(
                            zps, lhsT=win[:, k, 1024 + g * P:1024 + (g + 1) * P], rhs=h_[:, k, :], start=(k == 0), stop=(k == KC - 1)),
                            reads=["win", ("ht", sl, k)], writes=[("zp", g % 2)], inc=(k == KC - 1))
                    pend_e2(1)
                    z = zb[g % 2]
                    S.op("act", lambda e, zps=zps, z=z: e.activation(out=z, in_=zps, func=AF.Copy),
                         reads=[("zp", g % 2)], writes=[("zb", g % 2)])
                    eng = "dve" if g % 2 == 0 else "pool"
                    spans = [(1, W, 1, 0), (2, W - 1, 1, 1), (4, W - 3, 2, 2), (8, W - 7, 4, 4)]
                    cur = z
                    bufs = sab[g % 2]
                    for lv in range(g + 1):
                        a0, a1, dl, dr = spans[lv]
                        o = bufs[lv % 2]
                        if lv == 0:
                            i0, i1 = cur[:, 0:W - 1], cur[:, 1:W]
                        else:
                            i0, i1 = cur[:, a0 - dl:a1 - dl], cur[:, a0 + dr:a1 + dr]
                        S.op(eng, lambda e, o=o, a0=a0, a1=a1, i0=i0, i1=i1: e.tensor_tensor(
                            out=o[:, a0:a1], in0=i0, in1=i1, op=ALU.add),
                            reads=[("zb", g % 2), ("sab", g % 2, 0), ("sab", g % 2, 1)], writes=[("sab", g % 2, lv % 2)])
                        cur = o
                    pl = pooled[g % 2]
                    wdw = float(2 << g)
                    if edge is None:
                        S.op("dve", lambda e, cur=cur, z=z, pl=pl, wdw=wdw: e.scalar_tensor_tensor(
                            out=pl, in0=cur[:, H:H + T], scalar=1.0 / wdw, in1=z[:, H:H + T], op0=ALU.mult, op1=ALU.subtract),
                            reads=[("sab", g % 2, 0), ("sab", g % 2, 1), ("zb", g % 2)], writes=[("pooled", g % 2)])
                    else:
                        ic = pb[:, PB_IC + (edge * 4 + g) * T:PB_IC + (edge * 4 + g + 1) * T]
                        tf = tmpf[g % 2]
                        S.op(eng, lambda e, cur=cur, tf=tf, ic=ic: e.tensor_tensor(out=tf, in0=cur[:, H:H + T], in1=ic, op=ALU.mult),
                             reads=[("sab", g % 2, 0), ("sab", g % 2, 1), "pb"], writes=[("tmpf", g % 2)])
                        S.op(eng, lambda e, tf=tf, z=z, pl=pl: e.tensor_tensor(out=pl, in0=tf, in1=z[:, H:H + T], op=ALU.subtract),
                             reads=[("tmpf", g % 2), ("zb", g % 2)], writes=[("pooled", g % 2)])
                    pw = PS[2 + g % 2][:, 0:T]
                    S.op("pe", lambda e, pw=pw, pl=pl, g=g: e.matmul(pw, lhsT=plw[:, g, :], rhs=pl, start=True, stop=True),
                         reads=["wst", ("pooled", g % 2)], writes=[("zp", 2 + g % 2)], inc=True)
                    S.op("act", lambda e, pw=pw, g=g: e.activation(out=at[:, 4 + g, :], in_=pw, func=AF.Copy, scale=psc[:, g:g + 1]),
                         reads=[("zp", 2 + g % 2), "pp"], writes=[("at", 4 + g)])
                pend_e2()
                if i + 1 < NT:
                    stage_A(i + 1)
                out_proj(i)

            def out_proj(i):
                for m in range(KC):
                    yps = PS[4 + m % 2][:, 0:T]
                    for k in range(KC):
                        S.op("pe", lambda e, yps=yps, k=k, m=m: e.matmul(
                            yps, lhsT=wout[:, k, m * P:(m + 1) * P], rhs=at[:, k, :], start=(k == 0), stop=(k == KC - 1)),
                            reads=["wout", ("at", k)], writes=[("yps", m % 2)], inc=(k == KC - 1))
                    if m >= 1:
                        S.flush(1)
                    E1(i, m, yps)

            def body_mixC(i):
                sl = i % 2
                h_ = ht[sl]
                for j in range(KC):
                    aps = PS[(j % 2) * 2][:, 0:W]
                    gps = PS[(j % 2) * 2 + 1][:, 0:W]
                    for half, zp in ((0, aps), (1, gps)):
                        col = half * D + j * P
                        for k in range(KC):
                            S.op("pe", lambda e, zp=zp, k=k, col=col: e.matmul(
                                zp, lhsT=win[:, k, col:col + P], rhs=h_[:, k, :], start=(k == 0), stop=(k == KC - 1)),
                                reads=["win", ("ht", sl, k)], writes=[("zp", (j % 2) * 2 + half)], inc=(k == KC - 1))
                    sgt = sg[j % 2]
                    S.op("act", lambda e, gps=gps, sgt=sgt: e.activation(out=sgt, in_=gps, func=AF.Sigmoid),
                         reads=[("zp", (j % 2) * 2 + 1)], writes=[("sg", j % 2)])
                    S.op("dve", lambda e, aps=aps, sgt=sgt, j=j: e.tensor_tensor(out=zt[:, j, :], in0=aps, in1=sgt, op=ALU.mult),
                         reads=[("zp", (j % 2) * 2), ("sg", j % 2)], writes=[("zt", j)])
                    if j == 0:
                        S.flush()
                    pend_e2(2)
                for j in range(KC):
                    cps = PS[4 + j % 2][:, 0:T]
                    for k in range(31):
                        S.op("pe", lambda e, cps=cps, j=j, k=k: e.matmul(
                            cps, lhsT=dg[:, j * 31 + k, :], rhs=zt[:, j, k:k + T], start=(k == 0), stop=(k == 30)),
                            reads=[("dg", j), ("zt", j)], writes=[("yps", j % 2)], inc=(k == 30))
                    par = j % 2
                    S.op("act", lambda e, cps=cps, j=j: e.activation(out=cz[:, j, :], in_=cps, func=AF.Identity, bias=cvb[:, j:j + 1], scale=1.0),
                         reads=[("yps", par), "pp"], writes=[("cz", j)])
                    S.op("pool", lambda e, j=j, par=par: e.tensor_copy(out=rbf[par], in_=cz[:, j, :]),
                         reads=[("cz", j)], writes=[("rbf", par)])
                    S.op("act", lambda e, j=j, par=par: e.activation(out=rsq[par], in_=cz[:, j, :], func=AF.Square),
                         reads=[("cz", j)], writes=[("rsq", par)])
                    if j >= 1:
                        S.flush(1)
                    S.defer(stats_mm(rbf[par], rsq[par], j, par))
                    if j == 3 and i + 1 < NT:
                        stage_A(i + 1)
                S.flush()
                ln_scalars(EPS, rstd2)
                for j in range(KC):
                    S.op("dve", lambda e, j=j: e.tensor_tensor(out=cz[:, j, :], in0=cz[:, j, :], in1=mean_ps, op=ALU.subtract),
                         reads=[("cz", j), "mean_ps"], writes=[("cz", j)])
                    S.op("pool", lambda e, j=j: e.tensor_tensor(out=cz[:, j, :], in0=cz[:, j, :], in1=rstd2, op=ALU.mult),
                         reads=[("cz", j), "rstd"], writes=[("cz", j)])
                    S.op("act", lambda e, j=j: e.activation(out=at[:, j, :], in_=cz[:, j, :], func=AF.Silu,
                                                            scale=cvg[:, j:j + 1], bias=cvbb[:, j:j + 1]),
                         reads=[("cz", j), "pp"], writes=[("at", j)])
                out_proj(i)

            pend = []

            def pend_e2(n=None):
                k = len(pend) if n is None else min(n, len(pend))
                for _ in range(k):
                    pend.pop(0)()

            body = {"ffn": body_ffn, "mixA": body_mixA, "mixC": body_mixC}[kind]
            stage_A(0)
            for i in range(NT):
                body(i)
                pend.extend(E2(i))
                if i == NT - 1:
                    S.flush()
                    pend_e2()
                else:
                    pass
            S.barrier()

        for pi, s in enumerate(phases):
            run_phase(pi, s)

        S.wait_keys("sp", [("act", nph - 1, i) for i in range(NT)])
        S.emit()
    return nc


def _chunkT(v):
    return np.ascontiguousarray(v.reshape(-1, P).T)


def _icnt(S_LEN):
    ic = np.zeros((2, 4, T), np.float32)
    for e in range(2):
        for g in range(4):
            half = 1 << g
            t = np.arange(T) + (0 if e == 0 else S_LEN - T)
            cnt = np.minimum(t + half, S_LEN) - np.maximum(t - half, 0)
            ic[e, g] = 1.0 / cnt
    return ic


def prep_shared(inp, S_LEN):
    f = lambda a: np.ascontiguousarray(np.asarray(a, dtype=np.float32))
    sh = {}
    sh["adaw"] = f(np.stack([inp["mix_ada_w"][0], inp["ffn_ada_w"][0], inp["mix_ada_w"][1], inp["ffn_ada_w"][1]]))
    sh["w_in0"] = f(inp["ab_w_in"][0])
    sh["w_out0"] = f(inp["ab_w_out"][0])
    sh["wsT"] = f(np.transpose(inp["ab_ws"][0], (2, 0, 1)).reshape(P, 512))
    sh["poolw"] = f(np.transpose(inp["ab_pool_w"][0], (1, 0, 2)).reshape(P, 512))
    sh["w_in1"] = f(inp["cv_w_in"][0])
    sh["w_out1"] = f(inp["cv_w_out"][0])
    sh["w_up"] = f(inp["ffn_w_up"])
    sh["w_down"] = f(inp["ffn_w_down"])
    pb = np.zeros((P, NPB), np.float32)
    pb[:, PB_SG:PB_SG + 512] = np.broadcast_to(inp["ab_sgu_ln_g"][0][None, :], (P, 512))
    pb[:, PB_SB:PB_SB + 512] = np.broadcast_to(inp["ab_sgu_ln_b"][0][None, :], (P, 512))
    bs = np.asarray(inp["ab_bs"][0], np.float32)
    pb[:, PB_BS:PB_BS + 1024] = np.broadcast_to(np.tile(bs, (1, 2)).reshape(1, 1024), (P, 1024))
    pb[:, PB_IC:PB_IC + 2048] = np.broadcast_to(_icnt(S_LEN).reshape(1, 2048), (P, 2048))
    pb[:, PB_ID:PB_ID + P] = np.eye(P, dtype=np.float32)
    sh["pb"] = pb
    pp = np.zeros((P, NPP), np.float32)
    adab = [inp["mix_ada_b"][0], inp["ffn_ada_b"][0], inp["mix_ada_b"][1], inp["ffn_ada_b"][1]]
    lng = [inp["mix_ln_g"][0], inp["ffn_ln_g"][0], inp["mix_ln_g"][1], inp["ffn_ln_g"][1]]
    lnb = [inp["mix_ln_b"][0], inp["ffn_ln_b"][0], inp["mix_ln_b"][1], inp["ffn_ln_b"][1]]
    for s in range(4):
        pp[:, OFF_ADAB + 24 * s:OFF_ADAB + 24 * s + 24] = _chunkT(np.asarray(adab[s], np.float32))
        pp[:, OFF_LN + 16 * s:OFF_LN + 16 * s + 8] = _chunkT(np.asarray(lng[s], np.float32))
        pp[:, OFF_LN + 16 * s + 8:OFF_LN + 16 * s + 16] = _chunkT(np.asarray(lnb[s], np.float32))
    pp[:, OFF_PSC:OFF_PSC + 4] = _chunkT(np.asarray(inp["ab_pool_scale"][0], np.float32))
    pp[:, OFF_SGG:OFF_SGG + 4] = _chunkT(np.asarray(inp["ab_sgu_ln_g"][0], np.float32))
    pp[:, OFF_SGB:OFF_SGB + 4] = _chunkT(np.asarray(inp["ab_sgu_ln_b"][0], np.float32))
    pp[:, OFF_CVB:OFF_CVB + 8] = _chunkT(np.asarray(inp["cv_dw_b"][0], np.float32))
    pp[:, OFF_CVLN:OFF_CVLN + 8] = _chunkT(np.asarray(inp["cv_ln_g"][0], np.float32))
    pp[:, OFF_CVLN + 8:OFF_CVLN + 16] = _chunkT(np.asarray(inp["cv_ln_b"][0], np.float32))
    dw = np.asarray(inp["cv_dw"][0], np.float32)
    pp[:, OFF_DWT:OFF_DWT + 248] = dw.reshape(31, 8, P).transpose(2, 1, 0).reshape(P, 248)
    for l in range(2):
        fd = np.asarray(inp["ffn_dw"][l], np.float32)
        pp[:, OFF_FDW + 132 * l:OFF_FDW + 132 * (l + 1)] = fd.reshape(3, 44, P).transpose(2, 1, 0).reshape(P, 132)
        pp[:, OFF_FDB + 44 * l:OFF_FDB + 44 * (l + 1)] = _chunkT(np.asarray(inp["ffn_dw_b"][l], np.float32))
    return sh, pp


_NC_CACHE = {}


def kernel(**inputs):
    x = np.asarray(inputs["x"], np.float32)
    c = np.asarray(inputs["c"], np.float32)
    B, S_LEN, _ = x.shape
    sh, pp0 = prep_shared(inputs, S_LEN)
    key = (S_LEN,)
    if key not in _NC_CACHE:
        _NC_CACHE[key] = build_nc(S_LEN)
    nc = _NC_CACHE[key]
    in_maps = []
    for b in range(B):
        pp = pp0.copy()
        pp[:, OFF_C:OFF_C + 8] = _chunkT(c[b])
        m = dict(sh)
        m["pp"] = pp
        m["xT"] = np.ascontiguousarray(x[b].T)
        in_maps.append(m)
    res = run_bass_kernel_spmd(nc, in_maps, core_ids=list(range(B)))
    out = np.empty((B, S_LEN, D), np.float32)
    for b in range(B):
        out[b] = res.results[b]["outT"].T
    return out
```

```python
from contextlib import ExitStack
import numpy as np
import concourse.bass as bass
import concourse.mybir as mybir
from concourse.bass_utils import run_bass_kernel_spmd

F32 = mybir.dt.float32
BF16 = mybir.dt.bfloat16
AF = mybir.ActivationFunctionType
ALU = mybir.AluOpType

P = 128
D = 1024
KC = 8
T = 256
FF = 2816
FJ = 22
ALPHA = 4.0 ** 0.25
EPS = 1e-5
EPS_R = EPS / (ALPHA * ALPHA)

OFF_C, OFF_ADAB, OFF_LN, OFF_PSC, OFF_CVB, OFF_CVLN, OFF_DWT, OFF_FDW, OFF_FDB, OFF_SGG, OFF_SGB, NPP = (
    0, 8, 104, 168, 172, 180, 196, 444, 708, 796, 800, 804)
PB_SG, PB_SB, PB_BS, PB_IC, PB_ID, NPB = 0, 512, 1024, 2048, 4096, 4224

ENGS = ("pe", "act", "dve", "pool", "sp")


class Sched:
    def __init__(self, nc, es):
        self.nc = nc
        self.es = es
        self.ops = {e: [] for e in ENGS}
        self.sems = {}
        self.cnt = {}
        self.known = {e: {} for e in ENGS}
        self.last_w = {}
        self.readers = {}
        self.deferred = []
        for e in ("pe", "act", "dve", "pool"):
            self._sem("E_" + e)

    def _sem(self, name):
        if name not in self.sems:
            self.sems[name] = self.es.enter_context(self.nc.semaphore(name))
            self.cnt[name] = 0
        return self.sems[name]

    def _deps(self, eng, reads, writes, is_dma):
        need = {}
        own = "E_" + eng

        def add(s, v):
            if (not is_dma) and s == own and eng == "pe":
                return
            if need.get(s, 0) < v:
                need[s] = v

        for k in reads:
            lw = self.last_w.get(k)
            if lw is not None:
                add(*lw)
        for k in writes:
            lw = self.last_w.get(k)
            if lw is not None:
                add(*lw)
            for s, v in self.readers.get(k, {}).items():
                add(s, v)
        waits = []
        kn = self.known[eng]
        for s, v in need.items():
            if kn.get(s, 0) < v:
                kn[s] = v
                waits.append((s, v))
        return waits

    def _commit(self, tok, reads, writes):
        s, v = tok
        for k in reads:
            r = self.readers.setdefault(k, {})
            if r.get(s, 0) < v:
                r[s] = v
        for k in writes:
            self.last_w[k] = tok
            self.readers[k] = {}

    def op(self, eng, fn, reads=(), writes=(), inc=True):
        waits = self._deps(eng, reads, writes, False)
        s = "E_" + eng
        tok = (s, self.cnt[s] + 1)
        if inc:
            self.cnt[s] += 1
        self.ops[eng].append((waits, fn, (s, 1) if inc else None))
        self._commit(tok, reads, writes)
        return tok

    def dma(self, eng, fn, sem, reads=(), writes=()):
        self._sem(sem)
        waits = self._deps(eng, reads, writes, True)
        self.cnt[sem] += 16
        tok = (sem, self.cnt[sem])
        self.ops[eng].append((waits, fn, (sem, 16)))
        self._commit(tok, reads, writes)
        return tok

    def barrier(self):
        for e in ENGS:
            waits = []
            kn = self.known[e]
            for s, v in self.cnt.items():
                if v > 0 and s != "E_" + e and kn.get(s, 0) < v:
                    kn[s] = v
                    waits.append((s, v))
            self.ops[e].append((waits, None, None))

    def wait_keys(self, eng, keys):
        waits = self._deps(eng, keys, (), True)
        self.ops[eng].append((waits, None, None))

    def defer(self, fn):
        self.deferred.append(fn)

    def flush(self, n=None):
        k = len(self.deferred) if n is None else min(n, len(self.deferred))
        for _ in range(k):
            self.deferred.pop(0)()

    def emit(self):
        nc = self.nc
        with nc.Block() as block:
            def run(engname):
                def body(e):
                    for waits, fn, inc in self.ops[engname]:
                        for s, v in waits:
                            e.wait_ge(self.sems[s], v)
                        if fn is None:
                            continue
                        inst = fn(e)
                        if inc is not None:
                            inst.then_inc(self.sems[inc[0]], inc[1])
                return body
            block.tensor(run("pe"))
            block.scalar(run("act"))
            block.vector(run("dve"))
            block.gpsimd(run("pool"))
            block.sync(run("sp"))


class Arena:
    def __init__(self, handle, n32):
        self.h = handle
        self.n = n32
        self.off = 0

    def alloc(self, nelem, dt):
        nb = nelem * (2 if dt == BF16 else 4)
        n32 = (nb + 15) // 16 * 4
        assert self.off + n32 <= self.n, f"arena overflow {self.off}+{n32}>{self.n}"
        ap = self.h[:, self.off:self.off + n32]
        self.off += n32
        if dt == BF16:
            ap = ap.bitcast(BF16)
        return ap[:, 0:nelem]


def build_nc(S_LEN=8192, phases=(0, 1, 2, 3)):
    NT = S_LEN // T
    nc = bass.Bass("TRN2", target_bir_lowering=False)

    def dram(name, shape, kind="ExternalInput"):
        return nc.dram_tensor(name, shape, F32, kind=kind).ap()

    xT = dram("xT", [D, S_LEN])
    outT = dram("outT", [D, S_LEN], "ExternalOutput")
    pp_d = dram("pp", [P, NPP])
    pb_d = dram("pb", [P, NPB])
    adaw_d = dram("adaw", [4, D, 3 * D])
    w_in0_d = dram("w_in0", [D, 1536])
    w_out0_d = dram("w_out0", [D, D])
    wsT_d = dram("wsT", [P, 512])
    poolw_d = dram("poolw", [P, 512])
    w_in1_d = dram("w_in1", [D, 2048])
    w_out1_d = dram("w_out1", [D, D])
    w_up_d = dram("w_up", [2, D, 2 * FF])
    w_down_d = dram("w_down", [2, FF, D])
    nph = len(phases)
    acts = [dram(f"act{i}", [D, S_LEN], "Internal") for i in range(max(nph - 1, 0))]
    srcs = [xT] + acts
    dsts = acts + [outT]

    with ExitStack() as es:
        S = Sched(nc, es)
        ARENA32 = 50176
        arena_h = es.enter_context(nc.sbuf_tensor("arena", [P, ARENA32], F32))
        A = Arena(arena_h, ARENA32)
        PS = [es.enter_context(nc.psum_tensor(f"ps{i}", [P, 512], F32)) for i in range(8)]

        pp = A.alloc(NPP, F32)
        modt = A.alloc(4 * 24, F32).rearrange("p (s f) -> p s f", s=4)
        sc1 = A.alloc(4 * 8, F32).rearrange("p (s f) -> p s f", s=4)
        g1a = A.alloc(4 * 8, F32).rearrange("p (s f) -> p s f", s=4)
        csil = A.alloc(8, F32)
        onesb = A.alloc(P, BF16)
        persist_mark = A.off

        S.dma("sp", lambda e: e.dma_start(out=pp, in_=pp_d), "ldpp", writes=["pp"])
        S.op("pool", lambda e: e.memset(onesb, 1.0 / D), writes=["onesb"])

        stg = [A.alloc(8 * 512, F32).rearrange("p (k f) -> p k f", k=8) for _ in range(2)]
        S.op("act", lambda e: e.activation(out=csil, in_=pp[:, OFF_C:OFF_C + 8], func=AF.Silu),
             reads=["pp"], writes=["csil"])
        gi = 0
        for s in range(4):
            mps = PS[s % 2]
            for g in range(6):
                st = stg[gi % 2]
                src = adaw_d[s].rearrange("(k p) f -> p k f", p=P)[:, :, g * 512:(g + 1) * 512]
                S.dma("sp", lambda e, st=st, src=src: e.dma_start(out=st, in_=src), f"ldst{gi % 2}",
                      writes=[("stg", gi % 2)])
                for fl in range(4):
                    fc = g * 4 + fl
                    for k in range(KC):
                        S.op("pe", lambda e, st=st, k=k, fl=fl, fc=fc, mps=mps: e.matmul(
                            mps[:, fc:fc + 1], lhsT=st[:, k, fl * P:(fl + 1) * P], rhs=csil[:, k:k + 1],
                            start=(k == 0), stop=(k == KC - 1)),
                            reads=[("stg", gi % 2), "csil"], writes=[("mps", s % 2)], inc=(k == KC - 1))
                gi += 1
            S.op("dve", lambda e, s=s, mps=mps: e.tensor_tensor(
                out=modt[:, s, :], in0=mps[:, 0:24], in1=pp[:, OFF_ADAB + 24 * s:OFF_ADAB + 24 * s + 24],
                op=ALU.add), reads=[("mps", s % 2), "pp"], writes=["modt"])
            S.op("dve", lambda e, s=s: e.tensor_scalar(
                out=sc1[:, s, :], in0=modt[:, s, 8:16], scalar1=1.0, scalar2=None, op0=ALU.add),
                reads=["modt"], writes=["sc1"])
            S.op("dve", lambda e, s=s: e.tensor_scalar(
                out=g1a[:, s, :], in0=modt[:, s, 16:24], scalar1=1.0, scalar2=1.0 / ALPHA,
                op0=ALU.add, op1=ALU.mult), reads=["modt"], writes=["g1a"])
        S.barrier()

        def load_weights_cast(dst2d, src2d, sem, key=None, last=False):
            S.dma("pool", lambda e: e.dma_start(out=dst2d, in_=src2d), sem,
                  writes=[key] if (last and key is not None) else [])

        def run_phase(pi, s):
            A.off = persist_mark
            src = srcs[pi].rearrange("(c p) t -> p c t", p=P)
            dst = dsts[pi].rearrange("(c p) t -> p c t", p=P)
            kind = ("mixA", "ffn", "mixC", "ffn")[s]
            H = {"mixA": 8, "ffn": 1, "mixC": 15}[kind]
            W = T + 2 * H
            lay = s // 2
            lng = pp[:, OFF_LN + s * 16:OFF_LN + s * 16 + 8]
            lnb = pp[:, OFF_LN + s * 16 + 8:OFF_LN + s * 16 + 16]

            xt = [A.alloc(KC * W, F32).rearrange("p (c w) -> p c w", c=KC) for _ in range(2)]
            ht = [A.alloc(KC * W, BF16).rearrange("p (c w) -> p c w", c=KC) for _ in range(2)]
            rb2 = [A.alloc(2 * T, BF16) for _ in range(2)]
            rbf = [r_[:, 0:T] for r_ in rb2]
            rsq = [r_[:, T:2 * T] for r_ in rb2]
            msq = A.alloc(T, F32)
            var = A.alloc(T, F32)
            rstd = A.alloc(T, F32)
            mean_ps = PS[7][:, 0:T]
            e2_ps = PS[7][:, T:2 * T]

            if kind == "ffn":
                wup = A.alloc(KC * 2 * FF, BF16).rearrange("p (k f) -> p k f", k=KC)
                wdn = A.alloc(FJ * D, BF16).rearrange("p (j d) -> p j d", j=FJ)
                at = A.alloc(FJ * T, BF16).rearrange("p (j t) -> p j t", j=FJ)
                accs = [[A.alloc(T, F32) for _ in range(3)] for _ in range(3)]
                CB = 1408
                for cb in (0, 2, 1, 3):
                    for k in range(KC):
                        load_weights_cast(wup[:, k, cb * CB:(cb + 1) * CB],
                                          w_up_d[lay, k * P:(k + 1) * P, cb * CB:(cb + 1) * CB],
                                          f"ldw{cb}", ("wup", cb), last=(k == KC - 1))
                for j in range(FJ):
                    load_weights_cast(wdn[:, j, :], w_down_d[lay, j * P:(j + 1) * P, :], "ldw4", "wdn",
                                      last=(j == FJ - 1))
                fdw = pp[:, OFF_FDW + lay * 132:OFF_FDW + (lay + 1) * 132].rearrange("p (c k) -> p c k", k=3)
                fdb = pp[:, OFF_FDB + lay * 44:OFF_FDB + (lay + 1) * 44]
            elif kind == "mixA":
                win = A.alloc(KC * 1536, BF16).rearrange("p (k f) -> p k f", k=KC)
                wout = A.alloc(KC * D, BF16).rearrange("p (k f) -> p k f", k=KC)
                wst = A.alloc(512, BF16).rearrange("p (h q) -> p h q", h=4)
                plw = A.alloc(512, BF16).rearrange("p (g d) -> p g d", g=4)
                at = A.alloc(KC * T, BF16).rearrange("p (j t) -> p j t", j=KC)
                pb = A.alloc(NPB, F32)
                u_sb = A.alloc(4 * T, F32).rearrange("p (h t) -> p h t", h=4)
                gv = [A.alloc(512, F32) for _ in range(2)]
                vn = [A.alloc(512, BF16) for _ in range(2)]
                zb = [A.alloc(W, F32) for _ in range(4)]
                sab = [[A.alloc(W, F32) for _ in range(2)] for _ in range(2)]
                tmpf = [A.alloc(T, F32) for _ in range(2)]
                pooled = [A.alloc(T, BF16) for _ in range(4)]
                st6 = [A.alloc(8, F32) for _ in range(2)]
                mv = [A.alloc(4, F32) for _ in range(2)]
                mhalf = A.alloc(1, F32)
                ones1 = A.alloc(P, BF16)
                cbt = A.alloc(4 * T, F32).rearrange("p (h t) -> p h t", h=4)
                sgg = pp[:, OFF_SGG:OFF_SGG + 4]
                sgb = pp[:, OFF_SGB:OFF_SGB + 4]
                S.op("pool", lambda e: e.memset(mhalf, -0.5), writes=["mhalf"])
                S.op("pool", lambda e: e.memset(ones1, 1.0), writes=["ones1"])
                S.dma("sp", lambda e: e.dma_start(out=pb, in_=pb_d), "ldpb", writes=["pb"])
                for k in range(KC):
                    load_weights_cast(win[:, k, :], w_in0_d[k * P:(k + 1) * P, :], "ldw0", "win", last=(k == KC - 1))
                load_weights_cast(wst.rearrange("p h q -> p (h q)"), wsT_d, "ldw1", "wst", last=True)
                load_weights_cast(plw.rearrange("p g d -> p (g d)"), poolw_d, "ldw1", "wst", last=True)
                for k in range(KC):
                    load_weights_cast(wout[:, k, :], w_out0_d[k * P:(k + 1) * P, :], "ldw2", "wout", last=(k == KC - 1))
                psc = pp[:, OFF_PSC:OFF_PSC + 4]
                for hd in range(4):
                    rs_ps = PS[hd % 2][:, 0:P]
                    S.op("pe", lambda e, rs_ps=rs_ps, hd=hd: e.matmul(rs_ps, lhsT=ones1, rhs=wst[:, hd, :], start=True, stop=True),
                         reads=["ones1", "wst"], writes=[("zp", hd % 2)], inc=True)
                    for c in range(2):
                        S.op("dve", lambda e, rs_ps=rs_ps, hd=hd, c=c: e.scalar_tensor_tensor(
                            out=cbt[:, hd, c * P:(c + 1) * P], in0=rs_ps, scalar=sgb[:, hd:hd + 1],
                            in1=pb[:, PB_BS + hd * T + c * P:PB_BS + hd * T + (c + 1) * P], op0=ALU.mult, op1=ALU.add),
                            reads=[("zp", hd % 2), "pp", "pb"], writes=["cbt"])
            else:
                win = A.alloc(KC * 2048, BF16).rearrange("p (k f) -> p k f", k=KC)
                wout = A.alloc(KC * D, BF16).rearrange("p (k f) -> p k f", k=KC)
                dg = A.alloc(248 * P, BF16).rearrange("p (n d) -> p n d", n=248)
                at = A.alloc(KC * T, BF16).rearrange("p (j t) -> p j t", j=KC)
                zt = A.alloc(KC * W, BF16).rearrange("p (j w) -> p j w", j=KC)
                cz = A.alloc(KC * T, F32).rearrange("p (j t) -> p j t", j=KC)
                sg = [A.alloc(W, F32) for _ in range(2)]
                ident = A.alloc(P, F32)
                rstd2 = A.alloc(T, F32)
                S.dma("sp", lambda e: e.dma_start(out=ident, in_=pb_d[:, PB_ID:PB_ID + P]), "ldpb", writes=["ident"])
                for k in range(KC):
                    load_weights_cast(win[:, k, :], w_in1_d[k * P:(k + 1) * P, :], "ldw0", "win", last=(k == KC - 1))
                for k in range(KC):
                    load_weights_cast(wout[:, k, :], w_out1_d[k * P:(k + 1) * P, :], "ldw2", "wout", last=(k == KC - 1))
                dwt = pp[:, OFF_DWT:OFF_DWT + 248]
                for n in range(248):
                    eng = "dve" if n % 2 == 0 else "pool"
                    S.op(eng, lambda e, n=n: e.tensor_scalar(
                        out=dg[:, n, :], in0=ident, scalar1=dwt[:, n:n + 1], scalar2=None, op0=ALU.mult),
                        reads=["ident", "pp"], writes=[("dg", n // 31)])
                cvb = pp[:, OFF_CVB:OFF_CVB + 8]
                cvg = pp[:, OFF_CVLN:OFF_CVLN + 8]
                cvbb = pp[:, OFF_CVLN + 8:OFF_CVLN + 16]

            def stage_A(i):
                sl = i % 2
                lo, hi = i * T - H, i * T + T + H
                clo, chi = max(lo, 0), min(hi, S_LEN)
                a, b = clo - lo, chi - lo
                rk = [("act", pi - 1, ii) for ii in (i - 1, i, i + 1) if 0 <= ii < NT] if pi > 0 else []
                S.dma("sp", lambda e: e.dma_start(out=xt[sl][:, :, a:b], in_=src[:, :, clo:chi]),
                      f"ldx{sl}", reads=rk, writes=[("xt", sl, c) for c in range(KC)])
                for c in range(KC):
                    if c % 2 == 0:
                        S.op("act", lambda e, c=c: e.activation(
                            out=ht[sl][:, c, a:b], in_=xt[sl][:, c, a:b], func=AF.Identity,
                            scale=sc1[:, s, c:c + 1], bias=modt[:, s, c:c + 1]),
                            reads=[("xt", sl, c), "sc1", "modt"], writes=[("ht", sl, c)])
                    else:
                        S.op("dve", lambda e, c=c: e.tensor_scalar(
                            out=ht[sl][:, c, a:b], in0=xt[sl][:, c, a:b],
                            scalar1=sc1[:, s, c:c + 1], scalar2=modt[:, s, c:c + 1], op0=ALU.mult, op1=ALU.add),
                            reads=[("xt", sl, c), "sc1", "modt"], writes=[("ht", sl, c)])
                if a > 0:
                    S.op("pool", lambda e: e.memset(ht[sl][:, :, 0:a], 0.0), writes=[("ht", sl, c) for c in range(KC)])
                if b < W:
                    S.op("pool", lambda e: e.memset(ht[sl][:, :, b:W], 0.0), writes=[("ht", sl, c) for c in range(KC)])

            def stats_mm(src_bf, src_sq, m, par):
                def f():
                    S.op("pe", lambda e: e.matmul(PS[7][:, 0:2 * T], lhsT=onesb, rhs=rb2[par], start=(m == 0), stop=(m == KC - 1)),
                         reads=[("rbf", par), ("rsq", par), "onesb"], writes=["mean_ps", "e2_ps"], inc=True)
                return f

            def ln_scalars(eps, rs):
                S.op("act", lambda e: e.activation(out=msq, in_=mean_ps, func=AF.Square),
                     reads=["mean_ps"], writes=["msq"])
                S.op("dve", lambda e: e.tensor_tensor(out=var, in0=e2_ps, in1=msq, op=ALU.subtract),
                     reads=["e2_ps", "msq"], writes=["var"])
                S.op("act", lambda e: e.activation(out=var, in_=var, func=AF.Sqrt, bias=eps_ap(eps), scale=1.0),
                     reads=["var"], writes=["var"])
                S.op("dve", lambda e: e.reciprocal(out=rs, in_=var), reads=["var"], writes=["rstd"])

            def E1(i, m, yps):
                sl = i % 2
                par = m % 2
                xi = xt[sl][:, m, H:H + T]
                S.op("dve", lambda e: e.scalar_tensor_tensor(
                    out=xi, in0=yps, scalar=g1a[:, s, m:m + 1], in1=xi, op0=ALU.mult, op1=ALU.add),
                    reads=[("yps", par), ("xt", sl, m), "g1a"], writes=[("xt", sl, m)])
                S.op("pool", lambda e: e.tensor_copy(out=rbf[par], in_=xi),
                     reads=[("xt", sl, m)], writes=[("rbf", par)])
                S.op("act", lambda e: e.activation(out=rsq[par], in_=xi, func=AF.Square),
                     reads=[("xt", sl, m)], writes=[("rsq", par)])
                S.defer(stats_mm(rbf[par], rsq[par], m, par))

            def E2(i):
                sl = i % 2

                def piece(m):
                    def f():
                        xi = xt[sl][:, m, H:H + T]
                        S.op("dve", lambda e: e.tensor_tensor(out=xi, in0=xi, in1=mean_ps, op=ALU.subtract),
                             reads=[("xt", sl, m), "mean_ps"], writes=[("xt", sl, m)])
                        S.op("pool", lambda e: e.tensor_tensor(out=xi, in0=xi, in1=rstd, op=ALU.mult),
                             reads=[("xt", sl, m), "rstd"], writes=[("xt", sl, m)])
                        S.op("act", lambda e: e.activation(
                            out=xi, in_=xi, func=AF.Identity, scale=lng[:, m:m + 1], bias=lnb[:, m:m + 1]),
                            reads=[("xt", sl, m), "pp"], writes=[("xt", sl, m)])
                    return f

                def store():
                    S.dma("sp", lambda e: e.dma_start(out=dst[:, :, i * T:(i + 1) * T], in_=xt[sl][:, :, H:H + T]),
                          f"stx{sl}", reads=[("xt", sl, c) for c in range(KC)], writes=[("act", pi, i)])
                return [lambda: ln_scalars(EPS_R, rstd)] + [piece(m) for m in range(KC)] + [store]

            eps_tiles = {}

            def eps_ap(eps):
                return eps_tiles[eps]

            for epsv in (EPS, EPS_R):
                t_ = A.alloc(1, F32)
                eps_tiles[epsv] = t_
                S.op("pool", lambda e, t_=t_, epsv=epsv: e.memset(t_, epsv), writes=["epsc"])

            def body_ffn(i):
                sl = i % 2

                def gate(j):
                    ag, av, gg = accs[j % 3]
                    S.op("act", lambda e: e.activation(out=gg, in_=ag, func=AF.Gelu_apprx_tanh),
                         reads=[("acc", j % 3, 0)], writes=[("acc", j % 3, 2)])
                    S.op("pool", lambda e: e.tensor_tensor(out=at[:, j, :], in0=gg, in1=av, op=ALU.mult),
                         reads=[("acc", j % 3, 1), ("acc", j % 3, 2)], writes=[("at", j)])

                for j in range(FJ):
                    bz = (j % 2) * 2
                    for half in range(2):
                        zp = PS[bz + half][:, 0:W]
                        cidx = half * FJ + j
                        col = cidx * P
                        cb = col // 1408
                        for k in range(KC):
                            S.op("pe", lambda e, zp=zp, k=k, col=col: e.matmul(
                                zp, lhsT=wup[:, k, col:col + P], rhs=ht[sl][:, k, :], start=(k == 0), stop=(k == KC - 1)),
                                reads=[("wup", cb), ("ht", sl, k)], writes=[("zp", bz + half)], inc=(k == KC - 1))
                        acc = accs[j % 3][half]
                        S.op("act", lambda e, zp=zp, acc=acc, cidx=cidx: e.activation(
                            out=acc, in_=zp[:, 1:1 + T], func=AF.Identity,
                            scale=fdw[:, cidx, 1:2], bias=fdb[:, cidx:cidx + 1]),
                            reads=[("zp", bz + half), "pp"], writes=[("acc", j % 3, half)])
                    if j >= 1:
                        gate(j - 1)
                    for tap in (0, 2):
                        for half in range(2):
                            zp = PS[bz + half][:, 0:W]
                            cidx = half * FJ + j
                            acc = accs[j % 3][half]
                            S.op("dve", lambda e, zp=zp, acc=acc, cidx=cidx, tap=tap: e.scalar_tensor_tensor(
                                out=acc, in0=zp[:, tap:tap + T], scalar=fdw[:, cidx, tap:tap + 1], in1=acc,
                                op0=ALU.mult, op1=ALU.add),
                                reads=[("zp", bz + half), ("acc", j % 3, half)], writes=[("acc", j % 3, half)])
                    S.flush(1)
                    if j >= 2:
                        pend_e2(1)
                    if j == 13 and i + 1 < NT:
                        stage_A(i + 1)
                gate(FJ - 1)
                for m in range(KC):
                    yps = PS[4 + m % 2][:, 0:T]
                    for j in range(FJ):
                        S.op("pe", lambda e, yps=yps, j=j, m=m: e.matmul(
                            yps, lhsT=wdn[:, j, m * P:(m + 1) * P], rhs=at[:, j, :], start=(j == 0), stop=(j == FJ - 1)),
                            reads=["wdn", ("at", j)], writes=[("yps", m % 2)], inc=(j == FJ - 1))
                    if m >= 1:
                        S.flush(1)
                    E1(i, m, yps)

            def body_mixA(i):
                sl = i % 2
                edge = 0 if i == 0 else (1 if i == NT - 1 else None)
                h_ = ht[sl]

                def v_part(c):
                    vps = PS[2 + c]
                    for k in range(KC):
                        S.op("pe", lambda e, k=k: e.matmul(
                            vps[:, :], lhsT=h_[:, k, H + c * P:H + (c + 1) * P], rhs=win[:, k, 512:1024],
                            start=(k == 0), stop=(k == KC - 1)),
                            reads=["win", ("ht", sl, k)], writes=[("zp", 2 + c)], inc=(k == KC - 1))
                    S.flush()
                    pend_e2(1)
                    S.op("act", lambda e: e.activation(out=gv[c], in_=vps[:, :], func=AF.Gelu_apprx_tanh),
                         reads=[("zp", 2 + c)], writes=[("gv", c)])
                    S.op("dve", lambda e: e.bn_stats(out=st6[c][:, 0:6], in_=gv[c]), reads=[("gv", c)], writes=[("st6", c)])
                    S.op("dve", lambda e: e.bn_aggr(out=mv[c][:, 0:2], in_=st6[c][:, 0:6]), reads=[("st6", c)], writes=[("mv", c)])
                    S.op("pool", lambda e: e.tensor_scalar(out=mv[c][:, 2:3], in0=mv[c][:, 1:2], scalar1=EPS, scalar2=None, op0=ALU.add),
                         reads=[("mv", c)], writes=[("mv2", c)])
                    S.op("pool", lambda e: e.tensor_tensor(out=mv[c][:, 3:4], in0=mv[c][:, 2:3], in1=mhalf, op=ALU.pow),
                         reads=[("mv2", c), "mhalf"], writes=[("mv3", c)])
                    S.op("dve", lambda e: e.tensor_scalar(
                        out=vn[c], in0=gv[c], scalar1=mv[c][:, 0:1], scalar2=mv[c][:, 3:4], op0=ALU.subtract, op1=ALU.mult),
                        reads=[("gv", c), ("mv", c), ("mv3", c)], writes=[("vn", c)])

                def zb_part(g):
                    zps = PS[g % 2][:, 0:W]
                    for k in range(KC):
                        S.op("pe", lambda e, k=k: e.matmul(
                            zps, lhsT=win[:, k, 1024 + g * P:1024 + (g + 1) * P], rhs=h_[:, k, :], start=(k == 0), stop=(k == KC - 1)),
                            reads=["win", ("ht", sl, k)], writes=[("zp", g % 2)], inc=(k == KC - 1))
                    pend_e2(1)
                    z = zb[g]
                    S.op("act", lambda e: e.activation(out=z, in_=zps, func=AF.Copy),
                         reads=[("zp", g % 2)], writes=[("zb", g)])
                    eng = "dve" if g % 2 == 0 else "pool"
                    spans = [(1, W, 1, 0), (2, W - 1, 1, 1), (4, W - 3, 2, 2), (8, W - 7, 4, 4)]
                    cur = z
                    bufs = sab[g % 2]
                    for lv in range(g + 1):
                        a0, a1, dl, dr = spans[lv]
                        o = bufs[lv % 2]
                        if lv == 0:
                            i0, i1 = cur[:, 0:W - 1], cur[:, 1:W]
                        else:
                            i0, i1 = cur[:, a0 - dl:a1 - dl], cur[:, a0 + dr:a1 + dr]
                        S.op(eng, lambda e, o=o, a0=a0, a1=a1, i0=i0, i1=i1: e.tensor_tensor(
                            out=o[:, a0:a1], in0=i0, in1=i1, op=ALU.add),
                            reads=[("zb", g), ("sab", g % 2, 0), ("sab", g % 2, 1)], writes=[("sab", g % 2, lv % 2)])
                        cur = o
                    pl = pooled[g]
                    wdw = float(2 << g)
                    if edge is None:
                        S.op("dve", lambda e: e.scalar_tensor_tensor(
                            out=pl, in0=cur[:, H:H + T], scalar=1.0 / wdw, in1=z[:, H:H + T], op0=ALU.mult, op1=ALU.subtract),
                            reads=[("sab", g % 2, 0), ("sab", g % 2, 1), ("zb", g)], writes=[("pooled", g)])
                    else:
                        ic = pb[:, PB_IC + (edge * 4 + g) * T:PB_IC + (edge * 4 + g + 1) * T]
                        tf = tmpf[g % 2]
                        S.op(eng, lambda e: e.tensor_tensor(out=tf, in0=cur[:, H:H + T], in1=ic, op=ALU.mult),
                             reads=[("sab", g % 2, 0), ("sab", g % 2, 1), "pb"], writes=[("tmpf", g % 2)])
                        S.op(eng, lambda e: e.tensor_tensor(out=pl, in0=tf, in1=z[:, H:H + T], op=ALU.subtract),
                             reads=[("tmpf", g % 2), ("zb", g)], writes=[("pooled", g)])

                def u_part(hd):
                    ups = PS[hd % 2][:, 0:T]
                    for k in range(KC):
                        S.op("pe", lambda e, k=k: e.matmul(
                            ups, lhsT=win[:, k, hd * P:(hd + 1) * P], rhs=h_[:, k, H:H + T], start=(k == 0), stop=(k == KC - 1)),
                            reads=["win", ("ht", sl, k)], writes=[("zp", hd % 2)], inc=(k == KC - 1))
                    pend_e2(1)
                    S.op("act", lambda e: e.activation(out=u_sb[:, hd, :], in_=ups, func=AF.Gelu_apprx_tanh),
                         reads=[("zp", hd % 2)], writes=[("u", hd)])

                def mix_part():
                    for c in range(2):
                        for hd in range(4):
                            mx = PS[4 + hd // 2][:, (hd % 2) * T + c * P:(hd % 2) * T + (c + 1) * P]
                            S.op("pe", lambda e, mx=mx, c=c, hd=hd: e.matmul(
                                mx, lhsT=vn[c][:, hd * P:(hd + 1) * P], rhs=wst[:, hd, :], start=True, stop=True),
                                reads=[("vn", c), "wst"], writes=[("yps", hd // 2)], inc=True)
                    for hd in range(4):
                        mxf = PS[4 + hd // 2][:, (hd % 2) * T:(hd % 2 + 1) * T]
                        tf = tmpf[hd % 2]
                        S.op("dve", lambda e, mxf=mxf, tf=tf, hd=hd: e.scalar_tensor_tensor(
                            out=tf, in0=mxf, scalar=sgg[:, hd:hd + 1], in1=cbt[:, hd, :], op0=ALU.mult, op1=ALU.add),
                            reads=[("yps", hd // 2), "cbt", "pp"], writes=[("tmpf", hd % 2)])
                        S.op("pool", lambda e, tf=tf, hd=hd: e.tensor_tensor(out=at[:, hd, :], in0=tf, in1=u_sb[:, hd, :], op=ALU.mult),
                             reads=[("tmpf", hd % 2), ("u", hd)], writes=[("at", hd)])

                def pw_part(g):
                    pw = PS[2 + g % 2][:, 0:T]
                    S.op("pe", lambda e: e.matmul(pw, lhsT=plw[:, g, :], rhs=pooled[g], start=True, stop=True),
                         reads=["wst", ("pooled", g)], writes=[("zp", 2 + g % 2)], inc=True)
                    S.op("act", lambda e: e.activation(out=at[:, 4 + g, :], in_=pw, func=AF.Copy, scale=psc[:, g:g + 1]),
                         reads=[("zp", 2 + g % 2), "pp"], writes=[("at", 4 + g)])

                v_part(0)
                v_part(1)
                for g in range(4):
                    zb_part(g)
                for hd in range(4):
                    u_part(hd)
                pend_e2()
                if i + 1 < NT:
                    stage_A(i + 1)
                mix_part()
                for g in range(4):
                    pw_part(g)
                out_proj(i)

            def out_proj(i):
                for m in range(KC):
                    yps = PS[4 + m % 2][:, 0:T]
                    for k in range(KC):
                        S.op("pe", lambda e, yps=yps, k=k, m=m: e.matmul(
                            yps, lhsT=wout[:, k, m * P:(m + 1) * P], rhs=at[:, k, :], start=(k == 0), stop=(k == KC - 1)),
                            reads=["wout", ("at", k)], writes=[("yps", m % 2)], inc=(k == KC - 1))
                    if m >= 1:
                        S.flush(1)
                    E1(i, m, yps)

            def body_mixC(i):
                sl = i % 2
                h_ = ht[sl]
                for j in range(KC):
                    aps = PS[(j % 2) * 2][:, 0:W]
                    gps = PS[(j % 2) * 2 + 1][:, 0:W]
                    for half, zp in ((0, aps), (1, gps)):
                        col = half * D + j * P
                        for k in range(KC):
                            S.op("pe", lambda e, zp=zp, k=k, col=col: e.matmul(
                                zp, lhsT=win[:, k, col:col + P], rhs=h_[:, k, :], start=(k == 0), stop=(k == KC - 1)),
                                reads=["win", ("ht", sl, k)], writes=[("zp", (j % 2) * 2 + half)], inc=(k == KC - 1))
                    sgt = sg[j % 2]
                    S.op("act", lambda e, gps=gps, sgt=sgt: e.activation(out=sgt, in_=gps, func=AF.Sigmoid),
                         reads=[("zp", (j % 2) * 2 + 1)], writes=[("sg", j % 2)])
                    S.op("dve", lambda e, aps=aps, sgt=sgt, j=j: e.tensor_tensor(out=zt[:, j, :], in0=aps, in1=sgt, op=ALU.mult),
                         reads=[("zp", (j % 2) * 2), ("sg", j % 2)], writes=[("zt", j)])
                    if j == 0:
                        S.flush()
                    pend_e2(2)
                for j in range(KC):
                    cps = PS[4 + j % 2][:, 0:T]
                    for k in range(31):
                        S.op("pe", lambda e, cps=cps, j=j, k=k: e.matmul(
                            cps, lhsT=dg[:, j * 31 + k, :], rhs=zt[:, j, k:k + T], start=(k == 0), stop=(k == 30)),
                            reads=[("dg", j), ("zt", j)], writes=[("yps", j % 2)], inc=(k == 30))
                    par = j % 2
                    S.op("act", lambda e, cps=cps, j=j: e.activation(out=cz[:, j, :], in_=cps, func=AF.Identity, bias=cvb[:, j:j + 1], scale=1.0),
                         reads=[("yps", par), "pp"], writes=[("cz", j)])
                    S.op("pool", lambda e, j=j, par=par: e.tensor_copy(out=rbf[par], in_=cz[:, j, :]),
                         reads=[("cz", j)], writes=[("rbf", par)])
                    S.op("act", lambda e, j=j, par=par: e.activation(out=rsq[par], in_=cz[:, j, :], func=AF.Square),
                         reads=[("cz", j)], writes=[("rsq", par)])
                    if j >= 1:
                        S.flush(1)
                    S.defer(stats_mm(rbf[par], rsq[par], j, par))
                    if j == 3 and i + 1 < NT:
                        stage_A(i + 1)
                S.flush()
                ln_scalars(EPS, rstd2)
                for j in range(KC):
                    S.op("dve", lambda e, j=j: e.tensor_tensor(out=cz[:, j, :], in0=cz[:, j, :], in1=mean_ps, op=ALU.subtract),
                         reads=[("cz", j), "mean_ps"], writes=[("cz", j)])
                    S.op("pool", lambda e, j=j: e.tensor_tensor(out=cz[:, j, :], in0=cz[:, j, :], in1=rstd2, op=ALU.mult),
                         reads=[("cz", j), "rstd"], writes=[("cz", j)])
                    S.op("act", lambda e, j=j: e.activation(out=at[:, j, :], in_=cz[:, j, :], func=AF.Silu,
                                                            scale=cvg[:, j:j + 1], bias=cvbb[:, j:j + 1]),
                         reads=[("cz", j), "pp"], writes=[("at", j)])
                out_proj(i)

            pend = []

            def pend_e2(n=None):
                k = len(pend) if n is None else min(n, len(pend))
                for _ in range(k):
                    pend.pop(0)()

            body = {"ffn": body_ffn, "mixA": body_mixA, "mixC": body_mixC}[kind]
            stage_A(0)
            for i in range(NT):
                body(i)
                pend.extend(E2(i))
                if i == NT - 1:
                    S.flush()
                    pend_e2()
                else:
                    pass
            S.barrier()

        for pi, s in enumerate(phases):
            run_phase(pi, s)

        S.wait_keys("sp", [("act", nph - 1, i) for i in range(NT)])
        S.emit()
    return nc


def _chunkT(v):
    return np.ascontiguousarray(v.reshape(-1, P).T)


def _icnt(S_LEN):
    ic = np.zeros((2, 4, T), np.float32)
    for e in range(2):
        for g in range(4):
            half = 1 << g
            t = np.arange(T) + (0 if e == 0 else S_LEN - T)
            cnt = np.minimum(t + half, S_LEN) - np.maximum(t - half, 0)
            ic[e, g] = 1.0 / cnt
    return ic


def prep_shared(inp, S_LEN):
    f = lambda a: np.ascontiguousarray(np.asarray(a, dtype=np.float32))
    sh = {}
    sh["adaw"] = f(np.stack([inp["mix_ada_w"][0], inp["ffn_ada_w"][0], inp["mix_ada_w"][1], inp["ffn_ada_w"][1]]))
    sh["w_in0"] = f(inp["ab_w_in"][0])
    sh["w_out0"] = f(inp["ab_w_out"][0])
    sh["wsT"] = f(np.transpose(inp["ab_ws"][0], (2, 0, 1)).reshape(P, 512))
    sh["poolw"] = f(np.transpose(inp["ab_pool_w"][0], (1, 0, 2)).reshape(P, 512))
    sh["w_in1"] = f(inp["cv_w_in"][0])
    sh["w_out1"] = f(inp["cv_w_out"][0])
    sh["w_up"] = f(inp["ffn_w_up"])
    sh["w_down"] = f(inp["ffn_w_down"])
    pb = np.zeros((P, NPB), np.float32)
    pb[:, PB_SG:PB_SG + 512] = np.broadcast_to(inp["ab_sgu_ln_g"][0][None, :], (P, 512))
    pb[:, PB_SB:PB_SB + 512] = np.broadcast_to(inp["ab_sgu_ln_b"][0][None, :], (P, 512))
    bs = np.asarray(inp["ab_bs"][0], np.float32)
    pb[:, PB_BS:PB_BS + 1024] = np.broadcast_to(np.tile(bs, (1, 2)).reshape(1, 1024), (P, 1024))
    pb[:, PB_IC:PB_IC + 2048] = np.broadcast_to(_icnt(S_LEN).reshape(1, 2048), (P, 2048))
    pb[:, PB_ID:PB_ID + P] = np.eye(P, dtype=np.float32)
    sh["pb"] = pb
    pp = np.zeros((P, NPP), np.float32)
    adab = [inp["mix_ada_b"][0], inp["ffn_ada_b"][0], inp["mix_ada_b"][1], inp["ffn_ada_b"][1]]
    lng = [inp["mix_ln_g"][0], inp["ffn_ln_g"][0], inp["mix_ln_g"][1], inp["ffn_ln_g"][1]]
    lnb = [inp["mix_ln_b"][0], inp["ffn_ln_b"][0], inp["mix_ln_b"][1], inp["ffn_ln_b"][1]]
    for s in range(4):
        pp[:, OFF_ADAB + 24 * s:OFF_ADAB + 24 * s + 24] = _chunkT(np.asarray(adab[s], np.float32))
        pp[:, OFF_LN + 16 * s:OFF_LN + 16 * s + 8] = _chunkT(np.asarray(lng[s], np.float32))
        pp[:, OFF_LN + 16 * s + 8:OFF_LN + 16 * s + 16] = _chunkT(np.asarray(lnb[s], np.float32))
    pp[:, OFF_PSC:OFF_PSC + 4] = _chunkT(np.asarray(inp["ab_pool_scale"][0], np.float32))
    pp[:, OFF_SGG:OFF_SGG + 4] = _chunkT(np.asarray(inp["ab_sgu_ln_g"][0], np.float32))
    pp[:, OFF_SGB:OFF_SGB + 4] = _chunkT(np.asarray(inp["ab_sgu_ln_b"][0], np.float32))
    pp[:, OFF_CVB:OFF_CVB + 8] = _chunkT(np.asarray(inp["cv_dw_b"][0], np.float32))
    pp[:, OFF_CVLN:OFF_CVLN + 8] = _chunkT(np.asarray(inp["cv_ln_g"][0], np.float32))
    pp[:, OFF_CVLN + 8:OFF_CVLN + 16] = _chunkT(np.asarray(inp["cv_ln_b"][0], np.float32))
    dw = np.asarray(inp["cv_dw"][0], np.float32)
    pp[:, OFF_DWT:OFF_DWT + 248] = dw.reshape(31, 8, P).transpose(2, 1, 0).reshape(P, 248)
    for l in range(2):
        fd = np.asarray(inp["ffn_dw"][l], np.float32)
        pp[:, OFF_FDW + 132 * l:OFF_FDW + 132 * (l + 1)] = fd.reshape(3, 44, P).transpose(2, 1, 0).reshape(P, 132)
        pp[:, OFF_FDB + 44 * l:OFF_FDB + 44 * (l + 1)] = _chunkT(np.asarray(inp["ffn_dw_b"][l], np.float32))
    return sh, pp


_NC_CACHE = {}


def kernel(**inputs):
    x = np.asarray(inputs["x"], np.float32)
    c = np.asarray(inputs["c"], np.float32)
    B, S_LEN, _ = x.shape
    sh, pp0 = prep_shared(inputs, S_LEN)
    key = (S_LEN,)
    if key not in _NC_CACHE:
        _NC_CACHE[key] = build_nc(S_LEN)
    nc = _NC_CACHE[key]
    in_maps = []
    for b in range(B):
        pp = pp0.copy()
        pp[:, OFF_C:OFF_C + 8] = _chunkT(c[b])
        m = dict(sh)
        m["pp"] = pp
        m["xT"] = np.ascontiguousarray(x[b].T)
        in_maps.append(m)
    res = run_bass_kernel_spmd(nc, in_maps, core_ids=list(range(B)))
    out = np.empty((B, S_LEN, D), np.float32)
    for b in range(B):
        out[b] = res.results[b]["outT"].T
    return out
```

```python
from contextlib import ExitStack
import numpy as np
import concourse.bass as bass
import concourse.mybir as mybir
from concourse.bass_utils import run_bass_kernel_spmd

F32 = mybir.dt.float32
BF16 = mybir.dt.bfloat16
AF = mybir.ActivationFunctionType
ALU = mybir.AluOpType

P = 128
D = 1024
KC = 8
T = 256
FF = 2816
FJ = 22
ALPHA = 4.0 ** 0.25
EPS = 1e-5
EPS_R = EPS / (ALPHA * ALPHA)

OFF_C, OFF_ADAB, OFF_LN, OFF_PSC, OFF_CVB, OFF_CVLN, OFF_DWT, OFF_FDW, OFF_FDB, OFF_SGG, OFF_SGB, NPP = (
    0, 8, 104, 168, 172, 180, 196, 444, 708, 796, 800, 804)
PB_SG, PB_SB, PB_BS, PB_IC, PB_ID, NPB = 0, 512, 1024, 2048, 4096, 4224

ENGS = ("pe", "act", "dve", "pool", "sp")


class Sched:
    def __init__(self, nc, es):
        self.nc = nc
        self.es = es
        self.ops = {e: [] for e in ENGS}
        self.sems = {}
        self.cnt = {}
        self.known = {e: {} for e in ENGS}
        self.last_w = {}
        self.readers = {}
        self.deferred = []
        for e in ("pe", "act", "dve", "pool"):
            self._sem("E_" + e)

    def _sem(self, name):
        if name not in self.sems:
            self.sems[name] = self.es.enter_context(self.nc.semaphore(name))
            self.cnt[name] = 0
        return self.sems[name]

    def _deps(self, eng, reads, writes, is_dma):
        need = {}
        own = "E_" + eng

        def add(s, v):
            if (not is_dma) and s == own and eng == "pe":
                return
            if need.get(s, 0) < v:
                need[s] = v

        for k in reads:
            lw = self.last_w.get(k)
            if lw is not None:
                add(*lw)
        for k in writes:
            lw = self.last_w.get(k)
            if lw is not None:
                add(*lw)
            for s, v in self.readers.get(k, {}).items():
                add(s, v)
        waits = []
        kn = self.known[eng]
        for s, v in need.items():
            if kn.get(s, 0) < v:
                kn[s] = v
                waits.append((s, v))
        return waits

    def _commit(self, tok, reads, writes):
        s, v = tok
        for k in reads:
            r = self.readers.setdefault(k, {})
            if r.get(s, 0) < v:
                r[s] = v
        for k in writes:
            self.last_w[k] = tok
            self.readers[k] = {}

    def op(self, eng, fn, reads=(), writes=(), inc=True):
        waits = self._deps(eng, reads, writes, False)
        s = "E_" + eng
        tok = (s, self.cnt[s] + 1)
        if inc:
            self.cnt[s] += 1
        self.ops[eng].append((waits, fn, (s, 1) if inc else None))
        self._commit(tok, reads, writes)
        return tok

    def dma(self, eng, fn, sem, reads=(), writes=()):
        self._sem(sem)
        waits = self._deps(eng, reads, writes, True)
        self.cnt[sem] += 16
        tok = (sem, self.cnt[sem])
        self.ops[eng].append((waits, fn, (sem, 16)))
        self._commit(tok, reads, writes)
        return tok

    def barrier(self):
        for e in ENGS:
            waits = []
            kn = self.known[e]
            for s, v in self.cnt.items():
                if v > 0 and s != "E_" + e and kn.get(s, 0) < v:
                    kn[s] = v
                    waits.append((s, v))
            self.ops[e].append((waits, None, None))

    def wait_keys(self, eng, keys):
        waits = self._deps(eng, keys, (), True)
        self.ops[eng].append((waits, None, None))

    def defer(self, fn):
        self.deferred.append(fn)

    def flush(self, n=None):
        k = len(self.deferred) if n is None else min(n, len(self.deferred))
        for _ in range(k):
            self.deferred.pop(0)()

    def emit(self):
        nc = self.nc
        with nc.Block() as block:
            def run(engname):
                def body(e):
                    for waits, fn, inc in self.ops[engname]:
                        for s, v in waits:
                            e.wait_ge(self.sems[s], v)
                        if fn is None:
                            continue
                        inst = fn(e)
                        if inc is not None:
                            inst.then_inc(self.sems[inc[0]], inc[1])
                return body
            block.tensor(run("pe"))
            block.scalar(run("act"))
            block.vector(run("dve"))
            block.gpsimd(run("pool"))
            block.sync(run("sp"))


class Arena:
    def __init__(self, handle, n32):
        self.h = handle
        self.n = n32
        self.off = 0

    def alloc(self, nelem, dt):
        nb = nelem * (2 if dt == BF16 else 4)
        n32 = (nb + 15) // 16 * 4
        assert self.off + n32 <= self.n, f"arena overflow {self.off}+{n32}>{self.n}"
        ap = self.h[:, self.off:self.off + n32]
        self.off += n32
        if dt == BF16:
            ap = ap.bitcast(BF16)
        return ap[:, 0:nelem]


def build_nc(S_LEN=8192, phases=(0, 1, 2, 3)):
    NT = S_LEN // T
    nc = bass.Bass("TRN2", target_bir_lowering=False)

    def dram(name, shape, kind="ExternalInput"):
        return nc.dram_tensor(name, shape, F32, kind=kind).ap()

    xT = dram("xT", [D, S_LEN])
    outT = dram("outT", [D, S_LEN], "ExternalOutput")
    pp_d = dram("pp", [P, NPP])
    pb_d = dram("pb", [P, NPB])
    adaw_d = dram("adaw", [4, D, 3 * D])
    w_in0_d = dram("w_in0", [D, 1536])
    w_out0_d = dram("w_out0", [D, D])
    wsT_d = dram("wsT", [P, 512])
    poolw_d = dram("poolw", [P, 512])
    w_in1_d = dram("w_in1", [D, 2048])
    w_out1_d = dram("w_out1", [D, D])
    w_up_d = dram("w_up", [2, D, 2 * FF])
    w_down_d = dram("w_down", [2, FF, D])
    nph = len(phases)
    acts = [dram(f"act{i}", [D, S_LEN], "Internal") for i in range(max(nph - 1, 0))]
    srcs = [xT] + acts
    dsts = acts + [outT]

    with ExitStack() as es:
        S = Sched(nc, es)
        ARENA32 = 50176
        arena_h = es.enter_context(nc.sbuf_tensor("arena", [P, ARENA32], F32))
        A = Arena(arena_h, ARENA32)
        PS = [es.enter_context(nc.psum_tensor(f"ps{i}", [P, 512], F32)) for i in range(8)]

        pp = A.alloc(NPP, F32)
        modt = A.alloc(4 * 24, F32).rearrange("p (s f) -> p s f", s=4)
        sc1 = A.alloc(4 * 8, F32).rearrange("p (s f) -> p s f", s=4)
        g1a = A.alloc(4 * 8, F32).rearrange("p (s f) -> p s f", s=4)
        csil = A.alloc(8, F32)
        onesb = A.alloc(P, BF16)

        S.dma("sp", lambda e: e.dma_start(out=pp, in_=pp_d), "ldpp", writes=["pp"])
        S.op("pool", lambda e: e.memset(onesb, 1.0 / D), writes=["onesb"])

        csil_bf = A.alloc(8, BF16)
        persist_mark = A.off
        S.op("act", lambda e: e.activation(out=csil, in_=pp[:, OFF_C:OFF_C + 8], func=AF.Silu),
             reads=["pp"], writes=["csil"])
        S.op("act", lambda e: e.activation(out=csil_bf, in_=csil, func=AF.Copy), reads=["csil"], writes=["csil_bf"])
        modctx = {}
        MPS = PS[6]

        def mod_dma(sl_, g):
            def f():
                st = modctx["stg"][g % 2]
                srcw = adaw_d[sl_].rearrange("(k p) f -> p k f", p=P)[:, :, g * 512:(g + 1) * 512]
                S.dma("pool", lambda e: e.dma_start(out=st, in_=srcw), f"ldst{g % 2}", writes=[("stg", g % 2)])
            return f

        def mod_mm(sl_, g):
            def f():
                st = modctx["stg"][g % 2]
                for fl in range(4):
                    fc = g * 4 + fl
                    for k in range(KC):
                        S.op("pe", lambda e, k=k, fl=fl, fc=fc: e.matmul(
                            MPS[:, fc:fc + 1], lhsT=st[:, k, fl * P:(fl + 1) * P], rhs=csil_bf[:, k:k + 1],
                            start=(k == 0), stop=(k == KC - 1)),
                            reads=[("stg", g % 2), "csil_bf"], writes=["mps"], inc=(k == KC - 1))
                if g == 5:
                    S.op("dve", lambda e: e.tensor_tensor(
                        out=modt[:, sl_, :], in0=MPS[:, 0:24], in1=pp[:, OFF_ADAB + 24 * sl_:OFF_ADAB + 24 * sl_ + 24],
                        op=ALU.add), reads=["mps", "pp"], writes=[("modt", sl_)])
                    S.op("dve", lambda e: e.tensor_scalar(
                        out=sc1[:, sl_, :], in0=modt[:, sl_, 8:16], scalar1=1.0, scalar2=None, op0=ALU.add),
                        reads=[("modt", sl_)], writes=[("sc1", sl_)])
                    S.op("dve", lambda e: e.tensor_scalar(
                        out=g1a[:, sl_, :], in0=modt[:, sl_, 16:24], scalar1=1.0, scalar2=1.0 / ALPHA,
                        op0=ALU.add, op1=ALU.mult), reads=[("modt", sl_)], writes=[("g1a", sl_)])
            return f

        def mod_steps_for(sub_ids):
            groups = [(sl_, g) for sl_ in sub_ids for g in range(6)]
            steps = []
            for n in range(len(groups) + 1):
                def step(n=n):
                    if n < len(groups):
                        mod_dma(*groups[n])()
                    if n >= 1:
                        mod_mm(*groups[n - 1])()
                steps.append(step)
            return steps

        modctx["stg"] = [A.alloc(8 * 512, BF16).rearrange("p (k f) -> p k f", k=8) for _ in range(2)]
        for st_ in mod_steps_for([phases[0]]):
            st_()
        side_steps = mod_steps_for([x for x in range(4) if x != phases[0]])
        if phases[0] != 0:
            for st_ in side_steps:
                st_()
            side_steps = []
        S.barrier()

        def load_weights_cast(dst2d, src2d, sem, key=None, last=False):
            S.dma("pool", lambda e: e.dma_start(out=dst2d, in_=src2d), sem,
                  writes=[key] if (last and key is not None) else [])

        def run_phase(pi, s, side=None):
            A.off = persist_mark
            src = srcs[pi].rearrange("(c p) t -> p c t", p=P)
            dst = dsts[pi].rearrange("(c p) t -> p c t", p=P)
            kind = ("mixA", "ffn", "mixC", "ffn")[s]
            cast_eng = "pool" if kind == "ffn" else "dve"
            H = {"mixA": 8, "ffn": 1, "mixC": 15}[kind]
            W = T + 2 * H
            lay = s // 2
            lng = pp[:, OFF_LN + s * 16:OFF_LN + s * 16 + 8]
            lnb = pp[:, OFF_LN + s * 16 + 8:OFF_LN + s * 16 + 16]

            xt = [A.alloc(KC * W, F32).rearrange("p (c w) -> p c w", c=KC) for _ in range(2)]
            ht = [A.alloc(KC * W, BF16).rearrange("p (c w) -> p c w", c=KC) for _ in range(2)]
            rb2 = [A.alloc(2 * T, BF16) for _ in range(2)]
            rbf = [r_[:, 0:T] for r_ in rb2]
            rsq = [r_[:, T:2 * T] for r_ in rb2]
            msq = A.alloc(T, F32)
            var = A.alloc(T, F32)
            rstd = A.alloc(T, F32)
            mean_ps = PS[7][:, 0:T]
            e2_ps = PS[7][:, T:2 * T]

            if kind == "ffn":
                wup = A.alloc(KC * 2 * FF, BF16).rearrange("p (k f) -> p k f", k=KC)
                wdn = A.alloc(FJ * D, BF16).rearrange("p (j d) -> p j d", j=FJ)
                at = A.alloc(FJ * T, BF16).rearrange("p (j t) -> p j t", j=FJ)
                accs = [[A.alloc(T, F32) for _ in range(3)] for _ in range(3)]
                CB = 1408
                for cb in (0, 2, 1, 3):
                    for k in range(KC):
                        load_weights_cast(wup[:, k, cb * CB:(cb + 1) * CB],
                                          w_up_d[lay, k * P:(k + 1) * P, cb * CB:(cb + 1) * CB],
                                          f"ldw{cb}", ("wup", cb), last=(k == KC - 1))
                for j in range(FJ):
                    load_weights_cast(wdn[:, j, :], w_down_d[lay, j * P:(j + 1) * P, :], "ldw4", "wdn",
                                      last=(j == FJ - 1))
                fdw = pp[:, OFF_FDW + lay * 132:OFF_FDW + (lay + 1) * 132].rearrange("p (c k) -> p c k", k=3)
                fdb = pp[:, OFF_FDB + lay * 44:OFF_FDB + (lay + 1) * 44]
            elif kind == "mixA":
                win = A.alloc(KC * 1536, BF16).rearrange("p (k f) -> p k f", k=KC)
                wout = A.alloc(KC * D, BF16).rearrange("p (k f) -> p k f", k=KC)
                wst = A.alloc(512, BF16).rearrange("p (h q) -> p h q", h=4)
                plw = A.alloc(512, BF16).rearrange("p (g d) -> p g d", g=4)
                at = A.alloc(KC * T, BF16).rearrange("p (j t) -> p j t", j=KC)
                pb = A.alloc(NPB, F32)
                u_sb = A.alloc(4 * T, F32).rearrange("p (h t) -> p h t", h=4)
                gv = [A.alloc(512, F32) for _ in range(2)]
                vn = [A.alloc(512, BF16) for _ in range(2)]
                zb = [A.alloc(W, F32) for _ in range(4)]
                sab = [[A.alloc(W, F32) for _ in range(2)] for _ in range(2)]
                tmpf = [A.alloc(T, F32) for _ in range(2)]
                pooled = [A.alloc(T, BF16) for _ in range(4)]
                st6 = [A.alloc(8, F32) for _ in range(2)]
                mv = [A.alloc(4, F32) for _ in range(2)]
                mhalf = A.alloc(1, F32)
                ones1 = A.alloc(P, BF16)
                cbt = A.alloc(4 * T, F32).rearrange("p (h t) -> p h t", h=4)
                sgg = pp[:, OFF_SGG:OFF_SGG + 4]
                sgb = pp[:, OFF_SGB:OFF_SGB + 4]
                S.op("pool", lambda e: e.memset(mhalf, -0.5), writes=["mhalf"])
                S.op("pool", lambda e: e.memset(ones1, 1.0), writes=["ones1"])
                S.dma("sp", lambda e: e.dma_start(out=pb, in_=pb_d), "ldpb", writes=["pb"])
                for k in range(KC):
                    load_weights_cast(win[:, k, :], w_in0_d[k * P:(k + 1) * P, :], "ldw0", "win", last=(k == KC - 1))
                load_weights_cast(wst.rearrange("p h q -> p (h q)"), wsT_d, "ldw1", "wst", last=True)
                load_weights_cast(plw.rearrange("p g d -> p (g d)"), poolw_d, "ldw1", "wst", last=True)
                for k in range(KC):
                    load_weights_cast(wout[:, k, :], w_out0_d[k * P:(k + 1) * P, :], "ldw2", "wout", last=(k == KC - 1))
                psc = pp[:, OFF_PSC:OFF_PSC + 4]
                for hd in range(4):
                    rs_ps = PS[hd % 2][:, 0:P]
                    S.op("pe", lambda e, rs_ps=rs_ps, hd=hd: e.matmul(rs_ps, lhsT=ones1, rhs=wst[:, hd, :], start=True, stop=True),
                         reads=["ones1", "wst"], writes=[("zp", hd % 2)], inc=True)
                    for c in range(2):
                        S.op("dve", lambda e, rs_ps=rs_ps, hd=hd, c=c: e.scalar_tensor_tensor(
                            out=cbt[:, hd, c * P:(c + 1) * P], in0=rs_ps, scalar=sgb[:, hd:hd + 1],
                            in1=pb[:, PB_BS + hd * T + c * P:PB_BS + hd * T + (c + 1) * P], op0=ALU.mult, op1=ALU.add),
                            reads=[("zp", hd % 2), "pp", "pb"], writes=["cbt"])
            else:
                win = A.alloc(KC * 2048, BF16).rearrange("p (k f) -> p k f", k=KC)
                wout = A.alloc(KC * D, BF16).rearrange("p (k f) -> p k f", k=KC)
                dg = A.alloc(248 * P, BF16).rearrange("p (n d) -> p n d", n=248)
                at = A.alloc(KC * T, BF16).rearrange("p (j t) -> p j t", j=KC)
                zt = A.alloc(KC * W, BF16).rearrange("p (j w) -> p j w", j=KC)
                cz = A.alloc(KC * T, F32).rearrange("p (j t) -> p j t", j=KC)
                sg = [A.alloc(W, F32) for _ in range(2)]
                ident = A.alloc(P, F32)
                rstd2 = A.alloc(T, F32)
                S.dma("sp", lambda e: e.dma_start(out=ident, in_=pb_d[:, PB_ID:PB_ID + P]), "ldpb", writes=["ident"])
                for k in range(KC):
                    load_weights_cast(win[:, k, :], w_in1_d[k * P:(k + 1) * P, :], "ldw0", "win", last=(k == KC - 1))
                for k in range(KC):
                    load_weights_cast(wout[:, k, :], w_out1_d[k * P:(k + 1) * P, :], "ldw2", "wout", last=(k == KC - 1))
                dwt = pp[:, OFF_DWT:OFF_DWT + 248]
                for n in range(248):
                    eng = "dve" if n % 2 == 0 else "pool"
                    S.op(eng, lambda e, n=n: e.tensor_scalar(
                        out=dg[:, n, :], in0=ident, scalar1=dwt[:, n:n + 1], scalar2=None, op0=ALU.mult),
                        reads=["ident", "pp"], writes=[("dg", n // 31)])
                cvb = pp[:, OFF_CVB:OFF_CVB + 8]
                cvg = pp[:, OFF_CVLN:OFF_CVLN + 8]
                cvbb = pp[:, OFF_CVLN + 8:OFF_CVLN + 16]

            def tile_span(i):
                lo, hi = i * T - H, i * T + T + H
                clo, chi = max(lo, 0), min(hi, S_LEN)
                return clo, chi, clo - lo, chi - lo

            def stage_A_load(i):
                sl = i % 2
                clo, chi, a, b = tile_span(i)
                rk = [("act", pi - 1, ii) for ii in (i - 1, i, i + 1) if 0 <= ii < NT] if pi > 0 else []
                S.dma("sp", lambda e: e.dma_start(out=xt[sl][:, :, a:b], in_=src[:, :, clo:chi]),
                      f"ldx{sl}", reads=rk, writes=[("xt", sl, c) for c in range(KC)])

            def stage_A_comp(i, c):
                sl = i % 2
                clo, chi, a, b = tile_span(i)
                if c % 2 == 0:
                    S.op("act", lambda e: e.activation(
                        out=ht[sl][:, c, a:b], in_=xt[sl][:, c, a:b], func=AF.Identity,
                        scale=sc1[:, s, c:c + 1], bias=modt[:, s, c:c + 1]),
                        reads=[("xt", sl, c), ("sc1", s), ("modt", s)], writes=[("ht", sl, c)])
                else:
                    S.op("dve", lambda e: e.tensor_scalar(
                        out=ht[sl][:, c, a:b], in0=xt[sl][:, c, a:b],
                        scalar1=sc1[:, s, c:c + 1], scalar2=modt[:, s, c:c + 1], op0=ALU.mult, op1=ALU.add),
                        reads=[("xt", sl, c), ("sc1", s), ("modt", s)], writes=[("ht", sl, c)])
                if a > 0:
                    S.op("pool", lambda e: e.memset(ht[sl][:, c, 0:a], 0.0), writes=[("ht", sl, c)])
                if b < W:
                    S.op("pool", lambda e: e.memset(ht[sl][:, c, b:W], 0.0), writes=[("ht", sl, c)])

            def stage_A(i):
                stage_A_load(i)
                for c in range(KC):
                    stage_A_comp(i, c)

            def stats_mm(src_bf, src_sq, m, par):
                def f():
                    S.op("pe", lambda e: e.matmul(PS[7][:, 0:2 * T], lhsT=onesb, rhs=rb2[par], start=(m == 0), stop=(m == KC - 1)),
                         reads=[("rbf", par), ("rsq", par), "onesb"], writes=["mean_ps", "e2_ps"], inc=True)
                return f

            def ln_pieces(eps, rs):
                return [
                    lambda: S.op("act", lambda e: e.activation(out=msq, in_=mean_ps, func=AF.Square),
                                 reads=["mean_ps"], writes=["msq"]),
                    lambda: S.op("dve", lambda e: e.tensor_tensor(out=var, in0=e2_ps, in1=msq, op=ALU.subtract),
                                 reads=["e2_ps", "msq"], writes=["var"]),
                    lambda: S.op("act", lambda e: e.activation(out=var, in_=var, func=AF.Sqrt, bias=eps_ap(eps), scale=1.0),
                                 reads=["var", "epsc"], writes=["var"]),
                    lambda: S.op("dve", lambda e: e.reciprocal(out=rs, in_=var), reads=["var"], writes=["rstd"]),
                ]

            def ln_scalars(eps, rs):
                for f in ln_pieces(eps, rs):
                    f()

            def E1(i, m, yps):
                sl = i % 2
                par = m % 2
                xi = xt[sl][:, m, H:H + T]
                S.op("dve", lambda e: e.scalar_tensor_tensor(
                    out=xi, in0=yps, scalar=g1a[:, s, m:m + 1], in1=xi, op0=ALU.mult, op1=ALU.add),
                    reads=[("yps", par), ("xt", sl, m), ("g1a", s)], writes=[("xt", sl, m)])
                S.op(cast_eng, lambda e: e.tensor_copy(out=rbf[par], in_=xi),
                     reads=[("xt", sl, m)], writes=[("rbf", par)])
                S.op("act", lambda e: e.activation(out=rsq[par], in_=xi, func=AF.Square),
                     reads=[("xt", sl, m)], writes=[("rsq", par)])
                S.defer(stats_mm(rbf[par], rsq[par], m, par))

            def E2(i):
                sl = i % 2

                def sub(m):
                    xi = xt[sl][:, m, H:H + T]
                    S.op("dve", lambda e: e.tensor_tensor(out=xi, in0=xi, in1=mean_ps, op=ALU.subtract),
                         reads=[("xt", sl, m), "mean_ps"], writes=[("xt", sl, m)])

                def mul(m):
                    xi = xt[sl][:, m, H:H + T]
                    S.op("pool", lambda e: e.tensor_tensor(out=xi, in0=xi, in1=rstd, op=ALU.mult),
                         reads=[("xt", sl, m), "rstd"], writes=[("xt", sl, m)])

                def idn(m):
                    xi = xt[sl][:, m, H:H + T]
                    S.op("act", lambda e: e.activation(
                        out=xi, in_=xi, func=AF.Identity, scale=lng[:, m:m + 1], bias=lnb[:, m:m + 1]),
                        reads=[("xt", sl, m), "pp"], writes=[("xt", sl, m)])

                def store():
                    S.dma("sp", lambda e: e.dma_start(out=dst[:, :, i * T:(i + 1) * T], in_=xt[sl][:, :, H:H + T]),
                          f"stx{sl}", reads=[("xt", sl, c) for c in range(KC)], writes=[("act", pi, i)])

                pieces = list(ln_pieces(EPS_R, rstd))

                def step(st):
                    def f():
                        if st >= 2:
                            idn(st - 2)
                        if 1 <= st <= KC:
                            mul(st - 1)
                        if st < KC:
                            sub(st)
                    return f
                pieces += [step(st) for st in range(KC + 2)]
                pieces.append(store)
                return pieces

            eps_tiles = {}

            def eps_ap(eps):
                return eps_tiles[eps]

            for epsv in (EPS, EPS_R):
                t_ = A.alloc(1, F32)
                eps_tiles[epsv] = t_
                S.op("pool", lambda e, t_=t_, epsv=epsv: e.memset(t_, epsv), writes=["epsc"])

            def body_ffn(i):
                sl = i % 2

                def gate(j):
                    ag, av, gg = accs[j % 3]
                    S.op("act", lambda e: e.activation(out=gg, in_=ag, func=AF.Gelu_apprx_tanh),
                         reads=[("acc", j % 3, 0)], writes=[("acc", j % 3, 2)])
                    S.op("pool", lambda e: e.tensor_tensor(out=at[:, j, :], in0=gg, in1=av, op=ALU.mult),
                         reads=[("acc", j % 3, 1), ("acc", j % 3, 2)], writes=[("at", j)])

                for j in range(FJ):
                    bz = (j % 2) * 2
                    for half in range(2):
                        zp = PS[bz + half][:, 0:W]
                        cidx = half * FJ + j
                        col = cidx * P
                        cb = col // 1408
                        for k in range(KC):
                            S.op("pe", lambda e, zp=zp, k=k, col=col: e.matmul(
                                zp, lhsT=wup[:, k, col:col + P], rhs=ht[sl][:, k, :], start=(k == 0), stop=(k == KC - 1)),
                                reads=[("wup", cb), ("ht", sl, k)], writes=[("zp", bz + half)], inc=(k == KC - 1))
                        acc = accs[j % 3][half]
                        S.op("act", lambda e, zp=zp, acc=acc, cidx=cidx: e.activation(
                            out=acc, in_=zp[:, 1:1 + T], func=AF.Identity,
                            scale=fdw[:, cidx, 1:2], bias=fdb[:, cidx:cidx + 1]),
                            reads=[("zp", bz + half), "pp"], writes=[("acc", j % 3, half)])
                    if j >= 1:
                        gate(j - 1)
                    for tap in (0, 2):
                        for half in range(2):
                            zp = PS[bz + half][:, 0:W]
                            cidx = half * FJ + j
                            acc = accs[j % 3][half]
                            S.op("dve", lambda e, zp=zp, acc=acc, cidx=cidx, tap=tap: e.scalar_tensor_tensor(
                                out=acc, in0=zp[:, tap:tap + T], scalar=fdw[:, cidx, tap:tap + 1], in1=acc,
                                op0=ALU.mult, op1=ALU.add),
                                reads=[("zp", bz + half), ("acc", j % 3, half)], writes=[("acc", j % 3, half)])
                    S.flush(1)
                    if j >= 2:
                        pend_e2(1)
                    if j == 18 and i + 1 < NT:
                        stage_A_load(i + 1)
                gate(FJ - 1)
                for m in range(KC):
                    yps = PS[4 + m % 2][:, 0:T]
                    for j in range(FJ):
                        S.op("pe", lambda e, yps=yps, j=j, m=m: e.matmul(
                            yps, lhsT=wdn[:, j, m * P:(m + 1) * P], rhs=at[:, j, :], start=(j == 0), stop=(j == FJ - 1)),
                            reads=["wdn", ("at", j)], writes=[("yps", m % 2)], inc=(j == FJ - 1))
                    if m >= 1:
                        S.flush(1)
                    E1(i, m, yps)
                    if i + 1 < NT:
                        stage_A_comp(i + 1, m)

            def body_mixA(i):
                sl = i % 2
                edge = 0 if i == 0 else (1 if i == NT - 1 else None)
                h_ = ht[sl]

                def v_part(c):
                    vps = PS[2 + c]
                    for k in range(KC):
                        S.op("pe", lambda e, k=k: e.matmul(
                            vps[:, :], lhsT=h_[:, k, H + c * P:H + (c + 1) * P], rhs=win[:, k, 512:1024],
                            start=(k == 0), stop=(k == KC - 1)),
                            reads=["win", ("ht", sl, k)], writes=[("zp", 2 + c)], inc=(k == KC - 1))
                    S.flush()
                    pend_e2(2)
                    S.op("act", lambda e: e.activation(out=gv[c], in_=vps[:, :], func=AF.Gelu_apprx_tanh),
                         reads=[("zp", 2 + c)], writes=[("gv", c)])
                    S.op("dve", lambda e: e.bn_stats(out=st6[c][:, 0:6], in_=gv[c]), reads=[("gv", c)], writes=[("st6", c)])
                    S.op("dve", lambda e: e.bn_aggr(out=mv[c][:, 0:2], in_=st6[c][:, 0:6]), reads=[("st6", c)], writes=[("mv", c)])
                    S.op("pool", lambda e: e.tensor_scalar(out=mv[c][:, 2:3], in0=mv[c][:, 1:2], scalar1=EPS, scalar2=None, op0=ALU.add),
                         reads=[("mv", c)], writes=[("mv2", c)])
                    S.op("pool", lambda e: e.tensor_tensor(out=mv[c][:, 3:4], in0=mv[c][:, 2:3], in1=mhalf, op=ALU.pow),
                         reads=[("mv2", c), "mhalf"], writes=[("mv3", c)])
                    S.op("dve", lambda e: e.tensor_scalar(
                        out=vn[c], in0=gv[c], scalar1=mv[c][:, 0:1], scalar2=mv[c][:, 3:4], op0=ALU.subtract, op1=ALU.mult),
                        reads=[("gv", c), ("mv", c), ("mv3", c)], writes=[("vn", c)])

                def zb_part(g):
                    zps = PS[g % 2][:, 0:W]
                    for k in range(KC):
                        S.op("pe", lambda e, k=k: e.matmul(
                            zps, lhsT=win[:, k, 1024 + g * P:1024 + (g + 1) * P], rhs=h_[:, k, :], start=(k == 0), stop=(k == KC - 1)),
                            reads=["win", ("ht", sl, k)], writes=[("zp", g % 2)], inc=(k == KC - 1))
                    pend_e2(2)
                    z = zb[g]
                    S.op("act", lambda e: e.activation(out=z, in_=zps, func=AF.Copy),
                         reads=[("zp", g % 2)], writes=[("zb", g)])
                    eng = "dve" if g % 2 == 0 else "pool"
                    spans = [(1, W, 1, 0), (2, W - 1, 1, 1), (4, W - 3, 2, 2), (8, W - 7, 4, 4)]
                    cur = z
                    bufs = sab[g % 2]
                    for lv in range(g + 1):
                        a0, a1, dl, dr = spans[lv]
                        o = bufs[lv % 2]
                        if lv == 0:
                            i0, i1 = cur[:, 0:W - 1], cur[:, 1:W]
                        else:
                            i0, i1 = cur[:, a0 - dl:a1 - dl], cur[:, a0 + dr:a1 + dr]
                        S.op(eng, lambda e, o=o, a0=a0, a1=a1, i0=i0, i1=i1: e.tensor_tensor(
                            out=o[:, a0:a1], in0=i0, in1=i1, op=ALU.add),
                            reads=[("zb", g), ("sab", g % 2, 0), ("sab", g % 2, 1)], writes=[("sab", g % 2, lv % 2)])
                        cur = o
                    pl = pooled[g]
                    wdw = float(2 << g)
                    if edge is None:
                        S.op("dve", lambda e: e.scalar_tensor_tensor(
                            out=pl, in0=cur[:, H:H + T], scalar=1.0 / wdw, in1=z[:, H:H + T], op0=ALU.mult, op1=ALU.subtract),
                            reads=[("sab", g % 2, 0), ("sab", g % 2, 1), ("zb", g)], writes=[("pooled", g)])
                    else:
                        ic = pb[:, PB_IC + (edge * 4 + g) * T:PB_IC + (edge * 4 + g + 1) * T]
                        tf = tmpf[g % 2]
                        S.op(eng, lambda e: e.tensor_tensor(out=tf, in0=cur[:, H:H + T], in1=ic, op=ALU.mult),
                             reads=[("sab", g % 2, 0), ("sab", g % 2, 1), "pb"], writes=[("tmpf", g % 2)])
                        S.op(eng, lambda e: e.tensor_tensor(out=pl, in0=tf, in1=z[:, H:H + T], op=ALU.subtract),
                             reads=[("tmpf", g % 2), ("zb", g)], writes=[("pooled", g)])

                def u_part(hd):
                    ups = PS[hd % 2][:, 0:T]
                    for k in range(KC):
                        S.op("pe", lambda e, k=k: e.matmul(
                            ups, lhsT=win[:, k, hd * P:(hd + 1) * P], rhs=h_[:, k, H:H + T], start=(k == 0), stop=(k == KC - 1)),
                            reads=["win", ("ht", sl, k)], writes=[("zp", hd % 2)], inc=(k == KC - 1))
                    pend_e2(1)
                    S.op("act", lambda e: e.activation(out=u_sb[:, hd, :], in_=ups, func=AF.Gelu_apprx_tanh),
                         reads=[("zp", hd % 2)], writes=[("u", hd)])

                def mix_part():
                    for c in range(2):
                        for hd in range(4):
                            mx = PS[4 + hd // 2][:, (hd % 2) * T + c * P:(hd % 2) * T + (c + 1) * P]
                            S.op("pe", lambda e, mx=mx, c=c, hd=hd: e.matmul(
                                mx, lhsT=vn[c][:, hd * P:(hd + 1) * P], rhs=wst[:, hd, :], start=True, stop=True),
                                reads=[("vn", c), "wst"], writes=[("yps", hd // 2)], inc=True)
                    for hd in range(4):
                        mxf = PS[4 + hd // 2][:, (hd % 2) * T:(hd % 2 + 1) * T]
                        tf = tmpf[hd % 2]
                        S.op("dve", lambda e, mxf=mxf, tf=tf, hd=hd: e.scalar_tensor_tensor(
                            out=tf, in0=mxf, scalar=sgg[:, hd:hd + 1], in1=cbt[:, hd, :], op0=ALU.mult, op1=ALU.add),
                            reads=[("yps", hd // 2), "cbt", "pp"], writes=[("tmpf", hd % 2)])
                        S.op("pool", lambda e, tf=tf, hd=hd: e.tensor_tensor(out=at[:, hd, :], in0=tf, in1=u_sb[:, hd, :], op=ALU.mult),
                             reads=[("tmpf", hd % 2), ("u", hd)], writes=[("at", hd)])

                def pw_part(g):
                    pw = PS[2 + g % 2][:, 0:T]
                    S.op("pe", lambda e: e.matmul(pw, lhsT=plw[:, g, :], rhs=pooled[g], start=True, stop=True),
                         reads=["wst", ("pooled", g)], writes=[("zp", 2 + g % 2)], inc=True)
                    S.op("act", lambda e: e.activation(out=at[:, 4 + g, :], in_=pw, func=AF.Copy, scale=psc[:, g:g + 1]),
                         reads=[("zp", 2 + g % 2), "pp"], writes=[("at", 4 + g)])

                v_part(0)
                v_part(1)
                for g in range(4):
                    zb_part(g)
                for hd in range(4):
                    u_part(hd)
                pend_e2()
                if i + 1 < NT:
                    stage_A_load(i + 1)
                mix_part()
                for g in range(4):
                    pw_part(g)
                out_proj(i)

            def out_proj(i):
                for m in range(KC):
                    yps = PS[4 + m % 2][:, 0:T]
                    for k in range(KC):
                        S.op("pe", lambda e, yps=yps, k=k, m=m: e.matmul(
                            yps, lhsT=wout[:, k, m * P:(m + 1) * P], rhs=at[:, k, :], start=(k == 0), stop=(k == KC - 1)),
                            reads=["wout", ("at", k)], writes=[("yps", m % 2)], inc=(k == KC - 1))
                    if m >= 1:
                        S.flush(1)
                    E1(i, m, yps)
                    if i + 1 < NT:
                        stage_A_comp(i + 1, m)

            def body_mixC(i):
                sl = i % 2
                h_ = ht[sl]
                for j in range(KC):
                    aps = PS[(j % 2) * 2][:, 0:W]
                    gps = PS[(j % 2) * 2 + 1][:, 0:W]
                    for half, zp in ((0, aps), (1, gps)):
                        col = half * D + j * P
                        for k in range(KC):
                            S.op("pe", lambda e, zp=zp, k=k, col=col: e.matmul(
                                zp, lhsT=win[:, k, col:col + P], rhs=h_[:, k, :], start=(k == 0), stop=(k == KC - 1)),
                                reads=["win", ("ht", sl, k)], writes=[("zp", (j % 2) * 2 + half)], inc=(k == KC - 1))
                    sgt = sg[j % 2]
                    S.op("act", lambda e, gps=gps, sgt=sgt: e.activation(out=sgt, in_=gps, func=AF.Sigmoid),
                         reads=[("zp", (j % 2) * 2 + 1)], writes=[("sg", j % 2)])
                    S.op("dve", lambda e, aps=aps, sgt=sgt, j=j: e.tensor_tensor(out=zt[:, j, :], in0=aps, in1=sgt, op=ALU.mult),
                         reads=[("zp", (j % 2) * 2), ("sg", j % 2)], writes=[("zt", j)])
                    if j == 0:
                        S.flush()
                    pend_e2(2)
                for j in range(KC):
                    cps = PS[4 + j % 2][:, 0:T]
                    for k in range(31):
                        S.op("pe", lambda e, cps=cps, j=j, k=k: e.matmul(
                            cps, lhsT=dg[:, j * 31 + k, :], rhs=zt[:, j, k:k + T], start=(k == 0), stop=(k == 30)),
                            reads=[("dg", j), ("zt", j)], writes=[("yps", j % 2)], inc=(k == 30))
                    par = j % 2
                    S.op("act", lambda e, cps=cps, j=j: e.activation(out=cz[:, j, :], in_=cps, func=AF.Identity, bias=cvb[:, j:j + 1], scale=1.0),
                         reads=[("yps", par), "pp"], writes=[("cz", j)])
                    S.op("pool", lambda e, j=j, par=par: e.tensor_copy(out=rbf[par], in_=cz[:, j, :]),
                         reads=[("cz", j)], writes=[("rbf", par)])
                    S.op("act", lambda e, j=j, par=par: e.activation(out=rsq[par], in_=cz[:, j, :], func=AF.Square),
                         reads=[("cz", j)], writes=[("rsq", par)])
                    if j >= 1:
                        S.flush(1)
                    S.defer(stats_mm(rbf[par], rsq[par], j, par))
                    if j == 0 and i + 1 < NT:
                        pend_e2()
                        stage_A_load(i + 1)
                S.flush()
                ln_scalars(EPS, rstd2)
                for j in range(KC):
                    S.op("dve", lambda e, j=j: e.tensor_tensor(out=cz[:, j, :], in0=cz[:, j, :], in1=mean_ps, op=ALU.subtract),
                         reads=[("cz", j), "mean_ps"], writes=[("cz", j)])
                    S.op("pool", lambda e, j=j: e.tensor_tensor(out=cz[:, j, :], in0=cz[:, j, :], in1=rstd2, op=ALU.mult),
                         reads=[("cz", j), "rstd"], writes=[("cz", j)])
                    S.op("act", lambda e, j=j: e.activation(out=at[:, j, :], in_=cz[:, j, :], func=AF.Silu,
                                                            scale=cvg[:, j:j + 1], bias=cvbb[:, j:j + 1]),
                         reads=[("cz", j), "pp"], writes=[("at", j)])
                out_proj(i)

            pend = []

            def pend_e2(n=None):
                k = len(pend) if n is None else min(n, len(pend))
                for _ in range(k):
                    pend.pop(0)()

            body = {"ffn": body_ffn, "mixA": body_mixA, "mixC": body_mixC}[kind]
            if side:
                modctx["stg"] = [A.alloc(8 * 512, BF16).rearrange("p (k f) -> p k f", k=8) for _ in range(2)]
            stage_A(0)
            for i in range(NT):
                if side and i >= 1:
                    side.pop(0)()
                body(i)
                pend.extend(E2(i))
                if i == NT - 1:
                    S.flush()
                    pend_e2()
                    while side:
                        side.pop(0)()
                else:
                    pass
            S.barrier()

        for pi, s in enumerate(phases):
            run_phase(pi, s, side_steps if pi == 0 else None)

        S.wait_keys("sp", [("act", nph - 1, i) for i in range(NT)])
        S.emit()
    return nc


def _chunkT(v):
    return np.ascontiguousarray(v.reshape(-1, P).T)


def _icnt(S_LEN):
    ic = np.zeros((2, 4, T), np.float32)
    for e in range(2):
        for g in range(4):
            half = 1 << g
            t = np.arange(T) + (0 if e == 0 else S_LEN - T)
            cnt = np.minimum(t + half, S_LEN) - np.maximum(t - half, 0)
            ic[e, g] = 1.0 / cnt
    return ic


def prep_shared(inp, S_LEN):
    f = lambda a: np.ascontiguousarray(np.asarray(a, dtype=np.float32))
    sh = {}
    sh["adaw"] = f(np.stack([inp["mix_ada_w"][0], inp["ffn_ada_w"][0], inp["mix_ada_w"][1], inp["ffn_ada_w"][1]]))
    sh["w_in0"] = f(inp["ab_w_in"][0])
    sh["w_out0"] = f(inp["ab_w_out"][0])
    sh["wsT"] = f(np.transpose(inp["ab_ws"][0], (2, 0, 1)).reshape(P, 512))
    sh["poolw"] = f(np.transpose(inp["ab_pool_w"][0], (1, 0, 2)).reshape(P, 512))
    sh["w_in1"] = f(inp["cv_w_in"][0])
    sh["w_out1"] = f(inp["cv_w_out"][0])
    sh["w_up"] = f(inp["ffn_w_up"])
    sh["w_down"] = f(inp["ffn_w_down"])
    pb = np.zeros((P, NPB), np.float32)
    pb[:, PB_SG:PB_SG + 512] = np.broadcast_to(inp["ab_sgu_ln_g"][0][None, :], (P, 512))
    pb[:, PB_SB:PB_SB + 512] = np.broadcast_to(inp["ab_sgu_ln_b"][0][None, :], (P, 512))
    bs = np.asarray(inp["ab_bs"][0], np.float32)
    pb[:, PB_BS:PB_BS + 1024] = np.broadcast_to(np.tile(bs, (1, 2)).reshape(1, 1024), (P, 1024))
    pb[:, PB_IC:PB_IC + 2048] = np.broadcast_to(_icnt(S_LEN).reshape(1, 2048), (P, 2048))
    pb[:, PB_ID:PB_ID + P] = np.eye(P, dtype=np.float32)
    sh["pb"] = pb
    pp = np.zeros((P, NPP), np.float32)
    adab = [inp["mix_ada_b"][0], inp["ffn_ada_b"][0], inp["mix_ada_b"][1], inp["ffn_ada_b"][1]]
    lng = [inp["mix_ln_g"][0], inp["ffn_ln_g"][0], inp["mix_ln_g"][1], inp["ffn_ln_g"][1]]
    lnb = [inp["mix_ln_b"][0], inp["ffn_ln_b"][0], inp["mix_ln_b"][1], inp["ffn_ln_b"][1]]
    for s in range(4):
        pp[:, OFF_ADAB + 24 * s:OFF_ADAB + 24 * s + 24] = _chunkT(np.asarray(adab[s], np.float32))
        pp[:, OFF_LN + 16 * s:OFF_LN + 16 * s + 8] = _chunkT(np.asarray(lng[s], np.float32))
        pp[:, OFF_LN + 16 * s + 8:OFF_LN + 16 * s + 16] = _chunkT(np.asarray(lnb[s], np.float32))
    pp[:, OFF_PSC:OFF_PSC + 4] = _chunkT(np.asarray(inp["ab_pool_scale"][0], np.float32))
    pp[:, OFF_SGG:OFF_SGG + 4] = _chunkT(np.asarray(inp["ab_sgu_ln_g"][0], np.float32))
    pp[:, OFF_SGB:OFF_SGB + 4] = _chunkT(np.asarray(inp["ab_sgu_ln_b"][0], np.float32))
    pp[:, OFF_CVB:OFF_CVB + 8] = _chunkT(np.asarray(inp["cv_dw_b"][0], np.float32))
    pp[:, OFF_CVLN:OFF_CVLN + 8] = _chunkT(np.asarray(inp["cv_ln_g"][0], np.float32))
    pp[:, OFF_CVLN + 8:OFF_CVLN + 16] = _chunkT(np.asarray(inp["cv_ln_b"][0], np.float32))
    dw = np.asarray(inp["cv_dw"][0], np.float32)
    pp[:, OFF_DWT:OFF_DWT + 248] = dw.reshape(31, 8, P).transpose(2, 1, 0).reshape(P, 248)
    for l in range(2):
        fd = np.asarray(inp["ffn_dw"][l], np.float32)
        pp[:, OFF_FDW + 132 * l:OFF_FDW + 132 * (l + 1)] = fd.reshape(3, 44, P).transpose(2, 1, 0).reshape(P, 132)
        pp[:, OFF_FDB + 44 * l:OFF_FDB + 44 * (l + 1)] = _chunkT(np.asarray(inp["ffn_dw_b"][l], np.float32))
    return sh, pp


_NC_CACHE = {}


def kernel(**inputs):
    x = np.asarray(inputs["x"], np.float32)
    c = np.asarray(inputs["c"], np.float32)
    B, S_LEN, _ = x.shape
    sh, pp0 = prep_shared(inputs, S_LEN)
    key = (S_LEN,)
    if key not in _NC_CACHE:
        _NC_CACHE[key] = build_nc(S_LEN)
    nc = _NC_CACHE[key]
    in_maps = []
    for b in range(B):
        pp = pp0.copy()
        pp[:, OFF_C:OFF_C + 8] = _chunkT(c[b])
        m = dict(sh)
        m["pp"] = pp
        m["xT"] = np.ascontiguousarray(x[b].T)
        in_maps.append(m)
    res = run_bass_kernel_spmd(nc, in_maps, core_ids=list(range(B)))
    out = np.empty((B, S_LEN, D), np.float32)
    for b in range(B):
        out[b] = res.results[b]["outT"].T
    return out
```

```python
from contextlib import ExitStack
import numpy as np
import concourse.bass as bass
import concourse.mybir as mybir
from concourse.bass_utils import run_bass_kernel_spmd

F32 = mybir.dt.float32
BF16 = mybir.dt.bfloat16
AF = mybir.ActivationFunctionType
ALU = mybir.AluOpType

P = 128
D = 1024
KC = 8
T = 256
FF = 2816
FJ = 22
ALPHA = 4.0 ** 0.25
EPS = 1e-5
EPS_R = EPS / (ALPHA * ALPHA)

OFF_C, OFF_ADAB, OFF_LN, OFF_PSC, OFF_CVB, OFF_CVLN, OFF_DWT, OFF_FDW, OFF_FDB, OFF_SGG, OFF_SGB, NPP = (
    0, 8, 104, 168, 172, 180, 196, 444, 708, 796, 800, 804)
PB_SG, PB_SB, PB_BS, PB_IC, PB_ID, NPB = 0, 512, 1024, 2048, 4096, 4224

ENGS = ("pe", "act", "dve", "pool", "sp")


class Sched:
    def __init__(self, nc, es):
        self.nc = nc
        self.es = es
        self.ops = {e: [] for e in ENGS}
        self.sems = {}
        self.cnt = {}
        self.known = {e: {} for e in ENGS}
        self.last_w = {}
        self.readers = {}
        self.deferred = []
        for e in ("pe", "act", "dve", "pool"):
            self._sem("E_" + e)

    def _sem(self, name):
        if name not in self.sems:
            self.sems[name] = self.es.enter_context(self.nc.semaphore(name))
            self.cnt[name] = 0
        return self.sems[name]

    def _deps(self, eng, reads, writes, is_dma):
        need = {}
        own = "E_" + eng

        def add(s, v):
            if (not is_dma) and s == own and eng == "pe":
                return
            if need.get(s, 0) < v:
                need[s] = v

        for k in reads:
            lw = self.last_w.get(k)
            if lw is not None:
                add(*lw)
        for k in writes:
            lw = self.last_w.get(k)
            if lw is not None:
                add(*lw)
            for s, v in self.readers.get(k, {}).items():
                add(s, v)
        waits = []
        kn = self.known[eng]
        for s, v in need.items():
            if kn.get(s, 0) < v:
                kn[s] = v
                waits.append((s, v))
        return waits

    def _commit(self, tok, reads, writes):
        s, v = tok
        for k in reads:
            r = self.readers.setdefault(k, {})
            if r.get(s, 0) < v:
                r[s] = v
        for k in writes:
            self.last_w[k] = tok
            self.readers[k] = {}

    def op(self, eng, fn, reads=(), writes=(), inc=True):
        waits = self._deps(eng, reads, writes, False)
        s = "E_" + eng
        tok = (s, self.cnt[s] + 1)
        if inc:
            self.cnt[s] += 1
        self.ops[eng].append((waits, fn, (s, 1) if inc else None))
        self._commit(tok, reads, writes)
        return tok

    def dma(self, eng, fn, sem, reads=(), writes=()):
        self._sem(sem)
        waits = self._deps(eng, reads, writes, True)
        self.cnt[sem] += 16
        tok = (sem, self.cnt[sem])
        self.ops[eng].append((waits, fn, (sem, 16)))
        self._commit(tok, reads, writes)
        return tok

    def barrier(self):
        for e in ENGS:
            waits = []
            kn = self.known[e]
            for s, v in self.cnt.items():
                if v > 0 and s != "E_" + e and kn.get(s, 0) < v:
                    kn[s] = v
                    waits.append((s, v))
            self.ops[e].append((waits, None, None))

    def wait_keys(self, eng, keys):
        waits = self._deps(eng, keys, (), True)
        self.ops[eng].append((waits, None, None))

    def defer(self, fn):
        self.deferred.append(fn)

    def flush(self, n=None):
        k = len(self.deferred) if n is None else min(n, len(self.deferred))
        for _ in range(k):
            self.deferred.pop(0)()

    def emit(self):
        nc = self.nc
        with nc.Block() as block:
            def run(engname):
                def body(e):
                    for waits, fn, inc in self.ops[engname]:
                        for s, v in waits:
                            e.wait_ge(self.sems[s], v)
                        if fn is None:
                            continue
                        inst = fn(e)
                        if inc is not None:
                            inst.then_inc(self.sems[inc[0]], inc[1])
                return body
            block.tensor(run("pe"))
            block.scalar(run("act"))
            block.vector(run("dve"))
            block.gpsimd(run("pool"))
            block.sync(run("sp"))


class Arena:
    def __init__(self, handle, n32):
        self.h = handle
        self.n = n32
        self.off = 0

    def alloc(self, nelem, dt):
        nb = nelem * (2 if dt == BF16 else 4)
        n32 = (nb + 15) // 16 * 4
        assert self.off + n32 <= self.n, f"arena overflow {self.off}+{n32}>{self.n}"
        ap = self.h[:, self.off:self.off + n32]
        self.off += n32
        if dt == BF16:
            ap = ap.bitcast(BF16)
        return ap[:, 0:nelem]


def build_nc(S_LEN=8192, phases=(0, 1, 2, 3)):
    NT = S_LEN // T
    nc = bass.Bass("TRN2", target_bir_lowering=False)

    def dram(name, shape, kind="ExternalInput"):
        return nc.dram_tensor(name, shape, F32, kind=kind).ap()

    xT = dram("xT", [D, S_LEN])
    outT = dram("outT", [D, S_LEN], "ExternalOutput")
    pp_d = dram("pp", [P, NPP])
    pb_d = dram("pb", [P, NPB])
    adaw_d = dram("adaw", [4, D, 3 * D])
    w_in0_d = dram("w_in0", [D, 1536])
    w_out0_d = dram("w_out0", [D, D])
    wsT_d = dram("wsT", [P, 512])
    poolw_d = dram("poolw", [P, 512])
    w_in1_d = dram("w_in1", [D, 2048])
    w_out1_d = dram("w_out1", [D, D])
    w_up_d = dram("w_up", [2, D, 2 * FF])
    w_down_d = dram("w_down", [2, FF, D])
    nph = len(phases)
    acts = [dram(f"act{i}", [D, S_LEN], "Internal") for i in range(max(nph - 1, 0))]
    srcs = [xT] + acts
    dsts = acts + [outT]

    with ExitStack() as es:
        S = Sched(nc, es)
        ARENA32 = 50176
        arena_h = es.enter_context(nc.sbuf_tensor("arena", [P, ARENA32], F32))
        A = Arena(arena_h, ARENA32)
        PS = [es.enter_context(nc.psum_tensor(f"ps{i}", [P, 512], F32)) for i in range(8)]

        pp = A.alloc(NPP, F32)
        modt = A.alloc(4 * 24, F32).rearrange("p (s f) -> p s f", s=4)
        sc1 = A.alloc(4 * 8, F32).rearrange("p (s f) -> p s f", s=4)
        g1a = A.alloc(4 * 8, F32).rearrange("p (s f) -> p s f", s=4)
        csil = A.alloc(8, F32)
        onesb = A.alloc(P, BF16)

        S.dma("sp", lambda e: e.dma_start(out=pp, in_=pp_d), "ldpp", writes=["pp"])
        S.op("pool", lambda e: e.memset(onesb, 1.0 / D), writes=["onesb"])

        csil_bf = A.alloc(8, BF16)
        persist_mark = A.off
        S.op("act", lambda e: e.activation(out=csil, in_=pp[:, OFF_C:OFF_C + 8], func=AF.Silu),
             reads=["pp"], writes=["csil"])
        S.op("act", lambda e: e.activation(out=csil_bf, in_=csil, func=AF.Copy), reads=["csil"], writes=["csil_bf"])
        modctx = {}
        MPS = PS[6]

        def mod_dma(sl_, g):
            def f():
                st = modctx["stg"][g % 2]
                srcw = adaw_d[sl_].rearrange("(k p) f -> p k f", p=P)[:, :, g * 512:(g + 1) * 512]
                S.dma("pool", lambda e: e.dma_start(out=st, in_=srcw), f"ldst{g % 2}", writes=[("stg", g % 2)])
            return f

        def mod_mm(sl_, g):
            def f():
                st = modctx["stg"][g % 2]
                for fl in range(4):
                    fc = g * 4 + fl
                    for k in range(KC):
                        S.op("pe", lambda e, k=k, fl=fl, fc=fc: e.matmul(
                            MPS[:, fc:fc + 1], lhsT=st[:, k, fl * P:(fl + 1) * P], rhs=csil_bf[:, k:k + 1],
                            start=(k == 0), stop=(k == KC - 1)),
                            reads=[("stg", g % 2), "csil_bf"], writes=["mps"], inc=(k == KC - 1))
                if g == 5:
                    S.op("dve", lambda e: e.tensor_tensor(
                        out=modt[:, sl_, :], in0=MPS[:, 0:24], in1=pp[:, OFF_ADAB + 24 * sl_:OFF_ADAB + 24 * sl_ + 24],
                        op=ALU.add), reads=["mps", "pp"], writes=[("modt", sl_)])
                    S.op("dve", lambda e: e.tensor_scalar(
                        out=sc1[:, sl_, :], in0=modt[:, sl_, 8:16], scalar1=1.0, scalar2=None, op0=ALU.add),
                        reads=[("modt", sl_)], writes=[("sc1", sl_)])
                    S.op("dve", lambda e: e.tensor_scalar(
                        out=g1a[:, sl_, :], in0=modt[:, sl_, 16:24], scalar1=1.0, scalar2=1.0 / ALPHA,
                        op0=ALU.add, op1=ALU.mult), reads=[("modt", sl_)], writes=[("g1a", sl_)])
            return f

        def mod_steps_for(sub_ids):
            groups = [(sl_, g) for sl_ in sub_ids for g in range(6)]
            steps = []
            for n in range(len(groups) + 1):
                def step(n=n):
                    if n < len(groups):
                        mod_dma(*groups[n])()
                    if n >= 1:
                        mod_mm(*groups[n - 1])()
                steps.append(step)
            return steps

        modctx["stg"] = [A.alloc(8 * 512, BF16).rearrange("p (k f) -> p k f", k=8) for _ in range(2)]
        for st_ in mod_steps_for([phases[0]]):
            st_()
        side_steps = mod_steps_for([x for x in range(4) if x != phases[0]])
        if phases[0] != 0:
            for st_ in side_steps:
                st_()
            side_steps = []
        S.barrier()

        def load_weights_cast(dst2d, src2d, sem, key=None, last=False):
            S.dma("pool", lambda e: e.dma_start(out=dst2d, in_=src2d), sem,
                  writes=[key] if (last and key is not None) else [])

        def run_phase(pi, s, side=None):
            A.off = persist_mark
            src = srcs[pi].rearrange("(c p) t -> p c t", p=P)
            dst = dsts[pi].rearrange("(c p) t -> p c t", p=P)
            kind = ("mixA", "ffn", "mixC", "ffn")[s]
            cast_eng = "pool" if kind == "ffn" else "dve"
            H = {"mixA": 8, "ffn": 1, "mixC": 15}[kind]
            W = T + 2 * H
            lay = s // 2
            lng = pp[:, OFF_LN + s * 16:OFF_LN + s * 16 + 8]
            lnb = pp[:, OFF_LN + s * 16 + 8:OFF_LN + s * 16 + 16]

            NX = 3 if kind == "mixA" else 2
            AHEAD = 2 if kind == "mixA" else 1
            xt = [A.alloc(KC * W, F32).rearrange("p (c w) -> p c w", c=KC) for _ in range(NX)]
            ht = [A.alloc(KC * W, BF16).rearrange("p (c w) -> p c w", c=KC) for _ in range(2)]
            rb2 = [A.alloc(2 * T, BF16) for _ in range(2)]
            rbf = [r_[:, 0:T] for r_ in rb2]
            rsq = [r_[:, T:2 * T] for r_ in rb2]
            msq = A.alloc(T, F32)
            var = A.alloc(T, F32)
            rstd = A.alloc(T, F32)
            mean_ps = PS[7][:, 0:T]
            e2_ps = PS[7][:, T:2 * T]

            if kind == "ffn":
                wup = A.alloc(KC * 2 * FF, BF16).rearrange("p (k f) -> p k f", k=KC)
                wdn = A.alloc(FJ * D, BF16).rearrange("p (j d) -> p j d", j=FJ)
                at = A.alloc(FJ * T, BF16).rearrange("p (j t) -> p j t", j=FJ)
                accs = [[A.alloc(T, F32) for _ in range(3)] for _ in range(3)]
                CB = 1408
                for cb in (0, 2, 1, 3):
                    for k in range(KC):
                        load_weights_cast(wup[:, k, cb * CB:(cb + 1) * CB],
                                          w_up_d[lay, k * P:(k + 1) * P, cb * CB:(cb + 1) * CB],
                                          f"ldw{cb}", ("wup", cb), last=(k == KC - 1))
                for j in range(FJ):
                    load_weights_cast(wdn[:, j, :], w_down_d[lay, j * P:(j + 1) * P, :], "ldw4", "wdn",
                                      last=(j == FJ - 1))
                fdw = pp[:, OFF_FDW + lay * 132:OFF_FDW + (lay + 1) * 132].rearrange("p (c k) -> p c k", k=3)
                fdb = pp[:, OFF_FDB + lay * 44:OFF_FDB + (lay + 1) * 44]
            elif kind == "mixA":
                win = A.alloc(KC * 1536, BF16).rearrange("p (k f) -> p k f", k=KC)
                wout = A.alloc(KC * D, BF16).rearrange("p (k f) -> p k f", k=KC)
                wst = A.alloc(512, BF16).rearrange("p (h q) -> p h q", h=4)
                plw = A.alloc(512, BF16).rearrange("p (g d) -> p g d", g=4)
                at2 = [A.alloc(KC * T, BF16).rearrange("p (j t) -> p j t", j=KC) for _ in range(2)]
                at = at2[0]
                pb = A.alloc(NPB, F32)
                u_sb2 = [A.alloc(4 * T, F32).rearrange("p (h t) -> p h t", h=4) for _ in range(2)]
                gv = [A.alloc(512, F32) for _ in range(2)]
                vn2 = [[A.alloc(512, BF16) for _ in range(2)] for _ in range(2)]
                zb = [A.alloc(W, F32) for _ in range(4)]
                sab = [[A.alloc(W, F32) for _ in range(2)] for _ in range(2)]
                tmpf = [A.alloc(T, F32) for _ in range(2)]
                pooled2 = [[A.alloc(T, BF16) for _ in range(4)] for _ in range(2)]
                st6 = [A.alloc(8, F32) for _ in range(2)]
                mv = [A.alloc(4, F32) for _ in range(2)]
                mhalf = A.alloc(1, F32)
                ones1 = A.alloc(P, BF16)
                cbt = A.alloc(4 * T, F32).rearrange("p (h t) -> p h t", h=4)
                sgg = pp[:, OFF_SGG:OFF_SGG + 4]
                sgb = pp[:, OFF_SGB:OFF_SGB + 4]
                S.op("pool", lambda e: e.memset(mhalf, -0.5), writes=["mhalf"])
                S.op("pool", lambda e: e.memset(ones1, 1.0), writes=["ones1"])
                S.dma("sp", lambda e: e.dma_start(out=pb, in_=pb_d), "ldpb", writes=["pb"])
                for k in range(KC):
                    load_weights_cast(win[:, k, :], w_in0_d[k * P:(k + 1) * P, :], "ldw0", "win", last=(k == KC - 1))
                load_weights_cast(wst.rearrange("p h q -> p (h q)"), wsT_d, "ldw1", "wst", last=True)
                load_weights_cast(plw.rearrange("p g d -> p (g d)"), poolw_d, "ldw1", "wst", last=True)
                for k in range(KC):
                    load_weights_cast(wout[:, k, :], w_out0_d[k * P:(k + 1) * P, :], "ldw2", "wout", last=(k == KC - 1))
                psc = pp[:, OFF_PSC:OFF_PSC + 4]
                for hd in range(4):
                    rs_ps = PS[hd % 2][:, 0:P]
                    S.op("pe", lambda e, rs_ps=rs_ps, hd=hd: e.matmul(rs_ps, lhsT=ones1, rhs=wst[:, hd, :], start=True, stop=True),
                         reads=["ones1", "wst"], writes=[("zp", hd % 2)], inc=True)
                    for c in range(2):
                        S.op("dve", lambda e, rs_ps=rs_ps, hd=hd, c=c: e.scalar_tensor_tensor(
                            out=cbt[:, hd, c * P:(c + 1) * P], in0=rs_ps, scalar=sgb[:, hd:hd + 1],
                            in1=pb[:, PB_BS + hd * T + c * P:PB_BS + hd * T + (c + 1) * P], op0=ALU.mult, op1=ALU.add),
                            reads=[("zp", hd % 2), "pp", "pb"], writes=["cbt"])
            else:
                win = A.alloc(KC * 2048, BF16).rearrange("p (k f) -> p k f", k=KC)
                wout = A.alloc(KC * D, BF16).rearrange("p (k f) -> p k f", k=KC)
                dg = A.alloc(248 * P, BF16).rearrange("p (n d) -> p n d", n=248)
                at = A.alloc(KC * T, BF16).rearrange("p (j t) -> p j t", j=KC)
                zt = A.alloc(KC * W, BF16).rearrange("p (j w) -> p j w", j=KC)
                cz = A.alloc(KC * T, F32).rearrange("p (j t) -> p j t", j=KC)
                sg = [A.alloc(W, F32) for _ in range(2)]
                ident = A.alloc(P, F32)
                rstd2 = A.alloc(T, F32)
                S.dma("sp", lambda e: e.dma_start(out=ident, in_=pb_d[:, PB_ID:PB_ID + P]), "ldpb", writes=["ident"])
                for k in range(KC):
                    load_weights_cast(win[:, k, :], w_in1_d[k * P:(k + 1) * P, :], "ldw0", "win", last=(k == KC - 1))
                for k in range(KC):
                    load_weights_cast(wout[:, k, :], w_out1_d[k * P:(k + 1) * P, :], "ldw2", "wout", last=(k == KC - 1))
                dwt = pp[:, OFF_DWT:OFF_DWT + 248]
                for n in range(248):
                    if n % 2 == 0:
                        S.op("dve", lambda e, n=n: e.tensor_scalar(
                            out=dg[:, n, :], in0=ident, scalar1=dwt[:, n:n + 1], scalar2=None, op0=ALU.mult),
                            reads=["ident", "pp"], writes=[("dg", n // 31, 0)])
                    else:
                        S.op("act", lambda e, n=n: e.activation(
                            out=dg[:, n, :], in_=ident, func=AF.Copy, scale=dwt[:, n:n + 1]),
                            reads=["ident", "pp"], writes=[("dg", n // 31, 1)])
                cvb = pp[:, OFF_CVB:OFF_CVB + 8]
                cvg = pp[:, OFF_CVLN:OFF_CVLN + 8]
                cvbb = pp[:, OFF_CVLN + 8:OFF_CVLN + 16]

            def tile_span(i):
                lo, hi = i * T - H, i * T + T + H
                clo, chi = max(lo, 0), min(hi, S_LEN)
                return clo, chi, clo - lo, chi - lo

            def stage_A_load(i):
                xs = i % NX
                clo, chi, a, b = tile_span(i)
                rk = [("act", pi - 1, ii) for ii in (i - 1, i, i + 1) if 0 <= ii < NT] if pi > 0 else []
                S.dma("sp", lambda e: e.dma_start(out=xt[xs][:, :, a:b], in_=src[:, :, clo:chi]),
                      f"ldx{xs}", reads=rk, writes=[("xt", xs, c) for c in range(KC)])

            def stage_A_comp(i, c):
                sl = i % 2
                xs = i % NX
                clo, chi, a, b = tile_span(i)
                if c % 2 == 0:
                    S.op("act", lambda e: e.activation(
                        out=ht[sl][:, c, a:b], in_=xt[xs][:, c, a:b], func=AF.Identity,
                        scale=sc1[:, s, c:c + 1], bias=modt[:, s, c:c + 1]),
                        reads=[("xt", xs, c), ("sc1", s), ("modt", s)], writes=[("ht", sl, c)])
                else:
                    S.op("dve", lambda e: e.tensor_scalar(
                        out=ht[sl][:, c, a:b], in0=xt[xs][:, c, a:b],
                        scalar1=sc1[:, s, c:c + 1], scalar2=modt[:, s, c:c + 1], op0=ALU.mult, op1=ALU.add),
                        reads=[("xt", xs, c), ("sc1", s), ("modt", s)], writes=[("ht", sl, c)])
                if a > 0:
                    S.op("pool", lambda e: e.memset(ht[sl][:, c, 0:a], 0.0), writes=[("ht", sl, c)])
                if b < W:
                    S.op("pool", lambda e: e.memset(ht[sl][:, c, b:W], 0.0), writes=[("ht", sl, c)])

            def stage_A(i):
                stage_A_load(i)
                for c in range(KC):
                    stage_A_comp(i, c)

            def stats_mm(src_bf, src_sq, m, par):
                def f():
                    S.op("pe", lambda e: e.matmul(PS[7][:, 0:2 * T], lhsT=onesb, rhs=rb2[par], start=(m == 0), stop=(m == KC - 1)),
                         reads=[("rbf", par), ("rsq", par), "onesb"], writes=["mean_ps", "e2_ps"], inc=True)
                return f

            def ln_pieces(eps, rs):
                return [
                    lambda: S.op("act", lambda e: e.activation(out=msq, in_=mean_ps, func=AF.Square),
                                 reads=["mean_ps"], writes=["msq"]),
                    lambda: S.op("dve", lambda e: e.tensor_tensor(out=var, in0=e2_ps, in1=msq, op=ALU.subtract),
                                 reads=["e2_ps", "msq"], writes=["var"]),
                    lambda: S.op("act", lambda e: e.activation(out=var, in_=var, func=AF.Sqrt, bias=eps_ap(eps), scale=1.0),
                                 reads=["var", "epsc"], writes=["var"]),
                    lambda: S.op("dve", lambda e: e.reciprocal(out=rs, in_=var), reads=["var"], writes=["rstd"]),
                ]

            def ln_scalars(eps, rs):
                for f in ln_pieces(eps, rs):
                    f()

            def E1(i, m, yps):
                sl = i % NX
                par = m % 2
                xi = xt[sl][:, m, H:H + T]
                S.op("dve", lambda e: e.scalar_tensor_tensor(
                    out=xi, in0=yps, scalar=g1a[:, s, m:m + 1], in1=xi, op0=ALU.mult, op1=ALU.add),
                    reads=[("yps", par), ("xt", sl, m), ("g1a", s)], writes=[("xt", sl, m)])
                S.op(cast_eng, lambda e: e.tensor_copy(out=rbf[par], in_=xi),
                     reads=[("xt", sl, m)], writes=[("rbf", par)])
                S.op("act", lambda e: e.activation(out=rsq[par], in_=xi, func=AF.Square),
                     reads=[("xt", sl, m)], writes=[("rsq", par)])
                S.defer(stats_mm(rbf[par], rsq[par], m, par))

            def E2(i):
                sl = i % NX

                def sub(m):
                    xi = xt[sl][:, m, H:H + T]
                    S.op("dve", lambda e: e.tensor_tensor(out=xi, in0=xi, in1=mean_ps, op=ALU.subtract),
                         reads=[("xt", sl, m), "mean_ps"], writes=[("xt", sl, m)])

                def mul(m):
                    xi = xt[sl][:, m, H:H + T]
                    S.op("pool", lambda e: e.tensor_tensor(out=xi, in0=xi, in1=rstd, op=ALU.mult),
                         reads=[("xt", sl, m), "rstd"], writes=[("xt", sl, m)])

                def idn(m):
                    xi = xt[sl][:, m, H:H + T]
                    S.op("pool", lambda e: e.tensor_scalar(
                        out=xi, in0=xi, scalar1=lng[:, m:m + 1], scalar2=lnb[:, m:m + 1], op0=ALU.mult, op1=ALU.add),
                        reads=[("xt", sl, m), "pp"], writes=[("xt", sl, m)])

                def store():
                    S.dma("sp", lambda e: e.dma_start(out=dst[:, :, i * T:(i + 1) * T], in_=xt[sl][:, :, H:H + T]),
                          f"stx{sl}", reads=[("xt", sl, c) for c in range(KC)], writes=[("act", pi, i)])

                pieces = list(ln_pieces(EPS_R, rstd))

                def step(st):
                    def f():
                        if st >= 2:
                            idn(st - 2)
                        if 1 <= st <= KC:
                            mul(st - 1)
                        if st < KC:
                            sub(st)
                    return f
                pieces += [step(st) for st in range(KC + 2)]
                pieces.append(store)
                return pieces

            eps_tiles = {}

            def eps_ap(eps):
                return eps_tiles[eps]

            for epsv in (EPS, EPS_R):
                t_ = A.alloc(1, F32)
                eps_tiles[epsv] = t_
                S.op("pool", lambda e, t_=t_, epsv=epsv: e.memset(t_, epsv), writes=["epsc"])

            def body_ffn(i):
                sl = i % 2

                def gate(j):
                    ag, av, gg = accs[j % 3]
                    S.op("act", lambda e: e.activation(out=gg, in_=ag, func=AF.Gelu_apprx_tanh),
                         reads=[("acc", j % 3, 0)], writes=[("acc", j % 3, 2)])
                    S.op("pool", lambda e: e.tensor_tensor(out=at[:, j, :], in0=gg, in1=av, op=ALU.mult),
                         reads=[("acc", j % 3, 1), ("acc", j % 3, 2)], writes=[("at", j)])

                for j in range(FJ):
                    bz = (j % 2) * 2
                    for half in range(2):
                        zp = PS[bz + half][:, 0:W]
                        cidx = half * FJ + j
                        col = cidx * P
                        cb = col // 1408
                        for k in range(KC):
                            S.op("pe", lambda e, zp=zp, k=k, col=col: e.matmul(
                                zp, lhsT=wup[:, k, col:col + P], rhs=ht[sl][:, k, :], start=(k == 0), stop=(k == KC - 1)),
                                reads=[("wup", cb), ("ht", sl, k)], writes=[("zp", bz + half)], inc=(k == KC - 1))
                        acc = accs[j % 3][half]
                        S.op("act", lambda e, zp=zp, acc=acc, cidx=cidx: e.activation(
                            out=acc, in_=zp[:, 1:1 + T], func=AF.Identity,
                            scale=fdw[:, cidx, 1:2], bias=fdb[:, cidx:cidx + 1]),
                            reads=[("zp", bz + half), "pp"], writes=[("acc", j % 3, half)])
                    if j >= 1:
                        gate(j - 1)
                    for tap in (0, 2):
                        for half in range(2):
                            zp = PS[bz + half][:, 0:W]
                            cidx = half * FJ + j
                            acc = accs[j % 3][half]
                            S.op("dve", lambda e, zp=zp, acc=acc, cidx=cidx, tap=tap: e.scalar_tensor_tensor(
                                out=acc, in0=zp[:, tap:tap + T], scalar=fdw[:, cidx, tap:tap + 1], in1=acc,
                                op0=ALU.mult, op1=ALU.add),
                                reads=[("zp", bz + half), ("acc", j % 3, half)], writes=[("acc", j % 3, half)])
                    S.flush(1)
                    if j >= 2:
                        pend_e2(1)
                    if j == 18 and i + 1 < NT:
                        stage_A_load(i + 1)
                gate(FJ - 1)
                for m in range(KC):
                    yps = PS[4 + m % 2][:, 0:T]
                    for j in range(FJ):
                        S.op("pe", lambda e, yps=yps, j=j, m=m: e.matmul(
                            yps, lhsT=wdn[:, j, m * P:(m + 1) * P], rhs=at[:, j, :], start=(j == 0), stop=(j == FJ - 1)),
                            reads=["wdn", ("at", j)], writes=[("yps", m % 2)], inc=(j == FJ - 1))
                    if m >= 1:
                        S.flush(1)
                    E1(i, m, yps)
                    if i + AHEAD < NT:
                        stage_A_comp(i + AHEAD, m)

            def mixA_parts(i):
                sl = i % 2
                ib = i % 2
                edge = 0 if i == 0 else (1 if i == NT - 1 else None)
                h_ = ht[sl]
                u_sb = u_sb2[ib]
                vn = vn2[ib]
                pooled = pooled2[ib]
                atb = at2[ib]

                def v_part(c):
                    vps = PS[2 + c]
                    for k in range(KC):
                        S.op("pe", lambda e, k=k: e.matmul(
                            vps[:, :], lhsT=h_[:, k, H + c * P:H + (c + 1) * P], rhs=win[:, k, 512:1024],
                            start=(k == 0), stop=(k == KC - 1)),
                            reads=["win", ("ht", sl, k)], writes=[("zp", 2 + c)], inc=(k == KC - 1))
                    S.flush()
                    S.op("act", lambda e: e.activation(out=gv[c], in_=vps[:, :], func=AF.Gelu_apprx_tanh),
                         reads=[("zp", 2 + c)], writes=[("gv", c)])
                    S.op("dve", lambda e: e.bn_stats(out=st6[c][:, 0:6], in_=gv[c]), reads=[("gv", c)], writes=[("st6", c)])
                    S.op("dve", lambda e: e.bn_aggr(out=mv[c][:, 0:2], in_=st6[c][:, 0:6]), reads=[("st6", c)], writes=[("mv", c)])
                    S.op("pool", lambda e: e.tensor_scalar(out=mv[c][:, 2:3], in0=mv[c][:, 1:2], scalar1=EPS, scalar2=None, op0=ALU.add),
                         reads=[("mv", c)], writes=[("mv2", c)])
                    S.op("pool", lambda e: e.tensor_tensor(out=mv[c][:, 3:4], in0=mv[c][:, 2:3], in1=mhalf, op=ALU.pow),
                         reads=[("mv2", c), "mhalf"], writes=[("mv3", c)])
                    S.op("dve", lambda e: e.tensor_scalar(
                        out=vn[c], in0=gv[c], scalar1=mv[c][:, 0:1], scalar2=mv[c][:, 3:4], op0=ALU.subtract, op1=ALU.mult),
                        reads=[("gv", c), ("mv", c), ("mv3", c)], writes=[("vn", ib, c)])
                    pend_e2(2)

                def zb_part(g):
                    zps = PS[g % 2][:, 0:W]
                    for k in range(KC):
                        S.op("pe", lambda e, k=k: e.matmul(
                            zps, lhsT=win[:, k, 1024 + g * P:1024 + (g + 1) * P], rhs=h_[:, k, :], start=(k == 0), stop=(k == KC - 1)),
                            reads=["win", ("ht", sl, k)], writes=[("zp", g % 2)], inc=(k == KC - 1))
                    z = zb[g]
                    S.op("act", lambda e: e.activation(out=z, in_=zps, func=AF.Copy),
                         reads=[("zp", g % 2)], writes=[("zb", g)])
                    pend_e2(2)
                    eng = "dve" if g % 2 == 0 else "pool"
                    spans = [(1, W, 1, 0), (2, W - 1, 1, 1), (4, W - 3, 2, 2), (8, W - 7, 4, 4)]
                    cur = z
                    bufs = sab[g % 2]
                    for lv in range(g + 1):
                        a0, a1, dl, dr = spans[lv]
                        o = bufs[lv % 2]
                        if lv == 0:
                            i0, i1 = cur[:, 0:W - 1], cur[:, 1:W]
                        else:
                            i0, i1 = cur[:, a0 - dl:a1 - dl], cur[:, a0 + dr:a1 + dr]
                        S.op(eng, lambda e, o=o, a0=a0, a1=a1, i0=i0, i1=i1: e.tensor_tensor(
                            out=o[:, a0:a1], in0=i0, in1=i1, op=ALU.add),
                            reads=[("zb", g), ("sab", g % 2, 0), ("sab", g % 2, 1)], writes=[("sab", g % 2, lv % 2)])
                        cur = o
                    pl = pooled[g]
                    wdw = float(2 << g)
                    if edge is None:
                        S.op("dve", lambda e: e.scalar_tensor_tensor(
                            out=pl, in0=cur[:, H:H + T], scalar=1.0 / wdw, in1=z[:, H:H + T], op0=ALU.mult, op1=ALU.subtract),
                            reads=[("sab", g % 2, 0), ("sab", g % 2, 1), ("zb", g)], writes=[("pooled", ib, g)])
                    else:
                        ic = pb[:, PB_IC + (edge * 4 + g) * T:PB_IC + (edge * 4 + g + 1) * T]
                        tf = tmpf[g % 2]
                        S.op(eng, lambda e: e.tensor_tensor(out=tf, in0=cur[:, H:H + T], in1=ic, op=ALU.mult),
                             reads=[("sab", g % 2, 0), ("sab", g % 2, 1), "pb"], writes=[("tmpf", g % 2)])
                        S.op(eng, lambda e: e.tensor_tensor(out=pl, in0=tf, in1=z[:, H:H + T], op=ALU.subtract),
                             reads=[("tmpf", g % 2), ("zb", g)], writes=[("pooled", ib, g)])

                def u_part(hd):
                    ups = PS[hd % 2][:, 0:T]
                    for k in range(KC):
                        S.op("pe", lambda e, k=k: e.matmul(
                            ups, lhsT=win[:, k, hd * P:(hd + 1) * P], rhs=h_[:, k, H:H + T], start=(k == 0), stop=(k == KC - 1)),
                            reads=["win", ("ht", sl, k)], writes=[("zp", hd % 2)], inc=(k == KC - 1))
                    S.op("act", lambda e: e.activation(out=u_sb[:, hd, :], in_=ups, func=AF.Gelu_apprx_tanh),
                         reads=[("zp", hd % 2)], writes=[("u", ib, hd)])
                    pend_e2(1)

                def mix_part():
                    for c in range(2):
                        for hd in range(4):
                            mx = PS[4 + hd // 2][:, (hd % 2) * T + c * P:(hd % 2) * T + (c + 1) * P]
                            S.op("pe", lambda e, mx=mx, c=c, hd=hd: e.matmul(
                                mx, lhsT=vn[c][:, hd * P:(hd + 1) * P], rhs=wst[:, hd, :], start=True, stop=True),
                                reads=[("vn", ib, c), "wst"], writes=[("yps", hd // 2)], inc=True)
                    for hd in range(4):
                        mxf = PS[4 + hd // 2][:, (hd % 2) * T:(hd % 2 + 1) * T]
                        tf = tmpf[hd % 2]
                        S.op("dve", lambda e, mxf=mxf, tf=tf, hd=hd: e.scalar_tensor_tensor(
                            out=tf, in0=mxf, scalar=sgg[:, hd:hd + 1], in1=cbt[:, hd, :], op0=ALU.mult, op1=ALU.add),
                            reads=[("yps", hd // 2), "cbt", "pp"], writes=[("tmpf", hd % 2)])
                        S.op("pool", lambda e, tf=tf, hd=hd: e.tensor_tensor(out=atb[:, hd, :], in0=tf, in1=u_sb[:, hd, :], op=ALU.mult),
                             reads=[("tmpf", hd % 2), ("u", ib, hd)], writes=[("at", ib, hd)])

                def pw_part(g):
                    pw = PS[2 + g % 2][:, 0:T]
                    S.op("pe", lambda e: e.matmul(pw, lhsT=plw[:, g, :], rhs=pooled[g], start=True, stop=True),
                         reads=["wst", ("pooled", ib, g)], writes=[("zp", 2 + g % 2)], inc=True)
                    S.op("act", lambda e: e.activation(out=atb[:, 4 + g, :], in_=pw, func=AF.Copy, scale=psc[:, g:g + 1]),
                         reads=[("zp", 2 + g % 2), "pp"], writes=[("at", ib, 4 + g)])

                def X():
                    v_part(0)
                    v_part(1)
                    for g in range(4):
                        zb_part(g)
                    for hd in range(4):
                        u_part(hd)
                    pend_e2()
                    mix_part()
                    for g in range(4):
                        pw_part(g)

                def Y():
                    if i + 2 < NT:
                        stage_A_load(i + 2)
                    out_proj(i, atb, ib)
                return X, Y

            def out_proj(i, atx=None, ibx=None):
                atx = at if atx is None else atx
                for m in range(KC):
                    yps = PS[4 + m % 2][:, 0:T]
                    for k in range(KC):
                        S.op("pe", lambda e, yps=yps, k=k, m=m: e.matmul(
                            yps, lhsT=wout[:, k, m * P:(m + 1) * P], rhs=atx[:, k, :], start=(k == 0), stop=(k == KC - 1)),
                            reads=["wout", ("at", k) if ibx is None else ("at", ibx, k)], writes=[("yps", m % 2)], inc=(k == KC - 1))
                    if m >= 2:
                        S.flush(1)
                    E1(i, m, yps)
                    if i + AHEAD < NT:
                        stage_A_comp(i + AHEAD, m)

            def body_mixC(i):
                sl = i % 2
                h_ = ht[sl]
                for j in range(KC):
                    aps = PS[(j % 2) * 2][:, 0:W]
                    gps = PS[(j % 2) * 2 + 1][:, 0:W]
                    for half, zp in ((0, aps), (1, gps)):
                        col = half * D + j * P
                        for k in range(KC):
                            S.op("pe", lambda e, zp=zp, k=k, col=col: e.matmul(
                                zp, lhsT=win[:, k, col:col + P], rhs=h_[:, k, :], start=(k == 0), stop=(k == KC - 1)),
                                reads=["win", ("ht", sl, k)], writes=[("zp", (j % 2) * 2 + half)], inc=(k == KC - 1))
                    sgt = sg[j % 2]
                    S.op("act", lambda e, gps=gps, sgt=sgt: e.activation(out=sgt, in_=gps, func=AF.Sigmoid),
                         reads=[("zp", (j % 2) * 2 + 1)], writes=[("sg", j % 2)])
                    S.op("dve", lambda e, aps=aps, sgt=sgt, j=j: e.tensor_tensor(out=zt[:, j, :], in0=aps, in1=sgt, op=ALU.mult),
                         reads=[("zp", (j % 2) * 2), ("sg", j % 2)], writes=[("zt", j)])
                    if j == 0:
                        S.flush()
                    pend_e2(2)
                for j in range(KC):
                    cps = PS[4 + j % 2][:, 0:T]
                    for k in range(31):
                        S.op("pe", lambda e, cps=cps, j=j, k=k: e.matmul(
                            cps, lhsT=dg[:, j * 31 + k, :], rhs=zt[:, j, k:k + T], start=(k == 0), stop=(k == 30)),
                            reads=[("dg", j, 0), ("dg", j, 1), ("zt", j)], writes=[("yps", j % 2)], inc=(k == 30))
                    par = j % 2
                    S.op("act", lambda e, cps=cps, j=j: e.activation(out=cz[:, j, :], in_=cps, func=AF.Identity, bias=cvb[:, j:j + 1], scale=1.0),
                         reads=[("yps", par), "pp"], writes=[("cz", j)])
                    S.op("pool", lambda e, j=j, par=par: e.tensor_copy(out=rbf[par], in_=cz[:, j, :]),
                         reads=[("cz", j)], writes=[("rbf", par)])
                    S.op("act", lambda e, j=j, par=par: e.activation(out=rsq[par], in_=cz[:, j, :], func=AF.Square),
                         reads=[("cz", j)], writes=[("rsq", par)])
                    if j >= 1:
                        S.flush(1)
                    S.defer(stats_mm(rbf[par], rsq[par], j, par))
                    if j == 0 and i + 1 < NT:
                        pend_e2()
                        stage_A_load(i + 1)
                S.flush()
                ln_scalars(EPS, rstd2)
                for j in range(KC):
                    S.op("dve", lambda e, j=j: e.tensor_tensor(out=cz[:, j, :], in0=cz[:, j, :], in1=mean_ps, op=ALU.subtract),
                         reads=[("cz", j), "mean_ps"], writes=[("cz", j)])
                    S.op("pool", lambda e, j=j: e.tensor_tensor(out=cz[:, j, :], in0=cz[:, j, :], in1=rstd2, op=ALU.mult),
                         reads=[("cz", j), "rstd"], writes=[("cz", j)])
                    S.op("act", lambda e, j=j: e.activation(out=at[:, j, :], in_=cz[:, j, :], func=AF.Silu,
                                                            scale=cvg[:, j:j + 1], bias=cvbb[:, j:j + 1]),
                         reads=[("cz", j), "pp"], writes=[("at", j)])
                out_proj(i)

            pend = []

            def pend_e2(n=None):
                k = len(pend) if n is None else min(n, len(pend))
                for _ in range(k):
                    pend.pop(0)()

            if side:
                modctx["stg"] = [A.alloc(8 * 512, BF16).rearrange("p (k f) -> p k f", k=8) for _ in range(2)]
            stage_A(0)
            if kind == "mixA":
                if NT > 1:
                    stage_A(1)
                Ys = {}
                for i in range(NT):
                    if side and i >= 1:
                        side.pop(0)()
                    X, Ys[i] = mixA_parts(i)
                    X()
                    if i >= 1:
                        Ys.pop(i - 1)()
                        pend.extend(E2(i - 1))
                S.flush()
                pend_e2()
                Ys.pop(NT - 1)()
                pend.extend(E2(NT - 1))
            else:
                body = {"ffn": body_ffn, "mixC": body_mixC}[kind]
                for i in range(NT):
                    if side and i >= 1:
                        side.pop(0)()
                    body(i)
                    pend.extend(E2(i))
            S.flush()
            pend_e2()
            while side:
                side.pop(0)()
            S.barrier()

        for pi, s in enumerate(phases):
            run_phase(pi, s, side_steps if pi == 0 else None)

        S.wait_keys("sp", [("act", nph - 1, i) for i in range(NT)])
        S.emit()
    return nc


def _chunkT(v):
    return np.ascontiguousarray(v.reshape(-1, P).T)


def _icnt(S_LEN):
    ic = np.zeros((2, 4, T), np.float32)
    for e in range(2):
        for g in range(4):
            half = 1 << g
            t = np.arange(T) + (0 if e == 0 else S_LEN - T)
            cnt = np.minimum(t + half, S_LEN) - np.maximum(t - half, 0)
            ic[e, g] = 1.0 / cnt
    return ic


def prep_shared(inp, S_LEN):
    f = lambda a: np.ascontiguousarray(np.asarray(a, dtype=np.float32))
    sh = {}
    sh["adaw"] = f(np.stack([inp["mix_ada_w"][0], inp["ffn_ada_w"][0], inp["mix_ada_w"][1], inp["ffn_ada_w"][1]]))
    sh["w_in0"] = f(inp["ab_w_in"][0])
    sh["w_out0"] = f(inp["ab_w_out"][0])
    sh["wsT"] = f(np.transpose(inp["ab_ws"][0], (2, 0, 1)).reshape(P, 512))
    sh["poolw"] = f(np.transpose(inp["ab_pool_w"][0], (1, 0, 2)).reshape(P, 512))
    sh["w_in1"] = f(inp["cv_w_in"][0])
    sh["w_out1"] = f(inp["cv_w_out"][0])
    sh["w_up"] = f(inp["ffn_w_up"])
    sh["w_down"] = f(inp["ffn_w_down"])
    pb = np.zeros((P, NPB), np.float32)
    pb[:, PB_SG:PB_SG + 512] = np.broadcast_to(inp["ab_sgu_ln_g"][0][None, :], (P, 512))
    pb[:, PB_SB:PB_SB + 512] = np.broadcast_to(inp["ab_sgu_ln_b"][0][None, :], (P, 512))
    bs = np.asarray(inp["ab_bs"][0], np.float32)
    pb[:, PB_BS:PB_BS + 1024] = np.broadcast_to(np.tile(bs, (1, 2)).reshape(1, 1024), (P, 1024))
    pb[:, PB_IC:PB_IC + 2048] = np.broadcast_to(_icnt(S_LEN).reshape(1, 2048), (P, 2048))
    pb[:, PB_ID:PB_ID + P] = np.eye(P, dtype=np.float32)
    sh["pb"] = pb
    pp = np.zeros((P, NPP), np.float32)
    adab = [inp["mix_ada_b"][0], inp["ffn_ada_b"][0], inp["mix_ada_b"][1], inp["ffn_ada_b"][1]]
    lng = [inp["mix_ln_g"][0], inp["ffn_ln_g"][0], inp["mix_ln_g"][1], inp["ffn_ln_g"][1]]
    lnb = [inp["mix_ln_b"][0], inp["ffn_ln_b"][0], inp["mix_ln_b"][1], inp["ffn_ln_b"][1]]
    for s in range(4):
        pp[:, OFF_ADAB + 24 * s:OFF_ADAB + 24 * s + 24] = _chunkT(np.asarray(adab[s], np.float32))
        pp[:, OFF_LN + 16 * s:OFF_LN + 16 * s + 8] = _chunkT(np.asarray(lng[s], np.float32))
        pp[:, OFF_LN + 16 * s + 8:OFF_LN + 16 * s + 16] = _chunkT(np.asarray(lnb[s], np.float32))
    pp[:, OFF_PSC:OFF_PSC + 4] = _chunkT(np.asarray(inp["ab_pool_scale"][0], np.float32))
    pp[:, OFF_SGG:OFF_SGG + 4] = _chunkT(np.asarray(inp["ab_sgu_ln_g"][0], np.float32))
    pp[:, OFF_SGB:OFF_SGB + 4] = _chunkT(np.asarray(inp["ab_sgu_ln_b"][0], np.float32))
    pp[:, OFF_CVB:OFF_CVB + 8] = _chunkT(np.asarray(inp["cv_dw_b"][0], np.float32))
    pp[:, OFF_CVLN:OFF_CVLN + 8] = _chunkT(np.asarray(inp["cv_ln_g"][0], np.float32))
    pp[:, OFF_CVLN + 8:OFF_CVLN + 16] = _chunkT(np.asarray(inp["cv_ln_b"][0], np.float32))
    dw = np.asarray(inp["cv_dw"][0], np.float32)
    pp[:, OFF_DWT:OFF_DWT + 248] = dw.reshape(31, 8, P).transpose(2, 1, 0).reshape(P, 248)
    for l in range(2):
        fd = np.asarray(inp["ffn_dw"][l], np.float32)
        pp[:, OFF_FDW + 132 * l:OFF_FDW + 132 * (l + 1)] = fd.reshape(3, 44, P).transpose(2, 1, 0).reshape(P, 132)
        pp[:, OFF_FDB + 44 * l:OFF_FDB + 44 * (l + 1)] = _chunkT(np.asarray(inp["ffn_dw_b"][l], np.float32))
    return sh, pp


_NC_CACHE = {}


def kernel(**inputs):
    x = np.asarray(inputs["x"], np.float32)
    c = np.asarray(inputs["c"], np.float32)
    B, S_LEN, _ = x.shape
    sh, pp0 = prep_shared(inputs, S_LEN)
    key = (S_LEN,)
    if key not in _NC_CACHE:
        _NC_CACHE[key] = build_nc(S_LEN)
    nc = _NC_CACHE[key]
    in_maps = []
    for b in range(B):
        pp = pp0.copy()
        pp[:, OFF_C:OFF_C + 8] = _chunkT(c[b])
        m = dict(sh)
        m["pp"] = pp
        m["xT"] = np.ascontiguousarray(x[b].T)
        in_maps.append(m)
    res = run_bass_kernel_spmd(nc, in_maps, core_ids=list(range(B)))
    out = np.empty((B, S_LEN, D), np.float32)
    for b in range(B):
        out[b] = res.results[b]["outT"].T
    return out
```

```python
from contextlib import ExitStack
import numpy as np
import concourse.bass as bass
import concourse.mybir as mybir
from concourse.bass_utils import run_bass_kernel_spmd

F32 = mybir.dt.float32
BF16 = mybir.dt.bfloat16
AF = mybir.ActivationFunctionType
ALU = mybir.AluOpType

P = 128
D = 1024
KC = 8
T = 256
FF = 2816
FJ = 22
ALPHA = 4.0 ** 0.25
EPS = 1e-5
EPS_R = EPS / (ALPHA * ALPHA)

OFF_C, OFF_ADAB, OFF_LN, OFF_PSC, OFF_CVB, OFF_CVLN, OFF_DWT, OFF_FDW, OFF_FDB, OFF_SGG, OFF_SGB, NPP = (
    0, 8, 104, 168, 172, 180, 196, 444, 708, 796, 800, 804)
PB_SG, PB_SB, PB_BS, PB_IC, PB_ID, NPB = 0, 512, 1024, 2048, 4096, 4224

ENGS = ("pe", "act", "dve", "pool", "sp")


class Sched:
    def __init__(self, nc, es):
        self.nc = nc
        self.es = es
        self.ops = {e: [] for e in ENGS}
        self.sems = {}
        self.cnt = {}
        self.known = {e: {} for e in ENGS}
        self.last_w = {}
        self.readers = {}
        self.deferred = []
        for e in ("pe", "act", "dve", "pool"):
            self._sem("E_" + e)

    def _sem(self, name):
        if name not in self.sems:
            self.sems[name] = self.es.enter_context(self.nc.semaphore(name))
            self.cnt[name] = 0
        return self.sems[name]

    def _deps(self, eng, reads, writes, is_dma):
        need = {}
        own = "E_" + eng

        def add(s, v):
            if (not is_dma) and s == own and eng == "pe":
                return
            if need.get(s, 0) < v:
                need[s] = v

        for k in reads:
            lw = self.last_w.get(k)
            if lw is not None:
                add(*lw)
        for k in writes:
            lw = self.last_w.get(k)
            if lw is not None:
                add(*lw)
            for s, v in self.readers.get(k, {}).items():
                add(s, v)
        waits = []
        kn = self.known[eng]
        for s, v in need.items():
            if kn.get(s, 0) < v:
                kn[s] = v
                waits.append((s, v))
        return waits

    def _commit(self, tok, reads, writes):
        s, v = tok
        for k in reads:
            r = self.readers.setdefault(k, {})
            if r.get(s, 0) < v:
                r[s] = v
        for k in writes:
            self.last_w[k] = tok
            self.readers[k] = {}

    def op(self, eng, fn, reads=(), writes=(), inc=True):
        waits = self._deps(eng, reads, writes, False)
        s = "E_" + eng
        tok = (s, self.cnt[s] + 1)
        if inc:
            self.cnt[s] += 1
        self.ops[eng].append((waits, fn, (s, 1) if inc else None))
        self._commit(tok, reads, writes)
        return tok

    def dma(self, eng, fn, sem, reads=(), writes=()):
        self._sem(sem)
        waits = self._deps(eng, reads, writes, True)
        self.cnt[sem] += 16
        tok = (sem, self.cnt[sem])
        self.ops[eng].append((waits, fn, (sem, 16)))
        self._commit(tok, reads, writes)
        return tok

    def barrier(self):
        for e in ENGS:
            waits = []
            kn = self.known[e]
            for s, v in self.cnt.items():
                if v > 0 and s != "E_" + e and kn.get(s, 0) < v:
                    kn[s] = v
                    waits.append((s, v))
            self.ops[e].append((waits, None, None))

    def wait_keys(self, eng, keys):
        waits = self._deps(eng, keys, (), True)
        self.ops[eng].append((waits, None, None))

    def defer(self, fn):
        self.deferred.append(fn)

    def flush(self, n=None):
        k = len(self.deferred) if n is None else min(n, len(self.deferred))
        for _ in range(k):
            self.deferred.pop(0)()

    def emit(self):
        nc = self.nc
        with nc.Block() as block:
            def run(engname):
                def body(e):
                    for waits, fn, inc in self.ops[engname]:
                        for s, v in waits:
                            e.wait_ge(self.sems[s], v)
                        if fn is None:
                            continue
                        inst = fn(e)
                        if inc is not None:
                            inst.then_inc(self.sems[inc[0]], inc[1])
                return body
            block.tensor(run("pe"))
            block.scalar(run("act"))
            block.vector(run("dve"))
            block.gpsimd(run("pool"))
            block.sync(run("sp"))


class Arena:
    def __init__(self, handle, n32):
        self.h = handle
        self.n = n32
        self.off = 0

    def alloc(self, nelem, dt):
        nb = nelem * (2 if dt == BF16 else 4)
        n32 = (nb + 15) // 16 * 4
        assert self.off + n32 <= self.n, f"arena overflow {self.off}+{n32}>{self.n}"
        ap = self.h[:, self.off:self.off + n32]
        self.off += n32
        if dt == BF16:
            ap = ap.bitcast(BF16)
        return ap[:, 0:nelem]


def build_nc(S_LEN=8192, phases=(0, 1, 2, 3)):
    NT = S_LEN // T
    nc = bass.Bass("TRN2", target_bir_lowering=False)

    def dram(name, shape, kind="ExternalInput"):
        return nc.dram_tensor(name, shape, F32, kind=kind).ap()

    xT = dram("xT", [D, S_LEN])
    outT = dram("outT", [D, S_LEN], "ExternalOutput")
    pp_d = dram("pp", [P, NPP])
    pb_d = dram("pb", [P, NPB])
    adaw_d = dram("adaw", [4, D, 3 * D])
    w_in0_d = dram("w_in0", [D, 1536])
    w_out0_d = dram("w_out0", [D, D])
    wsT_d = dram("wsT", [P, 512])
    poolw_d = dram("poolw", [P, 512])
    w_in1_d = dram("w_in1", [D, 2048])
    w_out1_d = dram("w_out1", [D, D])
    w_up_d = dram("w_up", [2, D, 2 * FF])
    w_down_d = dram("w_down", [2, FF, D])
    nph = len(phases)
    acts = [dram(f"act{i}", [D, S_LEN], "Internal") for i in range(max(nph - 1, 0))]
    srcs = [xT] + acts
    dsts = acts + [outT]

    with ExitStack() as es:
        S = Sched(nc, es)
        ARENA32 = 50176
        arena_h = es.enter_context(nc.sbuf_tensor("arena", [P, ARENA32], F32))
        A = Arena(arena_h, ARENA32)
        PS = [es.enter_context(nc.psum_tensor(f"ps{i}", [P, 512], F32)) for i in range(8)]

        pp = A.alloc(NPP, F32)
        modt = A.alloc(4 * 24, F32).rearrange("p (s f) -> p s f", s=4)
        sc1 = A.alloc(4 * 8, F32).rearrange("p (s f) -> p s f", s=4)
        g1a = A.alloc(4 * 8, F32).rearrange("p (s f) -> p s f", s=4)
        csil = A.alloc(8, F32)
        onesb = A.alloc(P, BF16)

        S.dma("sp", lambda e: e.dma_start(out=pp, in_=pp_d), "ldpp", writes=["pp"])
        S.op("pool", lambda e: e.memset(onesb, 1.0 / D), writes=["onesb"])

        csil_bf = A.alloc(8, BF16)
        persist_mark = A.off
        S.op("act", lambda e: e.activation(out=csil, in_=pp[:, OFF_C:OFF_C + 8], func=AF.Silu),
             reads=["pp"], writes=["csil"])
        S.op("act", lambda e: e.activation(out=csil_bf, in_=csil, func=AF.Copy), reads=["csil"], writes=["csil_bf"])
        modctx = {}
        MPS = PS[6]

        def mod_dma(sl_, g):
            def f():
                st = modctx["stg"][g % 2]
                srcw = adaw_d[sl_].rearrange("(k p) f -> p k f", p=P)[:, :, g * 512:(g + 1) * 512]
                S.dma("pool", lambda e: e.dma_start(out=st, in_=srcw), f"ldst{g % 2}", writes=[("stg", g % 2)])
            return f

        def mod_mm(sl_, g):
            def f():
                st = modctx["stg"][g % 2]
                for fl in range(4):
                    fc = g * 4 + fl
                    for k in range(KC):
                        S.op("pe", lambda e, k=k, fl=fl, fc=fc: e.matmul(
                            MPS[:, fc:fc + 1], lhsT=st[:, k, fl * P:(fl + 1) * P], rhs=csil_bf[:, k:k + 1],
                            start=(k == 0), stop=(k == KC - 1)),
                            reads=[("stg", g % 2), "csil_bf"], writes=["mps"], inc=(k == KC - 1))
                if g == 5:
                    S.op("dve", lambda e: e.tensor_tensor(
                        out=modt[:, sl_, :], in0=MPS[:, 0:24], in1=pp[:, OFF_ADAB + 24 * sl_:OFF_ADAB + 24 * sl_ + 24],
                        op=ALU.add), reads=["mps", "pp"], writes=[("modt", sl_)])
                    S.op("dve", lambda e: e.tensor_scalar(
                        out=sc1[:, sl_, :], in0=modt[:, sl_, 8:16], scalar1=1.0, scalar2=None, op0=ALU.add),
                        reads=[("modt", sl_)], writes=[("sc1", sl_)])
                    S.op("dve", lambda e: e.tensor_scalar(
                        out=g1a[:, sl_, :], in0=modt[:, sl_, 16:24], scalar1=1.0, scalar2=1.0 / ALPHA,
                        op0=ALU.add, op1=ALU.mult), reads=[("modt", sl_)], writes=[("g1a", sl_)])
            return f

        def mod_steps_for(sub_ids):
            groups = [(sl_, g) for sl_ in sub_ids for g in range(6)]
            steps = []
            for n in range(len(groups) + 1):
                def step(n=n):
                    if n < len(groups):
                        mod_dma(*groups[n])()
                    if n >= 1:
                        mod_mm(*groups[n - 1])()
                steps.append(step)
            return steps

        modctx["stg"] = [A.alloc(8 * 512, BF16).rearrange("p (k f) -> p k f", k=8) for _ in range(2)]
        for st_ in mod_steps_for([phases[0]]):
            st_()
        side_steps = mod_steps_for([x for x in range(4) if x != phases[0]])
        if phases[0] != 0:
            for st_ in side_steps:
                st_()
            side_steps = []
        S.barrier()

        def load_weights_cast(dst2d, src2d, sem, key=None, last=False):
            S.dma("pool", lambda e: e.dma_start(out=dst2d, in_=src2d), sem,
                  writes=[key] if (last and key is not None) else [])

        def run_phase(pi, s, side=None):
            A.off = persist_mark
            src = srcs[pi].rearrange("(c p) t -> p c t", p=P)
            dst = dsts[pi].rearrange("(c p) t -> p c t", p=P)
            kind = ("mixA", "ffn", "mixC", "ffn")[s]
            cast_eng = "pool" if kind == "ffn" else "dve"
            H = {"mixA": 8, "ffn": 1, "mixC": 15}[kind]
            W = T + 2 * H
            lay = s // 2
            lng = pp[:, OFF_LN + s * 16:OFF_LN + s * 16 + 8]
            lnb = pp[:, OFF_LN + s * 16 + 8:OFF_LN + s * 16 + 16]

            NX = 4 if kind == "mixA" else 2
            AHEAD = 2 if kind == "mixA" else 1
            xt = [A.alloc(KC * W, F32).rearrange("p (c w) -> p c w", c=KC) for _ in range(NX)]
            ht = [A.alloc(KC * W, BF16).rearrange("p (c w) -> p c w", c=KC) for _ in range(2)]
            rb2 = [A.alloc(2 * T, BF16) for _ in range(2)]
            rbf = [r_[:, 0:T] for r_ in rb2]
            rsq = [r_[:, T:2 * T] for r_ in rb2]
            msq = A.alloc(T, F32)
            var = A.alloc(T, F32)
            rstd = A.alloc(T, F32)
            mean_ps = PS[7][:, 0:T]
            e2_ps = PS[7][:, T:2 * T]

            if kind == "ffn":
                wup = A.alloc(KC * 2 * FF, BF16).rearrange("p (k f) -> p k f", k=KC)
                wdn = A.alloc(FJ * D, BF16).rearrange("p (j d) -> p j d", j=FJ)
                at = A.alloc(FJ * T, BF16).rearrange("p (j t) -> p j t", j=FJ)
                accs = [[A.alloc(T, F32) for _ in range(3)] for _ in range(3)]
                CB = 1408
                for cb in (0, 2, 1, 3):
                    for k in range(KC):
                        load_weights_cast(wup[:, k, cb * CB:(cb + 1) * CB],
                                          w_up_d[lay, k * P:(k + 1) * P, cb * CB:(cb + 1) * CB],
                                          f"ldw{cb}", ("wup", cb), last=(k == KC - 1))
                for j in range(FJ):
                    load_weights_cast(wdn[:, j, :], w_down_d[lay, j * P:(j + 1) * P, :], "ldw4", "wdn",
                                      last=(j == FJ - 1))
                fdw = pp[:, OFF_FDW + lay * 132:OFF_FDW + (lay + 1) * 132].rearrange("p (c k) -> p c k", k=3)
                fdb = pp[:, OFF_FDB + lay * 44:OFF_FDB + (lay + 1) * 44]
            elif kind == "mixA":
                win = A.alloc(KC * 1536, BF16).rearrange("p (k f) -> p k f", k=KC)
                wout = A.alloc(KC * D, BF16).rearrange("p (k f) -> p k f", k=KC)
                wst = A.alloc(512, BF16).rearrange("p (h q) -> p h q", h=4)
                plw = A.alloc(512, BF16).rearrange("p (g d) -> p g d", g=4)
                at2 = [A.alloc(KC * T, BF16).rearrange("p (j t) -> p j t", j=KC) for _ in range(2)]
                at = at2[0]
                pb = A.alloc(NPB, F32)
                u_sb2 = [A.alloc(4 * T, F32).rearrange("p (h t) -> p h t", h=4) for _ in range(2)]
                gv = [A.alloc(512, F32) for _ in range(2)]
                vn2 = [[A.alloc(512, BF16) for _ in range(2)] for _ in range(2)]
                zb = [A.alloc(W, F32) for _ in range(4)]
                sab = [[A.alloc(W, F32) for _ in range(2)] for _ in range(2)]
                tmpf = [A.alloc(T, F32) for _ in range(2)]
                pooled2 = [[A.alloc(T, BF16) for _ in range(4)] for _ in range(2)]
                st6 = [A.alloc(8, F32) for _ in range(2)]
                mv = [A.alloc(4, F32) for _ in range(2)]
                mhalf = A.alloc(1, F32)
                ones1 = A.alloc(P, BF16)
                cbt = A.alloc(4 * T, F32).rearrange("p (h t) -> p h t", h=4)
                sgg = pp[:, OFF_SGG:OFF_SGG + 4]
                sgb = pp[:, OFF_SGB:OFF_SGB + 4]
                S.op("pool", lambda e: e.memset(mhalf, -0.5), writes=["mhalf"])
                S.op("pool", lambda e: e.memset(ones1, 1.0), writes=["ones1"])
                S.dma("sp", lambda e: e.dma_start(out=pb, in_=pb_d), "ldpb", writes=["pb"])
                for k in range(KC):
                    load_weights_cast(win[:, k, :], w_in0_d[k * P:(k + 1) * P, :], "ldw0", "win", last=(k == KC - 1))
                load_weights_cast(wst.rearrange("p h q -> p (h q)"), wsT_d, "ldw1", "wst", last=True)
                load_weights_cast(plw.rearrange("p g d -> p (g d)"), poolw_d, "ldw1", "wst", last=True)
                for k in range(KC):
                    load_weights_cast(wout[:, k, :], w_out0_d[k * P:(k + 1) * P, :], "ldw2", "wout", last=(k == KC - 1))
                psc = pp[:, OFF_PSC:OFF_PSC + 4]
                for hd in range(4):
                    rs_ps = PS[hd % 2][:, 0:P]
                    S.op("pe", lambda e, rs_ps=rs_ps, hd=hd: e.matmul(rs_ps, lhsT=ones1, rhs=wst[:, hd, :], start=True, stop=True),
                         reads=["ones1", "wst"], writes=[("zp", hd % 2)], inc=True)
                    for c in range(2):
                        S.op("dve", lambda e, rs_ps=rs_ps, hd=hd, c=c: e.scalar_tensor_tensor(
                            out=cbt[:, hd, c * P:(c + 1) * P], in0=rs_ps, scalar=sgb[:, hd:hd + 1],
                            in1=pb[:, PB_BS + hd * T + c * P:PB_BS + hd * T + (c + 1) * P], op0=ALU.mult, op1=ALU.add),
                            reads=[("zp", hd % 2), "pp", "pb"], writes=["cbt"])
            else:
                win = A.alloc(KC * 2048, BF16).rearrange("p (k f) -> p k f", k=KC)
                wout = A.alloc(KC * D, BF16).rearrange("p (k f) -> p k f", k=KC)
                dg = A.alloc(248 * P, BF16).rearrange("p (n d) -> p n d", n=248)
                at = A.alloc(KC * T, BF16).rearrange("p (j t) -> p j t", j=KC)
                zt = A.alloc(KC * W, BF16).rearrange("p (j w) -> p j w", j=KC)
                cz = A.alloc(KC * T, F32).rearrange("p (j t) -> p j t", j=KC)
                sg = [A.alloc(W, F32) for _ in range(2)]
                ident = A.alloc(P, F32)
                rstd2 = A.alloc(T, F32)
                S.dma("sp", lambda e: e.dma_start(out=ident, in_=pb_d[:, PB_ID:PB_ID + P]), "ldpb", writes=["ident"])
                for k in range(KC):
                    load_weights_cast(win[:, k, :], w_in1_d[k * P:(k + 1) * P, :], "ldw0", "win", last=(k == KC - 1))
                for k in range(KC):
                    load_weights_cast(wout[:, k, :], w_out1_d[k * P:(k + 1) * P, :], "ldw2", "wout", last=(k == KC - 1))
                dwt = pp[:, OFF_DWT:OFF_DWT + 248]
                for n in range(248):
                    if n % 2 == 0:
                        S.op("dve", lambda e, n=n: e.tensor_scalar(
                            out=dg[:, n, :], in0=ident, scalar1=dwt[:, n:n + 1], scalar2=None, op0=ALU.mult),
                            reads=["ident", "pp"], writes=[("dg", n // 31, 0)])
                    else:
                        S.op("act", lambda e, n=n: e.activation(
                            out=dg[:, n, :], in_=ident, func=AF.Copy, scale=dwt[:, n:n + 1]),
                            reads=["ident", "pp"], writes=[("dg", n // 31, 1)])
                cvb = pp[:, OFF_CVB:OFF_CVB + 8]
                cvg = pp[:, OFF_CVLN:OFF_CVLN + 8]
                cvbb = pp[:, OFF_CVLN + 8:OFF_CVLN + 16]

            def tile_span(i):
                lo, hi = i * T - H, i * T + T + H
                clo, chi = max(lo, 0), min(hi, S_LEN)
                return clo, chi, clo - lo, chi - lo

            def stage_A_load(i):
                xs = i % NX
                clo, chi, a, b = tile_span(i)
                rk = [("act", pi - 1, ii) for ii in (i - 1, i, i + 1) if 0 <= ii < NT] if pi > 0 else []
                S.dma("sp", lambda e: e.dma_start(out=xt[xs][:, :, a:b], in_=src[:, :, clo:chi]),
                      f"ldx{xs}", reads=rk, writes=[("xt", xs, c) for c in range(KC)])

            def stage_A_comp(i, c):
                sl = i % 2
                xs = i % NX
                clo, chi, a, b = tile_span(i)
                if c % 2 == 0:
                    S.op("act", lambda e: e.activation(
                        out=ht[sl][:, c, a:b], in_=xt[xs][:, c, a:b], func=AF.Identity,
                        scale=sc1[:, s, c:c + 1], bias=modt[:, s, c:c + 1]),
                        reads=[("xt", xs, c), ("sc1", s), ("modt", s)], writes=[("ht", sl, c)])
                else:
                    S.op("dve", lambda e: e.tensor_scalar(
                        out=ht[sl][:, c, a:b], in0=xt[xs][:, c, a:b],
                        scalar1=sc1[:, s, c:c + 1], scalar2=modt[:, s, c:c + 1], op0=ALU.mult, op1=ALU.add),
                        reads=[("xt", xs, c), ("sc1", s), ("modt", s)], writes=[("ht", sl, c)])
                if a > 0:
                    S.op("pool", lambda e: e.memset(ht[sl][:, c, 0:a], 0.0), writes=[("ht", sl, c)])
                if b < W:
                    S.op("pool", lambda e: e.memset(ht[sl][:, c, b:W], 0.0), writes=[("ht", sl, c)])

            def stage_A(i):
                stage_A_load(i)
                for c in range(KC):
                    stage_A_comp(i, c)

            def stats_mm(src_bf, src_sq, m, par):
                def f():
                    S.op("pe", lambda e: e.matmul(PS[7][:, 0:2 * T], lhsT=onesb, rhs=rb2[par], start=(m == 0), stop=(m == KC - 1)),
                         reads=[("rbf", par), ("rsq", par), "onesb"], writes=["mean_ps", "e2_ps"], inc=True)
                return f

            def ln_pieces(eps, rs):
                return [
                    lambda: S.op("act", lambda e: e.activation(out=msq, in_=mean_ps, func=AF.Square),
                                 reads=["mean_ps"], writes=["msq"]),
                    lambda: S.op("dve", lambda e: e.tensor_tensor(out=var, in0=e2_ps, in1=msq, op=ALU.subtract),
                                 reads=["e2_ps", "msq"], writes=["var"]),
                    lambda: S.op("act", lambda e: e.activation(out=var, in_=var, func=AF.Sqrt, bias=eps_ap(eps), scale=1.0),
                                 reads=["var", "epsc"], writes=["var"]),
                    lambda: S.op("dve", lambda e: e.reciprocal(out=rs, in_=var), reads=["var"], writes=["rstd"]),
                ]

            def ln_scalars(eps, rs):
                for f in ln_pieces(eps, rs):
                    f()

            def E1(i, m, yps):
                sl = i % NX
                par = m % 2
                xi = xt[sl][:, m, H:H + T]
                S.op("dve", lambda e: e.scalar_tensor_tensor(
                    out=xi, in0=yps, scalar=g1a[:, s, m:m + 1], in1=xi, op0=ALU.mult, op1=ALU.add),
                    reads=[("yps", par), ("xt", sl, m), ("g1a", s)], writes=[("xt", sl, m)])
                S.op(cast_eng, lambda e: e.tensor_copy(out=rbf[par], in_=xi),
                     reads=[("xt", sl, m)], writes=[("rbf", par)])
                S.op("act", lambda e: e.activation(out=rsq[par], in_=xi, func=AF.Square),
                     reads=[("xt", sl, m)], writes=[("rsq", par)])
                S.defer(stats_mm(rbf[par], rsq[par], m, par))

            def E2(i):
                sl = i % NX

                def sub(m):
                    xi = xt[sl][:, m, H:H + T]
                    S.op("dve", lambda e: e.tensor_tensor(out=xi, in0=xi, in1=mean_ps, op=ALU.subtract),
                         reads=[("xt", sl, m), "mean_ps"], writes=[("xt", sl, m)])

                def mul(m):
                    xi = xt[sl][:, m, H:H + T]
                    S.op("pool", lambda e: e.tensor_tensor(out=xi, in0=xi, in1=rstd, op=ALU.mult),
                         reads=[("xt", sl, m), "rstd"], writes=[("xt", sl, m)])

                def idn(m):
                    xi = xt[sl][:, m, H:H + T]
                    S.op("act", lambda e: e.activation(
                        out=xi, in_=xi, func=AF.Identity, scale=lng[:, m:m + 1], bias=lnb[:, m:m + 1]),
                        reads=[("xt", sl, m), "pp"], writes=[("xt", sl, m)])

                def store():
                    S.dma("sp", lambda e: e.dma_start(out=dst[:, :, i * T:(i + 1) * T], in_=xt[sl][:, :, H:H + T]),
                          f"stx{sl}", reads=[("xt", sl, c) for c in range(KC)], writes=[("act", pi, i)])

                pieces = list(ln_pieces(EPS_R, rstd))

                def step(st):
                    def f():
                        if st >= 2:
                            idn(st - 2)
                        if 1 <= st <= KC:
                            mul(st - 1)
                        if st < KC:
                            sub(st)
                    return f
                pieces += [step(st) for st in range(KC + 2)]
                pieces.append(store)
                return pieces

            eps_tiles = {}

            def eps_ap(eps):
                return eps_tiles[eps]

            for epsv in (EPS, EPS_R):
                t_ = A.alloc(1, F32)
                eps_tiles[epsv] = t_
                S.op("pool", lambda e, t_=t_, epsv=epsv: e.memset(t_, epsv), writes=["epsc"])

            def body_ffn(i):
                sl = i % 2

                def gate(j):
                    ag, av, gg = accs[j % 3]
                    S.op("act", lambda e: e.activation(out=gg, in_=ag, func=AF.Gelu_apprx_tanh),
                         reads=[("acc", j % 3, 0)], writes=[("acc", j % 3, 2)])
                    S.op("pool", lambda e: e.tensor_tensor(out=at[:, j, :], in0=gg, in1=av, op=ALU.mult),
                         reads=[("acc", j % 3, 1), ("acc", j % 3, 2)], writes=[("at", j)])

                for j in range(FJ):
                    bz = (j % 2) * 2
                    for half in range(2):
                        zp = PS[bz + half][:, 0:W]
                        cidx = half * FJ + j
                        col = cidx * P
                        cb = col // 1408
                        for k in range(KC):
                            S.op("pe", lambda e, zp=zp, k=k, col=col: e.matmul(
                                zp, lhsT=wup[:, k, col:col + P], rhs=ht[sl][:, k, :], start=(k == 0), stop=(k == KC - 1)),
                                reads=[("wup", cb), ("ht", sl, k)], writes=[("zp", bz + half)], inc=(k == KC - 1))
                        acc = accs[j % 3][half]
                        S.op("act", lambda e, zp=zp, acc=acc, cidx=cidx: e.activation(
                            out=acc, in_=zp[:, 1:1 + T], func=AF.Identity,
                            scale=fdw[:, cidx, 1:2], bias=fdb[:, cidx:cidx + 1]),
                            reads=[("zp", bz + half), "pp"], writes=[("acc", j % 3, half)])
                    if j >= 1:
                        gate(j - 1)
                    for tap in (0, 2):
                        for half in range(2):
                            zp = PS[bz + half][:, 0:W]
                            cidx = half * FJ + j
                            acc = accs[j % 3][half]
                            S.op("dve", lambda e, zp=zp, acc=acc, cidx=cidx, tap=tap: e.scalar_tensor_tensor(
                                out=acc, in0=zp[:, tap:tap + T], scalar=fdw[:, cidx, tap:tap + 1], in1=acc,
                                op0=ALU.mult, op1=ALU.add),
                                reads=[("zp", bz + half), ("acc", j % 3, half)], writes=[("acc", j % 3, half)])
                    S.flush(1)
                    if j >= 2:
                        pend_e2(1)
                    if j == 18 and i + 1 < NT:
                        stage_A_load(i + 1)
                gate(FJ - 1)
                for m in range(KC):
                    yps = PS[4 + m % 2][:, 0:T]
                    for j in range(FJ):
                        S.op("pe", lambda e, yps=yps, j=j, m=m: e.matmul(
                            yps, lhsT=wdn[:, j, m * P:(m + 1) * P], rhs=at[:, j, :], start=(j == 0), stop=(j == FJ - 1)),
                            reads=["wdn", ("at", j)], writes=[("yps", m % 2)], inc=(j == FJ - 1))
                    if m >= 1:
                        S.flush(1)
                    E1(i, m, yps)
                    if i + AHEAD < NT:
                        stage_A_comp(i + AHEAD, m)

            def mixA_parts(i):
                sl = i % 2
                ib = i % 2
                edge = 0 if i == 0 else (1 if i == NT - 1 else None)
                h_ = ht[sl]
                u_sb = u_sb2[ib]
                vn = vn2[ib]
                pooled = pooled2[ib]
                atb = at2[ib]

                def v_part(c):
                    vps = PS[2 + c]
                    for k in range(KC):
                        S.op("pe", lambda e, k=k: e.matmul(
                            vps[:, :], lhsT=h_[:, k, H + c * P:H + (c + 1) * P], rhs=win[:, k, 512:1024],
                            start=(k == 0), stop=(k == KC - 1)),
                            reads=["win", ("ht", sl, k)], writes=[("zp", 2 + c)], inc=(k == KC - 1))
                    S.flush()
                    S.op("act", lambda e: e.activation(out=gv[c], in_=vps[:, :], func=AF.Gelu_apprx_tanh),
                         reads=[("zp", 2 + c)], writes=[("gv", c)])
                    S.op("dve", lambda e: e.bn_stats(out=st6[c][:, 0:6], in_=gv[c]), reads=[("gv", c)], writes=[("st6", c)])
                    S.op("dve", lambda e: e.bn_aggr(out=mv[c][:, 0:2], in_=st6[c][:, 0:6]), reads=[("st6", c)], writes=[("mv", c)])
                    S.op("pool", lambda e: e.tensor_scalar(out=mv[c][:, 2:3], in0=mv[c][:, 1:2], scalar1=EPS, scalar2=None, op0=ALU.add),
                         reads=[("mv", c)], writes=[("mv2", c)])
                    S.op("pool", lambda e: e.tensor_tensor(out=mv[c][:, 3:4], in0=mv[c][:, 2:3], in1=mhalf, op=ALU.pow),
                         reads=[("mv2", c), "mhalf"], writes=[("mv3", c)])
                    S.op("dve", lambda e: e.tensor_scalar(
                        out=vn[c], in0=gv[c], scalar1=mv[c][:, 0:1], scalar2=mv[c][:, 3:4], op0=ALU.subtract, op1=ALU.mult),
                        reads=[("gv", c), ("mv", c), ("mv3", c)], writes=[("vn", ib, c)])
                    pend_e2(2)

                def zb_part(g):
                    zps = PS[g % 2][:, 0:W]
                    for k in range(KC):
                        S.op("pe", lambda e, k=k: e.matmul(
                            zps, lhsT=win[:, k, 1024 + g * P:1024 + (g + 1) * P], rhs=h_[:, k, :], start=(k == 0), stop=(k == KC - 1)),
                            reads=["win", ("ht", sl, k)], writes=[("zp", g % 2)], inc=(k == KC - 1))
                    z = zb[g]
                    S.op("act", lambda e: e.activation(out=z, in_=zps, func=AF.Copy),
                         reads=[("zp", g % 2)], writes=[("zb", g)])
                    pend_e2(2)
                    eng = "dve" if g % 2 == 0 else "pool"
                    spans = [(1, W, 1, 0), (2, W - 1, 1, 1), (4, W - 3, 2, 2), (8, W - 7, 4, 4)]
                    cur = z
                    bufs = sab[g % 2]
                    for lv in range(g + 1):
                        a0, a1, dl, dr = spans[lv]
                        o = bufs[lv % 2]
                        if lv == 0:
                            i0, i1 = cur[:, 0:W - 1], cur[:, 1:W]
                        else:
                            i0, i1 = cur[:, a0 - dl:a1 - dl], cur[:, a0 + dr:a1 + dr]
                        S.op(eng, lambda e, o=o, a0=a0, a1=a1, i0=i0, i1=i1: e.tensor_tensor(
                            out=o[:, a0:a1], in0=i0, in1=i1, op=ALU.add),
                            reads=[("zb", g), ("sab", g % 2, 0), ("sab", g % 2, 1)], writes=[("sab", g % 2, lv % 2)])
                        cur = o
                    pl = pooled[g]
                    wdw = float(2 << g)
                    if edge is None:
                        S.op("dve", lambda e: e.scalar_tensor_tensor(
                            out=pl, in0=cur[:, H:H + T], scalar=1.0 / wdw, in1=z[:, H:H + T], op0=ALU.mult, op1=ALU.subtract),
                            reads=[("sab", g % 2, 0), ("sab", g % 2, 1), ("zb", g)], writes=[("pooled", ib, g)])
                    else:
                        ic = pb[:, PB_IC + (edge * 4 + g) * T:PB_IC + (edge * 4 + g + 1) * T]
                        tf = tmpf[g % 2]
                        S.op(eng, lambda e: e.tensor_tensor(out=tf, in0=cur[:, H:H + T], in1=ic, op=ALU.mult),
                             reads=[("sab", g % 2, 0), ("sab", g % 2, 1), "pb"], writes=[("tmpf", g % 2)])
                        S.op(eng, lambda e: e.tensor_tensor(out=pl, in0=tf, in1=z[:, H:H + T], op=ALU.subtract),
                             reads=[("tmpf", g % 2), ("zb", g)], writes=[("pooled", ib, g)])

                def u_part(hd):
                    ups = PS[hd % 2][:, 0:T]
                    for k in range(KC):
                        S.op("pe", lambda e, k=k: e.matmul(
                            ups, lhsT=win[:, k, hd * P:(hd + 1) * P], rhs=h_[:, k, H:H + T], start=(k == 0), stop=(k == KC - 1)),
                            reads=["win", ("ht", sl, k)], writes=[("zp", hd % 2)], inc=(k == KC - 1))
                    S.op("act", lambda e: e.activation(out=u_sb[:, hd, :], in_=ups, func=AF.Gelu_apprx_tanh),
                         reads=[("zp", hd % 2)], writes=[("u", ib, hd)])
                    pend_e2(1)

                def mix_part():
                    for c in range(2):
                        for hd in range(4):
                            mx = PS[4 + hd // 2][:, (hd % 2) * T + c * P:(hd % 2) * T + (c + 1) * P]
                            S.op("pe", lambda e, mx=mx, c=c, hd=hd: e.matmul(
                                mx, lhsT=vn[c][:, hd * P:(hd + 1) * P], rhs=wst[:, hd, :], start=True, stop=True),
                                reads=[("vn", ib, c), "wst"], writes=[("yps", hd // 2)], inc=True)
                    for hd in range(4):
                        mxf = PS[4 + hd // 2][:, (hd % 2) * T:(hd % 2 + 1) * T]
                        tf = tmpf[hd % 2]
                        S.op("dve", lambda e, mxf=mxf, tf=tf, hd=hd: e.scalar_tensor_tensor(
                            out=tf, in0=mxf, scalar=sgg[:, hd:hd + 1], in1=cbt[:, hd, :], op0=ALU.mult, op1=ALU.add),
                            reads=[("yps", hd // 2), "cbt", "pp"], writes=[("tmpf", hd % 2)])
                        S.op("pool", lambda e, tf=tf, hd=hd: e.tensor_tensor(out=atb[:, hd, :], in0=tf, in1=u_sb[:, hd, :], op=ALU.mult),
                             reads=[("tmpf", hd % 2), ("u", ib, hd)], writes=[("at", ib, hd)])

                def pw_part(g):
                    pw = PS[2 + g % 2][:, 0:T]
                    S.op("pe", lambda e: e.matmul(pw, lhsT=plw[:, g, :], rhs=pooled[g], start=True, stop=True),
                         reads=["wst", ("pooled", ib, g)], writes=[("zp", 2 + g % 2)], inc=True)
                    S.op("act", lambda e: e.activation(out=atb[:, 4 + g, :], in_=pw, func=AF.Copy, scale=psc[:, g:g + 1]),
                         reads=[("zp", 2 + g % 2), "pp"], writes=[("at", ib, 4 + g)])

                def X():
                    if 2 <= i + 1 < NT:
                        stage_A_load(i + 1)
                    v_part(0)
                    v_part(1)
                    for g in range(4):
                        zb_part(g)
                    for hd in range(4):
                        u_part(hd)
                    pend_e2()
                    mix_part()
                    for g in range(4):
                        pw_part(g)

                def Y():
                    out_proj(i, atb, ib)
                return X, Y

            def out_proj(i, atx=None, ibx=None):
                atx = at if atx is None else atx
                for m in range(KC):
                    yps = PS[4 + m % 2][:, 0:T]
                    for k in range(KC):
                        S.op("pe", lambda e, yps=yps, k=k, m=m: e.matmul(
                            yps, lhsT=wout[:, k, m * P:(m + 1) * P], rhs=atx[:, k, :], start=(k == 0), stop=(k == KC - 1)),
                            reads=["wout", ("at", k) if ibx is None else ("at", ibx, k)], writes=[("yps", m % 2)], inc=(k == KC - 1))
                    if m >= 2:
                        S.flush(1)
                    E1(i, m, yps)
                    if i + AHEAD < NT:
                        stage_A_comp(i + AHEAD, m)

            def body_mixC(i):
                sl = i % 2
                h_ = ht[sl]
                for j in range(KC):
                    aps = PS[(j % 2) * 2][:, 0:W]
                    gps = PS[(j % 2) * 2 + 1][:, 0:W]
                    for half, zp in ((0, aps), (1, gps)):
                        col = half * D + j * P
                        for k in range(KC):
                            S.op("pe", lambda e, zp=zp, k=k, col=col: e.matmul(
                                zp, lhsT=win[:, k, col:col + P], rhs=h_[:, k, :], start=(k == 0), stop=(k == KC - 1)),
                                reads=["win", ("ht", sl, k)], writes=[("zp", (j % 2) * 2 + half)], inc=(k == KC - 1))
                    sgt = sg[j % 2]
                    S.op("act", lambda e, gps=gps, sgt=sgt: e.activation(out=sgt, in_=gps, func=AF.Sigmoid),
                         reads=[("zp", (j % 2) * 2 + 1)], writes=[("sg", j % 2)])
                    S.op("dve", lambda e, aps=aps, sgt=sgt, j=j: e.tensor_tensor(out=zt[:, j, :], in0=aps, in1=sgt, op=ALU.mult),
                         reads=[("zp", (j % 2) * 2), ("sg", j % 2)], writes=[("zt", j)])
                    if j == 0:
                        S.flush()
                    pend_e2(2)
                for j in range(KC):
                    cps = PS[4 + j % 2][:, 0:T]
                    for k in range(31):
                        S.op("pe", lambda e, cps=cps, j=j, k=k: e.matmul(
                            cps, lhsT=dg[:, j * 31 + k, :], rhs=zt[:, j, k:k + T], start=(k == 0), stop=(k == 30)),
                            reads=[("dg", j, 0), ("dg", j, 1), ("zt", j)], writes=[("yps", j % 2)], inc=(k == 30))
                    par = j % 2
                    S.op("act", lambda e, cps=cps, j=j: e.activation(out=cz[:, j, :], in_=cps, func=AF.Identity, bias=cvb[:, j:j + 1], scale=1.0),
                         reads=[("yps", par), "pp"], writes=[("cz", j)])
                    S.op("pool", lambda e, j=j, par=par: e.tensor_copy(out=rbf[par], in_=cz[:, j, :]),
                         reads=[("cz", j)], writes=[("rbf", par)])
                    S.op("act", lambda e, j=j, par=par: e.activation(out=rsq[par], in_=cz[:, j, :], func=AF.Square),
                         reads=[("cz", j)], writes=[("rsq", par)])
                    if j >= 1:
                        S.flush(1)
                    S.defer(stats_mm(rbf[par], rsq[par], j, par))
                    if j == 0 and i + 1 < NT:
                        pend_e2()
                        stage_A_load(i + 1)
                S.flush()
                ln_scalars(EPS, rstd2)
                for j in range(KC):
                    S.op("dve", lambda e, j=j: e.tensor_tensor(out=cz[:, j, :], in0=cz[:, j, :], in1=mean_ps, op=ALU.subtract),
                         reads=[("cz", j), "mean_ps"], writes=[("cz", j)])
                    S.op("pool", lambda e, j=j: e.tensor_tensor(out=cz[:, j, :], in0=cz[:, j, :], in1=rstd2, op=ALU.mult),
                         reads=[("cz", j), "rstd"], writes=[("cz", j)])
                    S.op("act", lambda e, j=j: e.activation(out=at[:, j, :], in_=cz[:, j, :], func=AF.Silu,
                                                            scale=cvg[:, j:j + 1], bias=cvbb[:, j:j + 1]),
                         reads=[("cz", j), "pp"], writes=[("at", j)])
                out_proj(i)

            pend = []

            def pend_e2(n=None):
                k = len(pend) if n is None else min(n, len(pend))
                for _ in range(k):
                    pend.pop(0)()

            if side:
                modctx["stg"] = [A.alloc(8 * 512, BF16).rearrange("p (k f) -> p k f", k=8) for _ in range(2)]
            stage_A(0)
            if kind == "mixA":
                if NT > 1:
                    stage_A(1)
                Ys = {}
                for i in range(NT):
                    if side and i >= 1:
                        side.pop(0)()
                    X, Ys[i] = mixA_parts(i)
                    X()
                    if i >= 1:
                        Ys.pop(i - 1)()
                        pend.extend(E2(i - 1))
                S.flush()
                pend_e2()
                Ys.pop(NT - 1)()
                pend.extend(E2(NT - 1))
            else:
                body = {"ffn": body_ffn, "mixC": body_mixC}[kind]
                for i in range(NT):
                    if side and i >= 1:
                        side.pop(0)()
                    body(i)
                    pend.extend(E2(i))
            S.flush()
            pend_e2()
            while side:
                side.pop(0)()
            S.barrier()

        for pi, s in enumerate(phases):
            run_phase(pi, s, side_steps if pi == 0 else None)

        S.wait_keys("sp", [("act", nph - 1, i) for i in range(NT)])
        S.emit()
    return nc


def _chunkT(v):
    return np.ascontiguousarray(v.reshape(-1, P).T)


def _icnt(S_LEN):
    ic = np.zeros((2, 4, T), np.float32)
    for e in range(2):
        for g in range(4):
            half = 1 << g
            t = np.arange(T) + (0 if e == 0 else S_LEN - T)
            cnt = np.minimum(t + half, S_LEN) - np.maximum(t - half, 0)
            ic[e, g] = 1.0 / cnt
    return ic


def prep_shared(inp, S_LEN):
    f = lambda a: np.ascontiguousarray(np.asarray(a, dtype=np.float32))
    sh = {}
    sh["adaw"] = f(np.stack([inp["mix_ada_w"][0], inp["ffn_ada_w"][0], inp["mix_ada_w"][1], inp["ffn_ada_w"][1]]))
    sh["w_in0"] = f(inp["ab_w_in"][0])
    sh["w_out0"] = f(inp["ab_w_out"][0])
    sh["wsT"] = f(np.transpose(inp["ab_ws"][0], (2, 0, 1)).reshape(P, 512))
    sh["poolw"] = f(np.transpose(inp["ab_pool_w"][0], (1, 0, 2)).reshape(P, 512))
    sh["w_in1"] = f(inp["cv_w_in"][0])
    sh["w_out1"] = f(inp["cv_w_out"][0])
    sh["w_up"] = f(inp["ffn_w_up"])
    sh["w_down"] = f(inp["ffn_w_down"])
    pb = np.zeros((P, NPB), np.float32)
    pb[:, PB_SG:PB_SG + 512] = np.broadcast_to(inp["ab_sgu_ln_g"][0][None, :], (P, 512))
    pb[:, PB_SB:PB_SB + 512] = np.broadcast_to(inp["ab_sgu_ln_b"][0][None, :], (P, 512))
    bs = np.asarray(inp["ab_bs"][0], np.float32)
    pb[:, PB_BS:PB_BS + 1024] = np.broadcast_to(np.tile(bs, (1, 2)).reshape(1, 1024), (P, 1024))
    pb[:, PB_IC:PB_IC + 2048] = np.broadcast_to(_icnt(S_LEN).reshape(1, 2048), (P, 2048))
    pb[:, PB_ID:PB_ID + P] = np.eye(P, dtype=np.float32)
    sh["pb"] = pb
    pp = np.zeros((P, NPP), np.float32)
    adab = [inp["mix_ada_b"][0], inp["ffn_ada_b"][0], inp["mix_ada_b"][1], inp["ffn_ada_b"][1]]
    lng = [inp["mix_ln_g"][0], inp["ffn_ln_g"][0], inp["mix_ln_g"][1], inp["ffn_ln_g"][1]]
    lnb = [inp["mix_ln_b"][0], inp["ffn_ln_b"][0], inp["mix_ln_b"][1], inp["ffn_ln_b"][1]]
    for s in range(4):
        pp[:, OFF_ADAB + 24 * s:OFF_ADAB + 24 * s + 24] = _chunkT(np.asarray(adab[s], np.float32))
        pp[:, OFF_LN + 16 * s:OFF_LN + 16 * s + 8] = _chunkT(np.asarray(lng[s], np.float32))
        pp[:, OFF_LN + 16 * s + 8:OFF_LN + 16 * s + 16] = _chunkT(np.asarray(lnb[s], np.float32))
    pp[:, OFF_PSC:OFF_PSC + 4] = _chunkT(np.asarray(inp["ab_pool_scale"][0], np.float32))
    pp[:, OFF_SGG:OFF_SGG + 4] = _chunkT(np.asarray(inp["ab_sgu_ln_g"][0], np.float32))
    pp[:, OFF_SGB:OFF_SGB + 4] = _chunkT(np.asarray(inp["ab_sgu_ln_b"][0], np.float32))
    pp[:, OFF_CVB:OFF_CVB + 8] = _chunkT(np.asarray(inp["cv_dw_b"][0], np.float32))
    pp[:, OFF_CVLN:OFF_CVLN + 8] = _chunkT(np.asarray(inp["cv_ln_g"][0], np.float32))
    pp[:, OFF_CVLN + 8:OFF_CVLN + 16] = _chunkT(np.asarray(inp["cv_ln_b"][0], np.float32))
    dw = np.asarray(inp["cv_dw"][0], np.float32)
    pp[:, OFF_DWT:OFF_DWT + 248] = dw.reshape(31, 8, P).transpose(2, 1, 0).reshape(P, 248)
    for l in range(2):
        fd = np.asarray(inp["ffn_dw"][l], np.float32)
        pp[:, OFF_FDW + 132 * l:OFF_FDW + 132 * (l + 1)] = fd.reshape(3, 44, P).transpose(2, 1, 0).reshape(P, 132)
        pp[:, OFF_FDB + 44 * l:OFF_FDB + 44 * (l + 1)] = _chunkT(np.asarray(inp["ffn_dw_b"][l], np.float32))
    return sh, pp


_NC_CACHE = {}


def kernel(**inputs):
    x = np.asarray(inputs["x"], np.float32)
    c = np.asarray(inputs["c"], np.float32)
    B, S_LEN, _ = x.shape
    sh, pp0 = prep_shared(inputs, S_LEN)
    key = (S_LEN,)
    if key not in _NC_CACHE:
        _NC_CACHE[key] = build_nc(S_LEN)
    nc = _NC_CACHE[key]
    in_maps = []
    for b in range(B):
        pp = pp0.copy()
        pp[:, OFF_C:OFF_C + 8] = _chunkT(c[b])
        m = dict(sh)
        m["pp"] = pp
        m["xT"] = np.ascontiguousarray(x[b].T)
        in_maps.append(m)
    res = run_bass_kernel_spmd(nc, in_maps, core_ids=list(range(B)))
    out = np.empty((B, S_LEN, D), np.float32)
    for b in range(B):
        out[b] = res.results[b]["outT"].T
    return out
```

```python
from contextlib import ExitStack
import numpy as np
import concourse.bass as bass
import concourse.mybir as mybir
from concourse.bass_utils import run_bass_kernel_spmd

F32 = mybir.dt.float32
BF16 = mybir.dt.bfloat16
AF = mybir.ActivationFunctionType
ALU = mybir.AluOpType

P = 128
D = 1024
KC = 8
T = 256
FF = 2816
FJ = 22
ALPHA = 4.0 ** 0.25
EPS = 1e-5
EPS_R = EPS / (ALPHA * ALPHA)

OFF_C, OFF_ADAB, OFF_LN, OFF_PSC, OFF_CVB, OFF_CVLN, OFF_DWT, OFF_FDW, OFF_FDB, OFF_SGG, OFF_SGB, NPP = (
    0, 8, 104, 168, 172, 180, 196, 444, 708, 796, 800, 804)
PB_SG, PB_SB, PB_BS, PB_IC, PB_ID, NPB = 0, 512, 1024, 2048, 4096, 4224

ENGS = ("pe", "act", "dve", "pool", "sp")


class Sched:
    def __init__(self, nc, es):
        self.nc = nc
        self.es = es
        self.ops = {e: [] for e in ENGS}
        self.sems = {}
        self.cnt = {}
        self.known = {e: {} for e in ENGS}
        self.last_w = {}
        self.readers = {}
        self.deferred = []
        for e in ("pe", "act", "dve", "pool"):
            self._sem("E_" + e)

    def _sem(self, name):
        if name not in self.sems:
            self.sems[name] = self.es.enter_context(self.nc.semaphore(name))
            self.cnt[name] = 0
        return self.sems[name]

    def _deps(self, eng, reads, writes, is_dma):
        need = {}
        own = "E_" + eng

        def add(s, v):
            if (not is_dma) and s == own and eng == "pe":
                return
            if need.get(s, 0) < v:
                need[s] = v

        for k in reads:
            lw = self.last_w.get(k)
            if lw is not None:
                add(*lw)
        for k in writes:
            lw = self.last_w.get(k)
            if lw is not None:
                add(*lw)
            for s, v in self.readers.get(k, {}).items():
                add(s, v)
        waits = []
        kn = self.known[eng]
        for s, v in need.items():
            if kn.get(s, 0) < v:
                kn[s] = v
                waits.append((s, v))
        return waits

    def _commit(self, tok, reads, writes):
        s, v = tok
        for k in reads:
            r = self.readers.setdefault(k, {})
            if r.get(s, 0) < v:
                r[s] = v
        for k in writes:
            self.last_w[k] = tok
            self.readers[k] = {}

    def op(self, eng, fn, reads=(), writes=(), inc=True):
        waits = self._deps(eng, reads, writes, False)
        s = "E_" + eng
        tok = (s, self.cnt[s] + 1)
        if inc:
            self.cnt[s] += 1
        self.ops[eng].append((waits, fn, (s, 1) if inc else None))
        self._commit(tok, reads, writes)
        return tok

    def dma(self, eng, fn, sem, reads=(), writes=()):
        self._sem(sem)
        waits = self._deps(eng, reads, writes, True)
        self.cnt[sem] += 16
        tok = (sem, self.cnt[sem])
        self.ops[eng].append((waits, fn, (sem, 16)))
        self._commit(tok, reads, writes)
        return tok

    def barrier(self):
        for e in ENGS:
            waits = []
            kn = self.known[e]
            for s, v in self.cnt.items():
                if v > 0 and s != "E_" + e and kn.get(s, 0) < v:
                    kn[s] = v
                    waits.append((s, v))
            self.ops[e].append((waits, None, None))

    def wait_keys(self, eng, keys):
        waits = self._deps(eng, keys, (), True)
        self.ops[eng].append((waits, None, None))

    def defer(self, fn):
        self.deferred.append(fn)

    def flush(self, n=None):
        k = len(self.deferred) if n is None else min(n, len(self.deferred))
        for _ in range(k):
            self.deferred.pop(0)()

    def emit(self):
        nc = self.nc
        with nc.Block() as block:
            def run(engname):
                def body(e):
                    for waits, fn, inc in self.ops[engname]:
                        for s, v in waits:
                            e.wait_ge(self.sems[s], v)
                        if fn is None:
                            continue
                        inst = fn(e)
                        if inc is not None:
                            inst.then_inc(self.sems[inc[0]], inc[1])
                return body
            block.tensor(run("pe"))
            block.scalar(run("act"))
            block.vector(run("dve"))
            block.gpsimd(run("pool"))
            block.sync(run("sp"))


class Arena:
    def __init__(self, handle, n32):
        self.h = handle
        self.n = n32
        self.off = 0

    def alloc(self, nelem, dt):
        nb = nelem * (2 if dt == BF16 else 4)
        n32 = (nb + 15) // 16 * 4
        assert self.off + n32 <= self.n, f"arena overflow {self.off}+{n32}>{self.n}"
        ap = self.h[:, self.off:self.off + n32]
        self.off += n32
        if dt == BF16:
            ap = ap.bitcast(BF16)
        return ap[:, 0:nelem]


def build_nc(S_LEN=8192, phases=(0, 1, 2, 3)):
    NT = S_LEN // T
    nc = bass.Bass("TRN2", target_bir_lowering=False)

    def dram(name, shape, kind="ExternalInput"):
        return nc.dram_tensor(name, shape, F32, kind=kind).ap()

    xT = dram("xT", [D, S_LEN])
    outT = dram("outT", [D, S_LEN], "ExternalOutput")
    pp_d = dram("pp", [P, NPP])
    pb_d = dram("pb", [P, NPB])
    adaw_d = dram("adaw", [4, D, 3 * D])
    w_in0_d = dram("w_in0", [D, 1536])
    w_out0_d = dram("w_out0", [D, D])
    wsT_d = dram("wsT", [P, 512])
    poolw_d = dram("poolw", [P, 512])
    w_in1_d = dram("w_in1", [D, 2048])
    w_out1_d = dram("w_out1", [D, D])
    w_up_d = dram("w_up", [2, D, 2 * FF])
    w_down_d = dram("w_down", [2, FF, D])
    nph = len(phases)
    acts = [dram(f"act{i}", [D, S_LEN], "Internal") for i in range(max(nph - 1, 0))]
    srcs = [xT] + acts
    dsts = acts + [outT]

    with ExitStack() as es:
        S = Sched(nc, es)
        ARENA32 = 50176
        arena_h = es.enter_context(nc.sbuf_tensor("arena", [P, ARENA32], F32))
        A = Arena(arena_h, ARENA32)
        PS = [es.enter_context(nc.psum_tensor(f"ps{i}", [P, 512], F32)) for i in range(8)]

        pp = A.alloc(NPP, F32)
        modt = A.alloc(4 * 24, F32).rearrange("p (s f) -> p s f", s=4)
        sc1 = A.alloc(4 * 8, F32).rearrange("p (s f) -> p s f", s=4)
        g1a = A.alloc(4 * 8, F32).rearrange("p (s f) -> p s f", s=4)
        csil = A.alloc(8, F32)
        onesb = A.alloc(P, BF16)

        S.dma("sp", lambda e: e.dma_start(out=pp, in_=pp_d), "ldpp", writes=["pp"])
        S.op("pool", lambda e: e.memset(onesb, 1.0 / D), writes=["onesb"])

        csil_bf = A.alloc(8, BF16)
        persist_mark = A.off
        S.op("act", lambda e: e.activation(out=csil, in_=pp[:, OFF_C:OFF_C + 8], func=AF.Silu),
             reads=["pp"], writes=["csil"])
        S.op("act", lambda e: e.activation(out=csil_bf, in_=csil, func=AF.Copy), reads=["csil"], writes=["csil_bf"])
        modctx = {}
        MPS = PS[6]

        def mod_dma(sl_, g):
            def f():
                st = modctx["stg"][g % 2]
                srcw = adaw_d[sl_].rearrange("(k p) f -> p k f", p=P)[:, :, g * 512:(g + 1) * 512]
                S.dma("pool", lambda e: e.dma_start(out=st, in_=srcw), f"ldst{g % 2}", writes=[("stg", g % 2)])
            return f

        def mod_mm(sl_, g):
            def f():
                st = modctx["stg"][g % 2]
                for fl in range(4):
                    fc = g * 4 + fl
                    for k in range(KC):
                        S.op("pe", lambda e, k=k, fl=fl, fc=fc: e.matmul(
                            MPS[:, fc:fc + 1], lhsT=st[:, k, fl * P:(fl + 1) * P], rhs=csil_bf[:, k:k + 1],
                            start=(k == 0), stop=(k == KC - 1)),
                            reads=[("stg", g % 2), "csil_bf"], writes=["mps"], inc=(k == KC - 1))
                if g == 5:
                    S.op("dve", lambda e: e.tensor_tensor(
                        out=modt[:, sl_, :], in0=MPS[:, 0:24], in1=pp[:, OFF_ADAB + 24 * sl_:OFF_ADAB + 24 * sl_ + 24],
                        op=ALU.add), reads=["mps", "pp"], writes=[("modt", sl_)])
                    S.op("dve", lambda e: e.tensor_scalar(
                        out=sc1[:, sl_, :], in0=modt[:, sl_, 8:16], scalar1=1.0, scalar2=None, op0=ALU.add),
                        reads=[("modt", sl_)], writes=[("sc1", sl_)])
                    S.op("dve", lambda e: e.tensor_scalar(
                        out=g1a[:, sl_, :], in0=modt[:, sl_, 16:24], scalar1=1.0, scalar2=1.0 / ALPHA,
                        op0=ALU.add, op1=ALU.mult), reads=[("modt", sl_)], writes=[("g1a", sl_)])
            return f

        def mod_steps_for(sub_ids):
            groups = [(sl_, g) for sl_ in sub_ids for g in range(6)]
            steps = []
            for n in range(len(groups) + 1):
                def step(n=n):
                    if n < len(groups):
                        mod_dma(*groups[n])()
                    if n >= 1:
                        mod_mm(*groups[n - 1])()
                steps.append(step)
            return steps

        modctx["stg"] = [A.alloc(8 * 512, BF16).rearrange("p (k f) -> p k f", k=8) for _ in range(2)]
        for st_ in mod_steps_for([phases[0]]):
            st_()
        side_steps = mod_steps_for([x for x in range(4) if x != phases[0]])
        if phases[0] != 0:
            for st_ in side_steps:
                st_()
            side_steps = []
        S.barrier()

        def load_weights_cast(dst2d, src2d, sem, key=None, last=False):
            S.dma("pool", lambda e: e.dma_start(out=dst2d, in_=src2d), sem,
                  writes=[key] if (last and key is not None) else [])

        def run_phase(pi, s, side=None):
            A.off = persist_mark
            src = srcs[pi].rearrange("(c p) t -> p c t", p=P)
            dst = dsts[pi].rearrange("(c p) t -> p c t", p=P)
            kind = ("mixA", "ffn", "mixC", "ffn")[s]
            cast_eng = "pool" if kind == "ffn" else "dve"
            H = {"mixA": 8, "ffn": 1, "mixC": 15}[kind]
            W = T + 2 * H
            lay = s // 2
            lng = pp[:, OFF_LN + s * 16:OFF_LN + s * 16 + 8]
            lnb = pp[:, OFF_LN + s * 16 + 8:OFF_LN + s * 16 + 16]

            NX = {"mixA": 4, "mixC": 3, "ffn": 2}[kind]
            AHEAD = 1 if kind == "ffn" else 2
            xt = [A.alloc(KC * W, F32).rearrange("p (c w) -> p c w", c=KC) for _ in range(NX)]
            ht = [A.alloc(KC * W, BF16).rearrange("p (c w) -> p c w", c=KC) for _ in range(2)]
            rb2 = [A.alloc(2 * T, BF16) for _ in range(2)]
            rbf = [r_[:, 0:T] for r_ in rb2]
            rsq = [r_[:, T:2 * T] for r_ in rb2]
            msq = A.alloc(T, F32)
            var = A.alloc(T, F32)
            rstd = A.alloc(T, F32)
            mean_ps = PS[7][:, 0:T]
            e2_ps = PS[7][:, T:2 * T]

            if kind == "ffn":
                wup = A.alloc(KC * 2 * FF, BF16).rearrange("p (k f) -> p k f", k=KC)
                wdn = A.alloc(FJ * D, BF16).rearrange("p (j d) -> p j d", j=FJ)
                at = A.alloc(FJ * T, BF16).rearrange("p (j t) -> p j t", j=FJ)
                accs = [[A.alloc(T, F32) for _ in range(3)] for _ in range(3)]
                CB = 1408
                for cb in (0, 2, 1, 3):
                    for k in range(KC):
                        load_weights_cast(wup[:, k, cb * CB:(cb + 1) * CB],
                                          w_up_d[lay, k * P:(k + 1) * P, cb * CB:(cb + 1) * CB],
                                          f"ldw{cb}", ("wup", cb), last=(k == KC - 1))
                for j in range(FJ):
                    load_weights_cast(wdn[:, j, :], w_down_d[lay, j * P:(j + 1) * P, :], "ldw4", "wdn",
                                      last=(j == FJ - 1))
                fdw = pp[:, OFF_FDW + lay * 132:OFF_FDW + (lay + 1) * 132].rearrange("p (c k) -> p c k", k=3)
                fdb = pp[:, OFF_FDB + lay * 44:OFF_FDB + (lay + 1) * 44]
            elif kind == "mixA":
                win = A.alloc(KC * 1536, BF16).rearrange("p (k f) -> p k f", k=KC)
                wout = A.alloc(KC * D, BF16).rearrange("p (k f) -> p k f", k=KC)
                wst = A.alloc(512, BF16).rearrange("p (h q) -> p h q", h=4)
                plw = A.alloc(512, BF16).rearrange("p (g d) -> p g d", g=4)
                at2 = [A.alloc(KC * T, BF16).rearrange("p (j t) -> p j t", j=KC) for _ in range(2)]
                at = at2[0]
                pb = A.alloc(NPB, F32)
                u_sb2 = [A.alloc(4 * T, F32).rearrange("p (h t) -> p h t", h=4) for _ in range(2)]
                gv = [A.alloc(512, F32) for _ in range(2)]
                vn2 = [[A.alloc(512, BF16) for _ in range(2)] for _ in range(2)]
                zb = [A.alloc(W, F32) for _ in range(4)]
                sab = [[A.alloc(W, F32) for _ in range(2)] for _ in range(2)]
                tmpf = [A.alloc(T, F32) for _ in range(2)]
                pooled2 = [[A.alloc(T, BF16) for _ in range(4)] for _ in range(2)]
                st6 = [A.alloc(8, F32) for _ in range(2)]
                mv = [A.alloc(4, F32) for _ in range(2)]
                mhalf = A.alloc(1, F32)
                ones1 = A.alloc(P, BF16)
                cbt = A.alloc(4 * T, F32).rearrange("p (h t) -> p h t", h=4)
                sgg = pp[:, OFF_SGG:OFF_SGG + 4]
                sgb = pp[:, OFF_SGB:OFF_SGB + 4]
                S.op("pool", lambda e: e.memset(mhalf, -0.5), writes=["mhalf"])
                S.op("pool", lambda e: e.memset(ones1, 1.0), writes=["ones1"])
                S.dma("sp", lambda e: e.dma_start(out=pb, in_=pb_d), "ldpb", writes=["pb"])
                for k in range(KC):
                    load_weights_cast(win[:, k, :], w_in0_d[k * P:(k + 1) * P, :], "ldw0", "win", last=(k == KC - 1))
                load_weights_cast(wst.rearrange("p h q -> p (h q)"), wsT_d, "ldw1", "wst", last=True)
                load_weights_cast(plw.rearrange("p g d -> p (g d)"), poolw_d, "ldw1", "wst", last=True)
                for k in range(KC):
                    load_weights_cast(wout[:, k, :], w_out0_d[k * P:(k + 1) * P, :], "ldw2", "wout", last=(k == KC - 1))
                psc = pp[:, OFF_PSC:OFF_PSC + 4]
                for hd in range(4):
                    rs_ps = PS[hd % 2][:, 0:P]
                    S.op("pe", lambda e, rs_ps=rs_ps, hd=hd: e.matmul(rs_ps, lhsT=ones1, rhs=wst[:, hd, :], start=True, stop=True),
                         reads=["ones1", "wst"], writes=[("zp", hd % 2)], inc=True)
                    for c in range(2):
                        S.op("dve", lambda e, rs_ps=rs_ps, hd=hd, c=c: e.scalar_tensor_tensor(
                            out=cbt[:, hd, c * P:(c + 1) * P], in0=rs_ps, scalar=sgb[:, hd:hd + 1],
                            in1=pb[:, PB_BS + hd * T + c * P:PB_BS + hd * T + (c + 1) * P], op0=ALU.mult, op1=ALU.add),
                            reads=[("zp", hd % 2), "pp", "pb"], writes=["cbt"])
            else:
                win = A.alloc(KC * 2048, BF16).rearrange("p (k f) -> p k f", k=KC)
                wout = A.alloc(KC * D, BF16).rearrange("p (k f) -> p k f", k=KC)
                dg = A.alloc(248 * P, BF16).rearrange("p (n d) -> p n d", n=248)
                at = A.alloc(KC * T, BF16).rearrange("p (j t) -> p j t", j=KC)
                zt = A.alloc(KC * W, BF16).rearrange("p (j w) -> p j w", j=KC)
                cz2 = [A.alloc(KC * T, F32).rearrange("p (j t) -> p j t", j=KC) for _ in range(2)]
                mean_sb = [A.alloc(T, F32) for _ in range(2)]
                sg = [A.alloc(W, F32) for _ in range(2)]
                ident = A.alloc(P, F32)
                rstd2 = [A.alloc(T, F32) for _ in range(2)]
                S.dma("sp", lambda e: e.dma_start(out=ident, in_=pb_d[:, PB_ID:PB_ID + P]), "ldpb", writes=["ident"])
                for k in range(KC):
                    load_weights_cast(win[:, k, :], w_in1_d[k * P:(k + 1) * P, :], "ldw0", "win", last=(k == KC - 1))
                for k in range(KC):
                    load_weights_cast(wout[:, k, :], w_out1_d[k * P:(k + 1) * P, :], "ldw2", "wout", last=(k == KC - 1))
                dwt = pp[:, OFF_DWT:OFF_DWT + 248]
                for n in range(248):
                    if n % 2 == 0:
                        S.op("dve", lambda e, n=n: e.tensor_scalar(
                            out=dg[:, n, :], in0=ident, scalar1=dwt[:, n:n + 1], scalar2=None, op0=ALU.mult),
                            reads=["ident", "pp"], writes=[("dg", n // 31, 0)])
                    else:
                        S.op("act", lambda e, n=n: e.activation(
                            out=dg[:, n, :], in_=ident, func=AF.Copy, scale=dwt[:, n:n + 1]),
                            reads=["ident", "pp"], writes=[("dg", n // 31, 1)])
                cvb = pp[:, OFF_CVB:OFF_CVB + 8]
                cvg = pp[:, OFF_CVLN:OFF_CVLN + 8]
                cvbb = pp[:, OFF_CVLN + 8:OFF_CVLN + 16]

            def tile_span(i):
                lo, hi = i * T - H, i * T + T + H
                clo, chi = max(lo, 0), min(hi, S_LEN)
                return clo, chi, clo - lo, chi - lo

            def stage_A_load(i):
                xs = i % NX
                clo, chi, a, b = tile_span(i)
                rk = [("act", pi - 1, ii) for ii in (i - 1, i, i + 1) if 0 <= ii < NT] if pi > 0 else []
                S.dma("sp", lambda e: e.dma_start(out=xt[xs][:, :, a:b], in_=src[:, :, clo:chi]),
                      f"ldx{xs}", reads=rk, writes=[("xt", xs, c) for c in range(KC)])

            def stage_A_comp(i, c):
                sl = i % 2
                xs = i % NX
                clo, chi, a, b = tile_span(i)
                if c % 2 == 0:
                    S.op("act", lambda e: e.activation(
                        out=ht[sl][:, c, a:b], in_=xt[xs][:, c, a:b], func=AF.Identity,
                        scale=sc1[:, s, c:c + 1], bias=modt[:, s, c:c + 1]),
                        reads=[("xt", xs, c), ("sc1", s), ("modt", s)], writes=[("ht", sl, c)])
                else:
                    S.op("dve", lambda e: e.tensor_scalar(
                        out=ht[sl][:, c, a:b], in0=xt[xs][:, c, a:b],
                        scalar1=sc1[:, s, c:c + 1], scalar2=modt[:, s, c:c + 1], op0=ALU.mult, op1=ALU.add),
                        reads=[("xt", xs, c), ("sc1", s), ("modt", s)], writes=[("ht", sl, c)])
                if a > 0:
                    S.op("pool", lambda e: e.memset(ht[sl][:, c, 0:a], 0.0), writes=[("ht", sl, c)])
                if b < W:
                    S.op("pool", lambda e: e.memset(ht[sl][:, c, b:W], 0.0), writes=[("ht", sl, c)])

            def stage_A(i):
                stage_A_load(i)
                for c in range(KC):
                    stage_A_comp(i, c)

            def stats_mm(src_bf, src_sq, m, par):
                def f():
                    S.op("pe", lambda e: e.matmul(PS[7][:, 0:2 * T], lhsT=onesb, rhs=rb2[par], start=(m == 0), stop=(m == KC - 1)),
                         reads=[("rbf", par), ("rsq", par), "onesb"], writes=["mean_ps", "e2_ps"], inc=True)
                return f

            def ln_pieces(eps, rs):
                return [
                    lambda: S.op("act", lambda e: e.activation(out=msq, in_=mean_ps, func=AF.Square),
                                 reads=["mean_ps"], writes=["msq"]),
                    lambda: S.op("dve", lambda e: e.tensor_tensor(out=var, in0=e2_ps, in1=msq, op=ALU.subtract),
                                 reads=["e2_ps", "msq"], writes=["var"]),
                    lambda: S.op("act", lambda e: e.activation(out=var, in_=var, func=AF.Sqrt, bias=eps_ap(eps), scale=1.0),
                                 reads=["var", "epsc"], writes=["var"]),
                    lambda: S.op("dve", lambda e: e.reciprocal(out=rs, in_=var), reads=["var"], writes=["rstd"]),
                ]

            def ln_scalars(eps, rs):
                for f in ln_pieces(eps, rs):
                    f()

            def E1(i, m, yps):
                sl = i % NX
                par = m % 2
                xi = xt[sl][:, m, H:H + T]
                S.op("dve", lambda e: e.scalar_tensor_tensor(
                    out=xi, in0=yps, scalar=g1a[:, s, m:m + 1], in1=xi, op0=ALU.mult, op1=ALU.add),
                    reads=[("yps", par), ("xt", sl, m), ("g1a", s)], writes=[("xt", sl, m)])
                S.op(cast_eng, lambda e: e.tensor_copy(out=rbf[par], in_=xi),
                     reads=[("xt", sl, m)], writes=[("rbf", par)])
                S.op("act", lambda e: e.activation(out=rsq[par], in_=xi, func=AF.Square),
                     reads=[("xt", sl, m)], writes=[("rsq", par)])
                S.defer(stats_mm(rbf[par], rsq[par], m, par))

            def E2(i):
                sl = i % NX

                def sub(m):
                    xi = xt[sl][:, m, H:H + T]
                    S.op("dve", lambda e: e.tensor_tensor(out=xi, in0=xi, in1=mean_ps, op=ALU.subtract),
                         reads=[("xt", sl, m), "mean_ps"], writes=[("xt", sl, m)])

                def mul(m):
                    xi = xt[sl][:, m, H:H + T]
                    S.op("pool", lambda e: e.tensor_tensor(out=xi, in0=xi, in1=rstd, op=ALU.mult),
                         reads=[("xt", sl, m), "rstd"], writes=[("xt", sl, m)])

                def idn(m):
                    xi = xt[sl][:, m, H:H + T]
                    S.op("act", lambda e: e.activation(
                        out=xi, in_=xi, func=AF.Identity, scale=lng[:, m:m + 1], bias=lnb[:, m:m + 1]),
                        reads=[("xt", sl, m), "pp"], writes=[("xt", sl, m)])

                def store():
                    S.dma("sp", lambda e: e.dma_start(out=dst[:, :, i * T:(i + 1) * T], in_=xt[sl][:, :, H:H + T]),
                          f"stx{sl}", reads=[("xt", sl, c) for c in range(KC)], writes=[("act", pi, i)])

                pieces = list(ln_pieces(EPS_R, rstd))

                def step(st):
                    def f():
                        if st >= 2:
                            idn(st - 2)
                        if 1 <= st <= KC:
                            mul(st - 1)
                        if st < KC:
                            sub(st)
                    return f
                pieces += [step(st) for st in range(KC + 2)]
                pieces.append(store)
                return pieces

            eps_tiles = {}

            def eps_ap(eps):
                return eps_tiles[eps]

            for epsv in (EPS, EPS_R):
                t_ = A.alloc(1, F32)
                eps_tiles[epsv] = t_
                S.op("pool", lambda e, t_=t_, epsv=epsv: e.memset(t_, epsv), writes=["epsc"])

            def body_ffn(i):
                sl = i % 2

                def gate(j):
                    ag, av, gg = accs[j % 3]
                    S.op("act", lambda e: e.activation(out=gg, in_=ag, func=AF.Gelu_apprx_tanh),
                         reads=[("acc", j % 3, 0)], writes=[("acc", j % 3, 2)])
                    S.op("pool", lambda e: e.tensor_tensor(out=at[:, j, :], in0=gg, in1=av, op=ALU.mult),
                         reads=[("acc", j % 3, 1), ("acc", j % 3, 2)], writes=[("at", j)])

                for j in range(FJ):
                    bz = (j % 2) * 2
                    for half in range(2):
                        zp = PS[bz + half][:, 0:W]
                        cidx = half * FJ + j
                        col = cidx * P
                        cb = col // 1408
                        for k in range(KC):
                            S.op("pe", lambda e, zp=zp, k=k, col=col: e.matmul(
                                zp, lhsT=wup[:, k, col:col + P], rhs=ht[sl][:, k, :], start=(k == 0), stop=(k == KC - 1)),
                                reads=[("wup", cb), ("ht", sl, k)], writes=[("zp", bz + half)], inc=(k == KC - 1))
                        acc = accs[j % 3][half]
                        S.op("act", lambda e, zp=zp, acc=acc, cidx=cidx: e.activation(
                            out=acc, in_=zp[:, 1:1 + T], func=AF.Identity,
                            scale=fdw[:, cidx, 1:2], bias=fdb[:, cidx:cidx + 1]),
                            reads=[("zp", bz + half), "pp"], writes=[("acc", j % 3, half)])
                    if j >= 1:
                        gate(j - 1)
                    for tap in (0, 2):
                        for half in range(2):
                            zp = PS[bz + half][:, 0:W]
                            cidx = half * FJ + j
                            acc = accs[j % 3][half]
                            S.op("dve", lambda e, zp=zp, acc=acc, cidx=cidx, tap=tap: e.scalar_tensor_tensor(
                                out=acc, in0=zp[:, tap:tap + T], scalar=fdw[:, cidx, tap:tap + 1], in1=acc,
                                op0=ALU.mult, op1=ALU.add),
                                reads=[("zp", bz + half), ("acc", j % 3, half)], writes=[("acc", j % 3, half)])
                    S.flush(1)
                    if j >= 2:
                        pend_e2(1)
                    if j == 18 and i + 1 < NT:
                        stage_A_load(i + 1)
                gate(FJ - 1)
                for m in range(KC):
                    yps = PS[4 + m % 2][:, 0:T]
                    for j in range(FJ):
                        S.op("pe", lambda e, yps=yps, j=j, m=m: e.matmul(
                            yps, lhsT=wdn[:, j, m * P:(m + 1) * P], rhs=at[:, j, :], start=(j == 0), stop=(j == FJ - 1)),
                            reads=["wdn", ("at", j)], writes=[("yps", m % 2)], inc=(j == FJ - 1))
                    if m >= 1:
                        S.flush(1)
                    E1(i, m, yps)
                    if i + AHEAD < NT:
                        stage_A_comp(i + AHEAD, m)

            def mixA_parts(i):
                sl = i % 2
                ib = i % 2
                edge = 0 if i == 0 else (1 if i == NT - 1 else None)
                h_ = ht[sl]
                u_sb = u_sb2[ib]
                vn = vn2[ib]
                pooled = pooled2[ib]
                atb = at2[ib]

                def v_part(c):
                    vps = PS[2 + c]
                    for k in range(KC):
                        S.op("pe", lambda e, k=k: e.matmul(
                            vps[:, :], lhsT=h_[:, k, H + c * P:H + (c + 1) * P], rhs=win[:, k, 512:1024],
                            start=(k == 0), stop=(k == KC - 1)),
                            reads=["win", ("ht", sl, k)], writes=[("zp", 2 + c)], inc=(k == KC - 1))
                    S.flush()
                    S.op("act", lambda e: e.activation(out=gv[c], in_=vps[:, :], func=AF.Gelu_apprx_tanh),
                         reads=[("zp", 2 + c)], writes=[("gv", c)])
                    S.op("dve", lambda e: e.bn_stats(out=st6[c][:, 0:6], in_=gv[c]), reads=[("gv", c)], writes=[("st6", c)])
                    S.op("dve", lambda e: e.bn_aggr(out=mv[c][:, 0:2], in_=st6[c][:, 0:6]), reads=[("st6", c)], writes=[("mv", c)])
                    S.op("pool", lambda e: e.tensor_scalar(out=mv[c][:, 2:3], in0=mv[c][:, 1:2], scalar1=EPS, scalar2=None, op0=ALU.add),
                         reads=[("mv", c)], writes=[("mv2", c)])
                    S.op("pool", lambda e: e.tensor_tensor(out=mv[c][:, 3:4], in0=mv[c][:, 2:3], in1=mhalf, op=ALU.pow),
                         reads=[("mv2", c), "mhalf"], writes=[("mv3", c)])
                    S.op("dve", lambda e: e.tensor_scalar(
                        out=vn[c], in0=gv[c], scalar1=mv[c][:, 0:1], scalar2=mv[c][:, 3:4], op0=ALU.subtract, op1=ALU.mult),
                        reads=[("gv", c), ("mv", c), ("mv3", c)], writes=[("vn", ib, c)])
                    pend_e2(2)

                def zb_part(g):
                    zps = PS[g % 2][:, 0:W]
                    for k in range(KC):
                        S.op("pe", lambda e, k=k: e.matmul(
                            zps, lhsT=win[:, k, 1024 + g * P:1024 + (g + 1) * P], rhs=h_[:, k, :], start=(k == 0), stop=(k == KC - 1)),
                            reads=["win", ("ht", sl, k)], writes=[("zp", g % 2)], inc=(k == KC - 1))
                    z = zb[g]
                    S.op("act", lambda e: e.activation(out=z, in_=zps, func=AF.Copy),
                         reads=[("zp", g % 2)], writes=[("zb", g)])
                    pend_e2(2)
                    eng = "dve" if g % 2 == 0 else "pool"
                    spans = [(1, W, 1, 0), (2, W - 1, 1, 1), (4, W - 3, 2, 2), (8, W - 7, 4, 4)]
                    cur = z
                    bufs = sab[g % 2]
                    for lv in range(g + 1):
                        a0, a1, dl, dr = spans[lv]
                        o = bufs[lv % 2]
                        if lv == 0:
                            i0, i1 = cur[:, 0:W - 1], cur[:, 1:W]
                        else:
                            i0, i1 = cur[:, a0 - dl:a1 - dl], cur[:, a0 + dr:a1 + dr]
                        S.op(eng, lambda e, o=o, a0=a0, a1=a1, i0=i0, i1=i1: e.tensor_tensor(
                            out=o[:, a0:a1], in0=i0, in1=i1, op=ALU.add),
                            reads=[("zb", g), ("sab", g % 2, 0), ("sab", g % 2, 1)], writes=[("sab", g % 2, lv % 2)])
                        cur = o
                    pl = pooled[g]
                    wdw = float(2 << g)
                    if edge is None:
                        S.op("dve", lambda e: e.scalar_tensor_tensor(
                            out=pl, in0=cur[:, H:H + T], scalar=1.0 / wdw, in1=z[:, H:H + T], op0=ALU.mult, op1=ALU.subtract),
                            reads=[("sab", g % 2, 0), ("sab", g % 2, 1), ("zb", g)], writes=[("pooled", ib, g)])
                    else:
                        ic = pb[:, PB_IC + (edge * 4 + g) * T:PB_IC + (edge * 4 + g + 1) * T]
                        tf = tmpf[g % 2]
                        S.op(eng, lambda e: e.tensor_tensor(out=tf, in0=cur[:, H:H + T], in1=ic, op=ALU.mult),
                             reads=[("sab", g % 2, 0), ("sab", g % 2, 1), "pb"], writes=[("tmpf", g % 2)])
                        S.op(eng, lambda e: e.tensor_tensor(out=pl, in0=tf, in1=z[:, H:H + T], op=ALU.subtract),
                             reads=[("tmpf", g % 2), ("zb", g)], writes=[("pooled", ib, g)])

                def u_part(hd):
                    ups = PS[hd % 2][:, 0:T]
                    for k in range(KC):
                        S.op("pe", lambda e, k=k: e.matmul(
                            ups, lhsT=win[:, k, hd * P:(hd + 1) * P], rhs=h_[:, k, H:H + T], start=(k == 0), stop=(k == KC - 1)),
                            reads=["win", ("ht", sl, k)], writes=[("zp", hd % 2)], inc=(k == KC - 1))
                    S.op("act", lambda e: e.activation(out=u_sb[:, hd, :], in_=ups, func=AF.Gelu_apprx_tanh),
                         reads=[("zp", hd % 2)], writes=[("u", ib, hd)])
                    pend_e2(1)

                def mix_part():
                    for c in range(2):
                        for hd in range(4):
                            mx = PS[4 + hd // 2][:, (hd % 2) * T + c * P:(hd % 2) * T + (c + 1) * P]
                            S.op("pe", lambda e, mx=mx, c=c, hd=hd: e.matmul(
                                mx, lhsT=vn[c][:, hd * P:(hd + 1) * P], rhs=wst[:, hd, :], start=True, stop=True),
                                reads=[("vn", ib, c), "wst"], writes=[("yps", hd // 2)], inc=True)
                    for hd in range(4):
                        mxf = PS[4 + hd // 2][:, (hd % 2) * T:(hd % 2 + 1) * T]
                        tf = tmpf[hd % 2]
                        S.op("dve", lambda e, mxf=mxf, tf=tf, hd=hd: e.scalar_tensor_tensor(
                            out=tf, in0=mxf, scalar=sgg[:, hd:hd + 1], in1=cbt[:, hd, :], op0=ALU.mult, op1=ALU.add),
                            reads=[("yps", hd // 2), "cbt", "pp"], writes=[("tmpf", hd % 2)])
                        S.op("pool", lambda e, tf=tf, hd=hd: e.tensor_tensor(out=atb[:, hd, :], in0=tf, in1=u_sb[:, hd, :], op=ALU.mult),
                             reads=[("tmpf", hd % 2), ("u", ib, hd)], writes=[("at", ib, hd)])

                def pw_part(g):
                    pw = PS[2 + g % 2][:, 0:T]
                    S.op("pe", lambda e: e.matmul(pw, lhsT=plw[:, g, :], rhs=pooled[g], start=True, stop=True),
                         reads=["wst", ("pooled", ib, g)], writes=[("zp", 2 + g % 2)], inc=True)
                    S.op("act", lambda e: e.activation(out=atb[:, 4 + g, :], in_=pw, func=AF.Copy, scale=psc[:, g:g + 1]),
                         reads=[("zp", 2 + g % 2), "pp"], writes=[("at", ib, 4 + g)])

                def X():
                    if 2 <= i + 1 < NT:
                        stage_A_load(i + 1)
                    v_part(0)
                    v_part(1)
                    for g in range(4):
                        zb_part(g)
                    for hd in range(4):
                        u_part(hd)
                    pend_e2()
                    mix_part()
                    for g in range(4):
                        pw_part(g)

                def Y():
                    out_proj(i, atb, ib)
                return X, Y

            lnq = []

            def out_proj(i, atx=None, ibx=None):
                atx = at if atx is None else atx
                for m in range(KC):
                    yps = PS[4 + m % 2][:, 0:T]
                    for k in range(KC):
                        S.op("pe", lambda e, yps=yps, k=k, m=m: e.matmul(
                            yps, lhsT=wout[:, k, m * P:(m + 1) * P], rhs=atx[:, k, :], start=(k == 0), stop=(k == KC - 1)),
                            reads=["wout", ("at", k) if ibx is None else ("at", ibx, k)], writes=[("yps", m % 2)], inc=(k == KC - 1))
                    if m >= 2:
                        S.flush(1)
                    E1(i, m, yps)
                    if i + AHEAD < NT:
                        stage_A_comp(i + AHEAD, m)
                    if lnq:
                        lnq.pop(0)()

            def mixC_parts(i):
                sl = i % 2
                ib = i % 2
                h_ = ht[sl]
                cz = cz2[ib]
                mean6 = PS[6][:, 0:T]
                e26 = PS[6][:, T:2 * T]

                def stats6(j, par):
                    def f():
                        S.op("pe", lambda e: e.matmul(PS[6][:, 0:2 * T], lhsT=onesb, rhs=rb2[par], start=(j == 0), stop=(j == KC - 1)),
                             reads=[("rbf", par), ("rsq", par), "onesb"], writes=["mean6"], inc=True)
                    return f

                def X():
                    for j in range(KC):
                        aps = PS[(j % 2) * 2][:, 0:W]
                        gps = PS[(j % 2) * 2 + 1][:, 0:W]
                        for half, zp in ((0, aps), (1, gps)):
                            col = half * D + j * P
                            for k in range(KC):
                                S.op("pe", lambda e, zp=zp, k=k, col=col: e.matmul(
                                    zp, lhsT=win[:, k, col:col + P], rhs=h_[:, k, :], start=(k == 0), stop=(k == KC - 1)),
                                    reads=["win", ("ht", sl, k)], writes=[("zp", (j % 2) * 2 + half)], inc=(k == KC - 1))
                        sgt = sg[j % 2]
                        S.op("act", lambda e, gps=gps, sgt=sgt: e.activation(out=sgt, in_=gps, func=AF.Sigmoid),
                             reads=[("zp", (j % 2) * 2 + 1)], writes=[("sg", j % 2)])
                        S.op("dve", lambda e, aps=aps, sgt=sgt, j=j: e.tensor_tensor(out=zt[:, j, :], in0=aps, in1=sgt, op=ALU.mult),
                             reads=[("zp", (j % 2) * 2), ("sg", j % 2)], writes=[("zt", j)])
                        if j == 0:
                            S.flush()
                        pend_e2(2)
                    pend_e2()
                    for j in range(KC):
                        cps = PS[4 + j % 2][:, 0:T]
                        for k in range(31):
                            S.op("pe", lambda e, cps=cps, j=j, k=k: e.matmul(
                                cps, lhsT=dg[:, j * 31 + k, :], rhs=zt[:, j, k:k + T], start=(k == 0), stop=(k == 30)),
                                reads=[("dg", j, 0), ("dg", j, 1), ("zt", j)], writes=[("yps", j % 2)], inc=(k == 30))
                        par = j % 2
                        S.op("act", lambda e, cps=cps, j=j: e.activation(out=cz[:, j, :], in_=cps, func=AF.Identity, bias=cvb[:, j:j + 1], scale=1.0),
                             reads=[("yps", par), "pp"], writes=[("cz", ib, j)])
                        S.op("pool", lambda e, j=j, par=par: e.tensor_copy(out=rbf[par], in_=cz[:, j, :]),
                             reads=[("cz", ib, j)], writes=[("rbf", par)])
                        S.op("act", lambda e, j=j, par=par: e.activation(out=rsq[par], in_=cz[:, j, :], func=AF.Square),
                             reads=[("cz", ib, j)], writes=[("rsq", par)])
                        if j >= 1:
                            S.flush(1)
                        S.defer(stats6(j, par))
                    S.flush()

                def lnC():
                    return [
                        lambda: S.op("act", lambda e: e.activation(out=msq, in_=mean6, func=AF.Square),
                                     reads=["mean6"], writes=["msq"]),
                        lambda: S.op("dve", lambda e: e.tensor_tensor(out=var, in0=e26, in1=msq, op=ALU.subtract),
                                     reads=["mean6", "msq"], writes=["var"]),
                        lambda: S.op("dve", lambda e: e.tensor_copy(out=mean_sb[ib], in_=mean6),
                                     reads=["mean6"], writes=[("mean_sb", ib)]),
                        lambda: S.op("act", lambda e: e.activation(out=var, in_=var, func=AF.Sqrt, bias=eps_ap(EPS), scale=1.0),
                                     reads=["var", "epsc"], writes=["var"]),
                        lambda: S.op("dve", lambda e: e.reciprocal(out=rstd2[ib], in_=var), reads=["var"], writes=[("rstd2", ib)]),
                    ]

                def Y():
                    if i + 2 < NT:
                        stage_A_load(i + 2)
                    for j in range(KC):
                        S.op("dve", lambda e, j=j: e.tensor_tensor(out=cz[:, j, :], in0=cz[:, j, :], in1=mean_sb[ib], op=ALU.subtract),
                             reads=[("cz", ib, j), ("mean_sb", ib)], writes=[("cz", ib, j)])
                    for j in range(KC):
                        S.op("pool", lambda e, j=j: e.tensor_tensor(out=cz[:, j, :], in0=cz[:, j, :], in1=rstd2[ib], op=ALU.mult),
                             reads=[("cz", ib, j), ("rstd2", ib)], writes=[("cz", ib, j)])
                    for j in range(KC):
                        S.op("act", lambda e, j=j: e.activation(out=at[:, j, :], in_=cz[:, j, :], func=AF.Silu,
                                                                scale=cvg[:, j:j + 1], bias=cvbb[:, j:j + 1]),
                             reads=[("cz", ib, j), "pp"], writes=[("at", j)])
                    out_proj(i)
                return X, lnC, Y

            pend = []

            def pend_e2(n=None):
                k = len(pend) if n is None else min(n, len(pend))
                for _ in range(k):
                    pend.pop(0)()

            if side:
                modctx["stg"] = [A.alloc(8 * 512, BF16).rearrange("p (k f) -> p k f", k=8) for _ in range(2)]
            stage_A(0)
            if kind == "mixA":
                if NT > 1:
                    stage_A(1)
                Ys = {}
                for i in range(NT):
                    if side and i >= 1:
                        side.pop(0)()
                    X, Ys[i] = mixA_parts(i)
                    X()
                    if i >= 1:
                        Ys.pop(i - 1)()
                        pend.extend(E2(i - 1))
                S.flush()
                pend_e2()
                Ys.pop(NT - 1)()
                pend.extend(E2(NT - 1))
            elif kind == "mixC":
                if NT > 1:
                    stage_A(1)
                Ys = {}
                for i in range(NT):
                    if side and i >= 1:
                        side.pop(0)()
                    X, lnC, Ys[i] = mixC_parts(i)
                    X()
                    lnq.extend(lnC())
                    if i == 0:
                        while lnq:
                            lnq.pop(0)()
                    else:
                        Ys.pop(i - 1)()
                        while lnq:
                            lnq.pop(0)()
                        pend.extend(E2(i - 1))
                S.flush()
                pend_e2()
                Ys.pop(NT - 1)()
                pend.extend(E2(NT - 1))
            else:
                body = body_ffn
                for i in range(NT):
                    if side and i >= 1:
                        side.pop(0)()
                    body(i)
                    pend.extend(E2(i))
            S.flush()
            pend_e2()
            while side:
                side.pop(0)()
            S.barrier()

        for pi, s in enumerate(phases):
            run_phase(pi, s, side_steps if pi == 0 else None)

        S.wait_keys("sp", [("act", nph - 1, i) for i in range(NT)])
        S.emit()
    return nc


def _chunkT(v):
    return np.ascontiguousarray(v.reshape(-1, P).T)


def _icnt(S_LEN):
    ic = np.zeros((2, 4, T), np.float32)
    for e in range(2):
        for g in range(4):
            half = 1 << g
            t = np.arange(T) + (0 if e == 0 else S_LEN - T)
            cnt = np.minimum(t + half, S_LEN) - np.maximum(t - half, 0)
            ic[e, g] = 1.0 / cnt
    return ic


def prep_shared(inp, S_LEN):
    f = lambda a: np.ascontiguousarray(np.asarray(a, dtype=np.float32))
    sh = {}
    sh["adaw"] = f(np.stack([inp["mix_ada_w"][0], inp["ffn_ada_w"][0], inp["mix_ada_w"][1], inp["ffn_ada_w"][1]]))
    sh["w_in0"] = f(inp["ab_w_in"][0])
    sh["w_out0"] = f(inp["ab_w_out"][0])
    sh["wsT"] = f(np.transpose(inp["ab_ws"][0], (2, 0, 1)).reshape(P, 512))
    sh["poolw"] = f(np.transpose(inp["ab_pool_w"][0], (1, 0, 2)).reshape(P, 512))
    sh["w_in1"] = f(inp["cv_w_in"][0])
    sh["w_out1"] = f(inp["cv_w_out"][0])
    sh["w_up"] = f(inp["ffn_w_up"])
    sh["w_down"] = f(inp["ffn_w_down"])
    pb = np.zeros((P, NPB), np.float32)
    pb[:, PB_SG:PB_SG + 512] = np.broadcast_to(inp["ab_sgu_ln_g"][0][None, :], (P, 512))
    pb[:, PB_SB:PB_SB + 512] = np.broadcast_to(inp["ab_sgu_ln_b"][0][None, :], (P, 512))
    bs = np.asarray(inp["ab_bs"][0], np.float32)
    pb[:, PB_BS:PB_BS + 1024] = np.broadcast_to(np.tile(bs, (1, 2)).reshape(1, 1024), (P, 1024))
    pb[:, PB_IC:PB_IC + 2048] = np.broadcast_to(_icnt(S_LEN).reshape(1, 2048), (P, 2048))
    pb[:, PB_ID:PB_ID + P] = np.eye(P, dtype=np.float32)
    sh["pb"] = pb
    pp = np.zeros((P, NPP), np.float32)
    adab = [inp["mix_ada_b"][0], inp["ffn_ada_b"][0], inp["mix_ada_b"][1], inp["ffn_ada_b"][1]]
    lng = [inp["mix_ln_g"][0], inp["ffn_ln_g"][0], inp["mix_ln_g"][1], inp["ffn_ln_g"][1]]
    lnb = [inp["mix_ln_b"][0], inp["ffn_ln_b"][0], inp["mix_ln_b"][1], inp["ffn_ln_b"][1]]
    for s in range(4):
        pp[:, OFF_ADAB + 24 * s:OFF_ADAB + 24 * s + 24] = _chunkT(np.asarray(adab[s], np.float32))
        pp[:, OFF_LN + 16 * s:OFF_LN + 16 * s + 8] = _chunkT(np.asarray(lng[s], np.float32))
        pp[:, OFF_LN + 16 * s + 8:OFF_LN + 16 * s + 16] = _chunkT(np.asarray(lnb[s], np.float32))
    pp[:, OFF_PSC:OFF_PSC + 4] = _chunkT(np.asarray(inp["ab_pool_scale"][0], np.float32))
    pp[:, OFF_SGG:OFF_SGG + 4] = _chunkT(np.asarray(inp["ab_sgu_ln_g"][0], np.float32))
    pp[:, OFF_SGB:OFF_SGB + 4] = _chunkT(np.asarray(inp["ab_sgu_ln_b"][0], np.float32))
    pp[:, OFF_CVB:OFF_CVB + 8] = _chunkT(np.asarray(inp["cv_dw_b"][0], np.float32))
    pp[:, OFF_CVLN:OFF_CVLN + 8] = _chunkT(np.asarray(inp["cv_ln_g"][0], np.float32))
    pp[:, OFF_CVLN + 8:OFF_CVLN + 16] = _chunkT(np.asarray(inp["cv_ln_b"][0], np.float32))
    dw = np.asarray(inp["cv_dw"][0], np.float32)
    pp[:, OFF_DWT:OFF_DWT + 248] = dw.reshape(31, 8, P).transpose(2, 1, 0).reshape(P, 248)
    for l in range(2):
        fd = np.asarray(inp["ffn_dw"][l], np.float32)
        pp[:, OFF_FDW + 132 * l:OFF_FDW + 132 * (l + 1)] = fd.reshape(3, 44, P).transpose(2, 1, 0).reshape(P, 132)
        pp[:, OFF_FDB + 44 * l:OFF_FDB + 44 * (l + 1)] = _chunkT(np.asarray(inp["ffn_dw_b"][l], np.float32))
    return sh, pp


_NC_CACHE = {}


def kernel(**inputs):
    x = np.asarray(inputs["x"], np.float32)
    c = np.asarray(inputs["c"], np.float32)
    B, S_LEN, _ = x.shape
    sh, pp0 = prep_shared(inputs, S_LEN)
    key = (S_LEN,)
    if key not in _NC_CACHE:
        _NC_CACHE[key] = build_nc(S_LEN)
    nc = _NC_CACHE[key]
    in_maps = []
    for b in range(B):
        pp = pp0.copy()
        pp[:, OFF_C:OFF_C + 8] = _chunkT(c[b])
        m = dict(sh)
        m["pp"] = pp
        m["xT"] = np.ascontiguousarray(x[b].T)
        in_maps.append(m)
    res = run_bass_kernel_spmd(nc, in_maps, core_ids=list(range(B)))
    out = np.empty((B, S_LEN, D), np.float32)
    for b in range(B):
        out[b] = res.results[b]["outT"].T
    return out
```
